# Optimizing a Trainium2 kernel written in Bass

```python
import math
import jax, jax.numpy as jnp
from jax import lax
import numpy as np

D_MODEL = 1024
BATCH = 2
SEQ = 8192
DEPTH = 2

N_MIXERS = 2
D_INNER = 2 * D_MODEL
RET_HEADS = 4
RET_QK_DIM = D_MODEL // RET_HEADS
RET_V_DIM = D_INNER // RET_HEADS
RET_CHUNK = 128
RET_IN_COLS = 2 * RET_HEADS * RET_QK_DIM + 2 * D_INNER
DIFF_HEADS = 16
DIFF_HEAD_DIM = D_INNER // DIFF_HEADS // 2
DIFF_V_DIM = 2 * DIFF_HEAD_DIM
Q_BLOCK = 128
DIFF_IN_COLS = 3 * DIFF_HEADS * 2 * DIFF_HEAD_DIM + D_INNER
REL_BUCKETS = 32
REL_MAX_DIST = 128
ROPE_BASE = 10000.0
RMS_EPS = 1e-6
GN_EPS = 1e-5

N_RET_LAYERS = (DEPTH + 1) // 2
N_DIFF_LAYERS = DEPTH // 2

kernel_name = "hybrid_retention_diffattn_trunk"


def rmsnorm(x, g, eps=RMS_EPS):
    xf = x.astype(jnp.float32)
    y = xf * lax.rsqrt(jnp.mean(xf * xf, axis=-1, keepdims=True) + eps)
    return (y * g.astype(jnp.float32)).astype(x.dtype)


def rope(x, pos):
    half = x.shape[-1] // 2
    inv = ROPE_BASE ** (-jnp.arange(half, dtype=jnp.float32) / half)
    ang = pos[:, None] * inv[None, :]
    c, s = jnp.cos(ang), jnp.sin(ang)
    x1, x2 = x[..., :half], x[..., half:]
    return jnp.concatenate([x1 * c - x2 * s, x1 * s + x2 * c], axis=-1)


def retention_mixer(h, w_in, w_out):
    B, S, _ = h.shape
    H, dk, dv, C = RET_HEADS, RET_QK_DIM, RET_V_DIM, RET_CHUNK
    N = S // C
    proj = h @ w_in
    qw = H * dk
    q, k, v, g = jnp.split(proj, [qw, 2 * qw, 2 * qw + D_INNER], axis=-1)
    q = q.reshape(B, S, H, dk).transpose(0, 2, 1, 3).astype(jnp.float32)
    k = k.reshape(B, S, H, dk).transpose(0, 2, 1, 3).astype(jnp.float32)
    v = v.reshape(B, S, H, dv).transpose(0, 2, 1, 3).astype(jnp.float32)
    pos = jnp.arange(S, dtype=jnp.float32)
    q = rope(q, pos)
    k = rope(k, pos) * (dk ** -0.5)
    log_gamma = jnp.log1p(-jnp.exp2(-5.0 - jnp.arange(H, dtype=jnp.float32)))
    idx = jnp.arange(C, dtype=jnp.float32)
    rel = idx[:, None] - idx[None, :]
    dmat = jnp.exp(log_gamma[:, None, None] * jnp.maximum(rel, 0.0)) * (rel >= 0)
    xi = jnp.exp(log_gamma[:, None] * (idx + 1.0))
    zeta = jnp.exp(log_gamma[:, None] * (C - 1.0 - idx))
    chunk_decay = jnp.exp(log_gamma * C)
    qc = q.reshape(B, H, N, C, dk)
    kc = k.reshape(B, H, N, C, dk)
    vc = v.reshape(B, H, N, C, dv)
    scores = jnp.einsum('bhncd,bhnmd->bhncm', qc, kc) * dmat[None, :, None]
    inner = jnp.einsum('bhncm,bhnme->bhnce', scores, vc)

    def step(R, xs):
        qn, kn, vn = xs
        cross = jnp.einsum('bhcd,bhde->bhce', qn, R) * xi[None, :, :, None]
        R_new = R * chunk_decay[None, :, None, None] + jnp.einsum(
            'bhcd,bhce->bhde', kn, vn * zeta[None, :, :, None])
        return R_new, cross

    R0 = jnp.zeros((B, H, dk, dv), jnp.float32)
    xs = (qc.transpose(2, 0, 1, 3, 4), kc.transpose(2, 0, 1, 3, 4), vc.transpose(2, 0, 1, 3, 4))
    _, cross = lax.scan(step, R0, xs)
    ret = (inner + cross.transpose(1, 2, 0, 3, 4)).reshape(B, H, S, dv)
    mu = jnp.mean(ret, axis=-1, keepdims=True)
    var = jnp.mean(jnp.square(ret - mu), axis=-1, keepdims=True)
    ret = (ret - mu) * lax.rsqrt(var + GN_EPS)
    y = ret.transpose(0, 2, 1, 3).reshape(B, S, D_INNER)
    gated = jax.nn.silu(g.astype(jnp.float32)) * y
    return gated.astype(w_out.dtype) @ w_out


def t5_bucket(q_pos, k_pos):
    n = jnp.maximum(q_pos[:, None] - k_pos[None, :], 0)
    max_exact = REL_BUCKETS // 2
    nf = jnp.maximum(n, max_exact).astype(jnp.float32)
    large = max_exact + (jnp.log(nf / max_exact) / math.log(REL_MAX_DIST / max_exact)
                         * (REL_BUCKETS - max_exact)).astype(jnp.int32)
    large = jnp.minimum(large, REL_BUCKETS - 1)
    return jnp.where(n < max_exact, n, large)


def diff_attn_mixer(h, w_in, w_out, lq1, lk1, lq2, lk2, subln_g, rel_bias, lambda_init):
    B, S, _ = h.shape
    H, d, dv = DIFF_HEADS, DIFF_HEAD_DIM, DIFF_V_DIM
    proj = h @ w_in
    qw = H * 2 * d
    q, k, v, g = jnp.split(proj, [qw, 2 * qw, 2 * qw + H * dv], axis=-1)
    q = q.reshape(B, S, H, 2, d).transpose(0, 2, 3, 1, 4).astype(jnp.float32) * (d ** -0.5)
    k = k.reshape(B, S, H, 2, d).transpose(0, 2, 3, 1, 4).astype(jnp.float32)
    v = v.reshape(B, S, H, dv).transpose(0, 2, 1, 3).astype(jnp.float32)
    lam = (jnp.exp(jnp.sum(lq1.astype(jnp.float32) * lk1.astype(jnp.float32)))
           - jnp.exp(jnp.sum(lq2.astype(jnp.float32) * lk2.astype(jnp.float32)))
           + lambda_init)
    NB = S // Q_BLOCK
    qb = q.reshape(B, H, 2, NB, Q_BLOCK, d).transpose(3, 0, 1, 2, 4, 5)
    starts = jnp.arange(NB, dtype=jnp.int32) * Q_BLOCK
    k_pos = jnp.arange(S, dtype=jnp.int32)
    table = rel_bias.astype(jnp.float32)

    def block(args):
        qblk, start = args
        q_pos = start + jnp.arange(Q_BLOCK, dtype=jnp.int32)
        logits = jnp.einsum('bhmqd,bhmkd->bhmqk', qblk, k)
        bias = table[t5_bucket(q_pos, k_pos)].transpose(2, 0, 1)
        logits = logits + bias[None, :, None]
        mask = k_pos[None, :] <= q_pos[:, None]
        logits = jnp.where(mask, logits, -jnp.inf)
        probs = jax.nn.softmax(logits, axis=-1)
        attn = probs[:, :, 0] - lam * probs[:, :, 1]
        return jnp.einsum('bhqk,bhke->bhqe', attn, v)

    out = lax.map(block, (qb, starts))
    out = out.transpose(1, 2, 0, 3, 4).reshape(B, H, S, dv)
    out = rmsnorm(out, subln_g, eps=GN_EPS) * (1.0 - lambda_init)
    y = out.transpose(0, 2, 1, 3).reshape(B, S, D_INNER)
    gated = jax.nn.silu(g.astype(jnp.float32)) * y
    return gated.astype(w_out.dtype) @ w_out


def setup_inputs(seed: int = 0) -> dict:
    key = jax.random.key(seed)
    ks = jax.random.split(key, 14)
    f32 = jnp.float32
    x = jax.random.normal(ks[0], (BATCH, SEQ, D_MODEL), f32)
    pre_norm_g = 1.0 + 0.02 * jax.random.normal(ks[1], (DEPTH, D_MODEL), f32)
    post_norm_g = 1.0 + 0.02 * jax.random.normal(ks[2], (DEPTH, D_MODEL), f32)
    ret_w_in = jax.random.normal(ks[3], (N_RET_LAYERS, D_MODEL, RET_IN_COLS), f32) * D_MODEL ** -0.5
    ret_w_out = jax.random.normal(ks[4], (N_RET_LAYERS, D_INNER, D_MODEL), f32) * D_INNER ** -0.5
    diff_w_in = jax.random.normal(ks[5], (N_DIFF_LAYERS, D_MODEL, DIFF_IN_COLS), f32) * D_MODEL ** -0.5
    diff_w_out = jax.random.normal(ks[6], (N_DIFF_LAYERS, D_INNER, D_MODEL), f32) * D_INNER ** -0.5
    diff_lambda_q1 = 0.1 * jax.random.normal(ks[7], (N_DIFF_LAYERS, DIFF_HEAD_DIM), f32)
    diff_lambda_k1 = 0.1 * jax.random.normal(ks[8], (N_DIFF_LAYERS, DIFF_HEAD_DIM), f32)
    diff_lambda_q2 = 0.1 * jax.random.normal(ks[9], (N_DIFF_LAYERS, DIFF_HEAD_DIM), f32)
    diff_lambda_k2 = 0.1 * jax.random.normal(ks[10], (N_DIFF_LAYERS, DIFF_HEAD_DIM), f32)
    diff_subln_g = 1.0 + 0.02 * jax.random.normal(ks[11], (N_DIFF_LAYERS, DIFF_V_DIM), f32)
    rel_bias = 0.5 * jax.random.normal(ks[12], (REL_BUCKETS, DIFF_HEADS), f32)
    return {"x": x, "pre_norm_g": pre_norm_g, "post_norm_g": post_norm_g,
            "ret_w_in": ret_w_in, "ret_w_out": ret_w_out,
            "diff_w_in": diff_w_in, "diff_w_out": diff_w_out,
            "diff_lambda_q1": diff_lambda_q1, "diff_lambda_k1": diff_lambda_k1,
            "diff_lambda_q2": diff_lambda_q2, "diff_lambda_k2": diff_lambda_k2,
            "diff_subln_g": diff_subln_g, "rel_bias": rel_bias}


def reference(x, pre_norm_g, post_norm_g, ret_w_in, ret_w_out, diff_w_in, diff_w_out,
              diff_lambda_q1, diff_lambda_k1, diff_lambda_q2, diff_lambda_k2,
              diff_subln_g, rel_bias):
    for i in range(DEPTH):
        h = rmsnorm(x, pre_norm_g[i])
        j = i // N_MIXERS
        if i % N_MIXERS == 0:
            y = retention_mixer(h, ret_w_in[j], ret_w_out[j])
        else:
            lambda_init = 0.8 - 0.6 * math.exp(-0.3 * i)
            y = diff_attn_mixer(h, diff_w_in[j], diff_w_out[j],
                                diff_lambda_q1[j], diff_lambda_k1[j],
                                diff_lambda_q2[j], diff_lambda_k2[j],
                                diff_subln_g[j], rel_bias, lambda_init)
        x = x + rmsnorm(y, post_norm_g[i])
    return x
```

```python
import math
from contextlib import ExitStack

import numpy as np
import concourse.bass as bass
import concourse.mybir as mybir
from concourse.bass_utils import run_bass_kernel_spmd

F32 = mybir.dt.float32
BF16 = mybir.dt.bfloat16
AF = mybir.ActivationFunctionType
ALU = mybir.AluOpType
AX = mybir.AxisListType

D_MODEL = 1024
BATCH = 2
SEQ = 8192
D_INNER = 2048
RMS_EPS = 1e-6
GN_EPS = 1e-5
NCORES = 8
NEG = -30000.0

SEM_LIMIT = 30000


class Sched:
    def __init__(self, nc, phase=None):
        self.nc = nc
        self.phase = phase
        self.ops = []
        self.last_w = {}
        self.readers = {}
        self.base = set()

    def op(self, eng, fn, reads=(), writes=(), dkey=None, inc=16):
        idx = len(self.ops)
        deps = set(self.base)
        for b in reads:
            if b in self.last_w:
                deps.add(self.last_w[b])
        for b in writes:
            if b in self.last_w:
                deps.add(self.last_w[b])
            for r in self.readers.get(b, ()):
                deps.add(r)
        for b in reads:
            self.readers.setdefault(b, []).append(idx)
        for b in writes:
            self.last_w[b] = idx
            self.readers[b] = []
        deps.discard(idx)
        self.ops.append(dict(eng=eng, fn=fn, deps=sorted(deps), dkey=dkey, inc=inc))
        return idx

    def dma(self, eng, out, in_, reads=(), writes=(), dkey=None):
        assert dkey is not None
        return self.op(eng, lambda e, o=out, i=in_: e.dma_start(out=o, in_=i), reads, writes, dkey)

    def barrier(self):
        last = {}
        for i, o in enumerate(self.ops):
            if o['dkey'] is not None:
                last[('d', o['dkey'])] = i
            else:
                last[('e', o['eng'])] = i
        self.base = set(last.values())
        self.last_w = {}
        self.readers = {}

    def emit(self):
        nc = self.nc
        ops = self.ops
        n = len(ops)
        need = [False] * n
        for o in ops:
            for d in o['deps']:
                po = ops[d]
                if po['dkey'] is None and o['dkey'] is None and po['eng'] == 'pe' and o['eng'] == 'pe':
                    continue
                need[d] = True
        for i, o in enumerate(ops):
            if o['dkey'] is not None:
                need[i] = True
        last_op = {}
        for i, o in enumerate(ops):
            last_op[o['eng']] = i
        for e_, i in last_op.items():
            need[i] = True
        eng_cnt = {}
        eng_gen = {}
        sem_names = []
        sig = [None] * n
        dcount = {}
        dgen = {}
        for i, o in enumerate(ops):
            if not need[i]:
                continue
            if o['dkey'] is not None:
                k = o['dkey']
                c = dcount.get(k, 0) + o['inc']
                g = dgen.get(k, 0)
                if c > SEM_LIMIT:
                    g += 1
                    c = o['inc']
                dcount[k] = c
                dgen[k] = g
                name = ('d', k, g)
                sig[i] = (name, c, o['inc'])
            else:
                e = o['eng']
                c = eng_cnt.get(e, 0) + 1
                g = eng_gen.get(e, 0)
                if c > SEM_LIMIT:
                    g += 1
                    c = 1
                eng_cnt[e] = c
                eng_gen[e] = g
                name = ('e', e, g)
                sig[i] = (name, c, 1)
            if name not in sem_names:
                sem_names.append(name)
        final = {}
        for i in range(n):
            if sig[i] is not None and ops[i]['dkey'] is not None:
                final[sig[i][0]] = max(final.get(sig[i][0], 0), sig[i][1])
        with ExitStack() as st:
            sems = {}
            for j, name in enumerate(sem_names):
                spfx = f"p{self.phase[1]}" if self.phase is not None else ""
                sems[name] = st.enter_context(nc.semaphore(f"{spfx}s{j}"))
            block = st.enter_context(nc.Block())
            per = {e: [] for e in ('pe', 'act', 'dve', 'pool', 'sp')}
            for i, o in enumerate(ops):
                per[o['eng']].append(i)

            def run(engname, eng):
                waited = {}
                if self.phase is not None and self.phase[1] > 0:
                    eng.wait_ge(self.phase[0], self.phase[1])
                for i in per[engname]:
                    o = ops[i]
                    wl = {}
                    for d in o['deps']:
                        po = ops[d]
                        if po['dkey'] is None and o['dkey'] is None and po['eng'] == 'pe' and engname == 'pe':
                            continue
                        nm, c, _ = sig[d]
                        if waited.get(nm, 0) >= c:
                            continue
                        wl[nm] = max(wl.get(nm, 0), c)
                    for nm, c in wl.items():
                        eng.wait_ge(sems[nm], c)
                        waited[nm] = c
                    ins = o['fn'](eng)
                    if sig[i] is not None:
                        ins.then_inc(sems[sig[i][0]], sig[i][2])
                if engname == 'pool':
                    fin = dict(final)
                    for e_, i in last_op.items():
                        nm, c, _ = sig[i]
                        fin[nm] = max(fin.get(nm, 0), c)
                    for nm, c in fin.items():
                        if waited.get(nm, 0) < c:
                            eng.wait_ge(sems[nm], c)
                    lastc = None
                    for nm in sem_names:
                        lastc = eng.sem_clear(sems[nm])
                    if self.phase is not None:
                        if self.phase[2]:
                            eng.sem_clear(self.phase[0])
                        else:
                            lastc.then_inc(self.phase[0], 1)

            @block.tensor
            def _(e):
                run('pe', e)

            @block.scalar
            def _(e):
                run('act', e)

            @block.vector
            def _(e):
                run('dve', e)

            @block.gpsimd
            def _(e):
                run('pool', e)

            @block.sync
            def _(e):
                run('sp', e)


def _dt(nc, io, name, shape, dt, kind):
    if io is not None and name in io:
        return io[name]
    if kind == "Internal":
        return nc.dram_tensor(name, shape, dt).ap()
    return nc.dram_tensor(name, shape, dt, kind=kind).ap()


RET_DK = 256
RET_DV = 512
RET_COLS = 2 * RET_DK + 2 * RET_DV
CH = 128
NCH = SEQ // CH


def build_l0(seq=SEQ, dbg=False, nc=None, io=None, phase=None):
    nchunks = seq // CH
    if nc is None:
        nc = bass.Bass("TRN2", target_bir_lowering=False)
    x_d = _dt(nc, io, "x", [seq, D_MODEL], F32, "ExternalInput")
    gpre_d = _dt(nc, io, "gpre", [128, 8], F32, "ExternalInput")
    win_d = _dt(nc, io, "win", [D_MODEL, RET_COLS], F32, "ExternalInput")
    wout_d = _dt(nc, io, "wout", [RET_DV, D_MODEL], F32, "ExternalInput")
    tab_d = _dt(nc, io, "tab", [seq, 1024], F32, "ExternalInput")
    mask_d = _dt(nc, io, "mask", [128, 128], F32, "ExternalInput")
    ident_d = _dt(nc, io, "ident", [128, 128], F32, "ExternalInput")
    gc_d = _dt(nc, io, "gc", [128, 1], F32, "ExternalInput")
    part_d = _dt(nc, io, "part", [seq, D_MODEL], F32, "ExternalOutput")
    if dbg:
        dbg_d = {nm: nc.dram_tensor("dbg_" + nm, shp, dt, kind="ExternalOutput").ap() for nm, shp, dt in (
            ("xn", [128, 1024], BF16), ("hT", [128, 1024], BF16), ("qk", [128, 512], BF16), ("vb", [128, 512], BF16),
            ("sg", [128, 512], F32), ("sT", [128, 128], BF16), ("yn", [128, 512], F32), ("gated", [128, 512], BF16),
            ("qkT", [128, 512], BF16), ("win", [128, 8 * RET_COLS], BF16), ("rstd", [128, 1], F32))}

    with ExitStack() as st:
        pfx = f"p{phase[1]}_" if phase is not None else ""

        def sb(name, shape, dt):
            return st.enter_context(nc.sbuf_tensor(pfx + name, shape, dt))

        def ps(name, shape, dt):
            return st.enter_context(nc.psum_tensor(pfx + name, shape, dt))

        win = sb("win_bf", [128, 8, RET_COLS], BF16)
        wout = sb("wout_bf", [128, 4, D_MODEL], BF16)
        wst = [sb(f"wst{i}", [128, 1024], F32) for i in range(2)]
        gpre = sb("gpre_s", [128, 8], F32)
        mask = sb("mask_s", [128, 128], F32)
        ident_f = sb("ident_f", [128, 128], F32)
        ident = sb("ident_b", [128, 128], BF16)
        gc = sb("gc_s", [128, 1], F32)
        nhalf = sb("nhalf", [128, 1], F32)
        NX = 3
        xt = [sb(f"xt{i}", [128, D_MODEL], F32) for i in range(NX)]
        tabt = [sb(f"tab{i}", [128, 1024], F32) for i in range(2)]
        sq_junk = sb("sqj", [128, D_MODEL], BF16)
        ssum = [sb(f"ss{i}", [128, 1], F32) for i in range(2)]
        rstd = [sb(f"rstd{i}", [128, 1], F32) for i in range(2)]
        xn = [sb(f"xn{i}", [128, D_MODEL], BF16) for i in range(2)]
        hT = [sb(f"hT{i}", [128, 8, 128], BF16) for i in range(2)]
        qkr = [sb(f"qkr{i}", [128, 512], F32) for i in range(2)]
        e1 = [sb(f"e1_{i}", [128, 512], F32) for i in range(2)]
        e2 = [sb(f"e2_{i}", [128, 512], F32) for i in range(2)]
        qk = [sb(f"qk{i}", [128, 512], BF16) for i in range(2)]
        qkT = [sb(f"qkT{i}", [128, 4, 128], BF16) for i in range(2)]
        vb = [sb(f"vb{i}", [128, 512], BF16) for i in range(2)]
        sg = [sb(f"sg{i}", [128, 512], F32) for i in range(2)]
        sT = [sb(f"sT{i}", [128, 128], BF16) for i in range(2)]
        U = sb("U", [128, 2, 512], F32)
        Rb = [sb(f"Rb{i}", [128, 2, 512], BF16) for i in range(2)]
        stats = [sb(f"st{i}", [128, 6], F32) for i in range(2)]
        mv = [sb(f"mv{i}", [128, 2], F32) for i in range(2)]
        grs = [sb(f"grs{i}", [128, 1], F32) for i in range(2)]
        gnb = [sb(f"gnb{i}", [128, 1], F32) for i in range(2)]
        yn = [sb(f"yn{i}", [128, 512], F32) for i in range(2)]
        gated = [sb(f"gated{i}", [128, 512], BF16) for i in range(2)]
        gT = sb("gT_all", [128, 4, seq], BF16)
        ost = [sb(f"ost{i}", [128, D_MODEL], F32) for i in range(2)]

        p_xT = ps("p_xT", [128, 8, 128], BF16)
        p_qk = ps("p_qk", [128, 512], F32)
        p_v = ps("p_v", [128, 512], F32)
        p_g = ps("p_g", [128, 512], F32)
        p_m = ps("p_m", [128, 512], F32)
        p_mb = p_m[:].bitcast(BF16)
        p_A = [ps(f"p_A{i}", [128, 512], F32) for i in range(2)]
        p_o = ps("p_o", [128, 512], F32)

        S = Sched(nc, phase)

        S.dma('sp', gpre[:], gpre_d, writes=['gpre'], dkey=('c0', 1))
        S.dma('sp', mask[:], mask_d, writes=['mask'], dkey=('c0', 2))
        S.dma('sp', ident_f[:], ident_d, writes=['identf'], dkey=('c0', 3))
        S.dma('sp', gc[:], gc_d, writes=['gc'], dkey=('c0', 4))
        S.op('dve', lambda e: e.tensor_copy(ident[:], ident_f[:]), reads=['identf'], writes=['ident'])
        S.op('dve', lambda e: e.memset(U[:], 0.0), writes=['U'])
        S.op('dve', lambda e: e.memset(nhalf[:], -0.5), writes=['nhalf'])
        S.op('dve', lambda e: e.memset(Rb[0][:], 0.0), writes=[('Rb', 0)])
        k = 0
        for c in range(8):
            for (c0, c1) in ((0, 1024), (1024, RET_COLS)):
                w = wst[k % 2]
                S.dma('sp', w[:, 0:c1 - c0], win_d[c * 128:(c + 1) * 128, c0:c1],
                      writes=[('wst', k % 2)], dkey=('wst', k % 2))
                S.op('dve', lambda e, w=w, c=c, c0=c0, c1=c1: e.tensor_scalar(
                    win[:, c, c0:c1], w[:, 0:c1 - c0], gpre[:, c:c + 1], None, ALU.mult),
                    reads=[('wst', k % 2), 'gpre'], writes=['win'])
                k += 1
        for j in range(4):
            w = wst[k % 2]
            S.dma('sp', w[:], wout_d[j * 128:(j + 1) * 128, :], writes=[('wst', k % 2)], dkey=('wst', k % 2))
            S.op('dve', lambda e, w=w, j=j: e.tensor_copy(wout[:, j, :], w[:]),
                 reads=[('wst', k % 2)], writes=['wout'])
            k += 1

        def s1a(i):
            a = i % NX
            b2 = i % 2
            t0 = i * CH
            S.dma('sp', xt[a][:], x_d[t0:t0 + CH, :], writes=[('xt', a)], dkey=('xt', a))
            S.dma('sp', tabt[b2][:], tab_d[t0:t0 + CH, :], writes=[('tab', b2)], dkey=('tab', b2))
            S.op('act', lambda e: e.activation(sq_junk[:], xt[a][:], AF.Square, scale=1.0 / 32.0,
                                               accum_out=ssum[b2][:]),
                 reads=[('xt', a)], writes=['sqj', ('ss', b2)])
            S.op('dve', lambda e: e.tensor_scalar(ssum[b2][:], ssum[b2][:], RMS_EPS, None, ALU.add),
                 reads=[('ss', b2)], writes=[('ss', b2)])
            S.op('pool', lambda e: e.tensor_tensor(rstd[b2][:], ssum[b2][:], nhalf[:], ALU.pow),
                 reads=[('ss', b2), 'nhalf'], writes=[('rstd', b2)])
            S.op('act', lambda e: e.activation(xn[b2][:], xt[a][:], AF.Copy, scale=rstd[b2][:]),
                 reads=[('xt', a), ('rstd', b2)], writes=[('xn', b2)])
            for c in range(8):
                S.op('pe', lambda e, c=c: e.transpose(p_xT[:, c, :], xn[b2][:, c * 128:(c + 1) * 128], ident[:]),
                     reads=[('xn', b2), 'ident'], writes=['p_xT'])
            S.op('dve', lambda e: e.tensor_copy(hT[b2][:], p_xT[:]), reads=['p_xT'], writes=[('hT', b2)])

        def s1b(i):
            b2 = i % 2
            for (pt, nm, c0) in ((p_qk, 'p_qk', 0), (p_v, 'p_v', 512), (p_g, 'p_g', 1024)):
                for c in range(8):
                    S.op('pe', lambda e, pt=pt, c=c, c0=c0: e.matmul(
                        pt[:], hT[b2][:, c, :], win[:, c, c0:c0 + 512], start=(c == 0), stop=(c == 7)),
                        reads=[('hT', b2), 'win'], writes=[nm])

        def s2a(i):
            b2 = i % 2
            S.op('act', lambda e: e.activation(qkr[b2][:], p_qk[:], AF.Copy), reads=['p_qk'], writes=[('qkr', b2)])
            S.op('act', lambda e: e.activation(vb[b2][:], p_v[:], AF.Copy), reads=['p_v'], writes=[('vb', b2)])
            S.op('act', lambda e: e.activation(sg[b2][:], p_g[:], AF.Silu), reads=['p_g'], writes=[('sg', b2)])

        def s2b(i):
            b2 = i % 2
            tA = tabt[b2][:, 0:512]
            tB = tabt[b2][:, 512:1024]
            S.op('dve', lambda e: e.tensor_tensor(e1[b2][:], qkr[b2][:], tA, ALU.mult),
                 reads=[('qkr', b2), ('tab', b2)], writes=[('e1', b2)])
            pq = qkr[b2][:].rearrange("p (a h d) -> p a h d", a=2, h=2)
            tBv = tB.rearrange("p (a h d) -> p a h d", a=2, h=2)
            e2v = e2[b2][:].rearrange("p (a h d) -> p a h d", a=2, h=2)
            S.op('pool', lambda e: e.tensor_tensor(e2v[:, :, 0, :], pq[:, :, 1, :], tBv[:, :, 0, :], ALU.mult),
                 reads=[('qkr', b2), ('tab', b2)], writes=[('e2', b2, 0)])
            S.op('pool', lambda e: e.tensor_tensor(e2v[:, :, 1, :], pq[:, :, 0, :], tBv[:, :, 1, :], ALU.mult),
                 reads=[('qkr', b2), ('tab', b2)], writes=[('e2', b2, 1)])
            S.op('dve', lambda e: e.tensor_tensor(qk[b2][:], e1[b2][:], e2[b2][:], ALU.add),
                 reads=[('e1', b2), ('e2', b2, 0), ('e2', b2, 1)], writes=[('qk', b2)])
            for c in range(4):
                S.op('pe', lambda e, c=c: e.transpose(p_mb[:, c * 128:(c + 1) * 128],
                                                      qk[b2][:, c * 128:(c + 1) * 128], ident[:]),
                     reads=[('qk', b2), 'ident'], writes=['p_m'])
            S.op('dve', lambda e: e.tensor_copy(qkT[b2][:].rearrange("p a d -> p (a d)"), p_mb[:, 0:512]),
                 reads=['p_m'], writes=[('qkT', b2)])

        def s3(i):
            b2 = i % 2
            rb = i % 2
            for dc in range(2):
                S.op('pe', lambda e, dc=dc: e.matmul(p_m[:, 0:128], qkT[b2][:, 2 + dc, :], qkT[b2][:, dc, :],
                                                     start=(dc == 0), stop=(dc == 1)),
                     reads=[('qkT', b2)], writes=['p_m'])
            S.op('dve', lambda e: e.tensor_tensor(sT[b2][:], p_m[:, 0:128], mask[:], ALU.mult),
                 reads=['p_m', 'mask'], writes=[('sT', b2)])
            for dc in range(2):
                S.op('pe', lambda e, dc=dc: e.matmul(p_A[dc][:], qk[b2][:, 256 + dc * 128:256 + (dc + 1) * 128],
                                                     vb[b2][:], start=True, stop=True),
                     reads=[('qk', b2), ('vb', b2)], writes=[('p_A', dc)])
            S.op('pe', lambda e: e.matmul(p_o[:], sT[b2][:], vb[b2][:], start=True, stop=False),
                 reads=[('sT', b2), ('vb', b2)], writes=['p_o'])
            for dc in range(2):
                S.op('pe', lambda e, dc=dc: e.matmul(p_o[:], qkT[b2][:, dc, :], Rb[rb][:, dc, :],
                                                     start=False, stop=(dc == 1)),
                     reads=[('qkT', b2), ('Rb', rb)], writes=['p_o'])
            for dc in range(2):
                S.op('dve', lambda e, dc=dc: e.scalar_tensor_tensor(
                    U[:, dc, :], U[:, dc, :], gc[:, 0:1], p_A[dc][:], ALU.mult, ALU.add),
                    reads=['U', ('p_A', dc), 'gc'], writes=['U'])
            S.op('act', lambda e: e.activation(Rb[1 - rb][:].rearrange("p a d -> p (a d)"),
                                               U[:].rearrange("p a d -> p (a d)"), AF.Copy, scale=gc[:, 0:1]),
                 reads=['U', 'gc'], writes=[('Rb', 1 - rb)])

        def s4a(i):
            b2 = i % 2
            t0 = i * CH
            S.op('dve', lambda e: e.bn_stats(stats[b2][:], p_o[:]), reads=['p_o'], writes=[('stats', b2)])
            S.op('dve', lambda e: e.bn_aggr(mv[b2][:], stats[b2][:]), reads=[('stats', b2)], writes=[('mv', b2)])
            S.op('dve', lambda e: e.tensor_scalar(mv[b2][:, 1:2], mv[b2][:, 1:2], GN_EPS, None, ALU.add),
                 reads=[('mv', b2)], writes=[('mv', b2)])
            S.op('pool', lambda e: e.tensor_tensor(grs[b2][:], mv[b2][:, 1:2], nhalf[:], ALU.pow),
                 reads=[('mv', b2), 'nhalf'], writes=[('grs', b2)])
            S.op('dve', lambda e: e.scalar_tensor_tensor(gnb[b2][:], mv[b2][:, 0:1], -1.0, grs[b2][:],
                                                         ALU.mult, ALU.mult),
                 reads=[('mv', b2), ('grs', b2)], writes=[('gnb', b2)])
            S.op('act', lambda e: e.activation(yn[b2][:], p_o[:], AF.Identity, bias=gnb[b2][:], scale=grs[b2][:]),
                 reads=['p_o', ('gnb', b2), ('grs', b2)], writes=[('yn', b2)])
            S.op('pool', lambda e: e.tensor_tensor(gated[b2][:], yn[b2][:], sg[b2][:], ALU.mult),
                 reads=[('yn', b2), ('sg', b2)], writes=[('gated', b2)])

        def s4b(i):
            b2 = i % 2
            t0 = i * CH
            for c in range(4):
                S.op('pe', lambda e, c=c: e.transpose(p_mb[:, 512 + c * 128:512 + (c + 1) * 128],
                                                      gated[b2][:, c * 128:(c + 1) * 128], ident[:]),
                     reads=[('gated', b2), 'ident'], writes=['p_m'])
            S.op('act', lambda e: e.activation(
                gT[:, :, t0:t0 + CH], p_mb[:, 512:1024].rearrange("p (a d) -> p a d", a=4), AF.Copy),
                reads=['p_m'], writes=[('gT', i)])

        s1a(0)
        s1b(0)
        for i in range(nchunks):
            if i + 1 < nchunks:
                s1a(i + 1)
            s2a(i)
            if i + 1 < nchunks:
                s1b(i + 1)
            s2b(i)
            if i >= 1:
                s4a(i - 1)
                s4b(i - 1)
            s3(i)
            if i == nchunks - 1:
                s4a(i)
                s4b(i)
            if dbg and i == 0:
                for nm, t, rd in (("xn", xn[0][:], ('xn', 0)), ("hT", hT[0][:].rearrange("p a d -> p (a d)"), ('hT', 0)),
                                  ("qk", qk[0][:], ('qk', 0)), ("vb", vb[0][:], ('vb', 0)), ("sg", sg[0][:], ('sg', 0)),
                                  ("sT", sT[0][:], ('sT', 0)), ("yn", yn[0][:], ('yn', 0)), ("gated", gated[0][:], ('gated', 0)),
                                  ("qkT", qkT[0][:].rearrange("p a d -> p (a d)"), ('qkT', 0)),
                                  ("win", win[:].rearrange("p a d -> p (a d)"), 'win'), ("rstd", rstd[0][:], ('rstd', 0))):
                    S.dma('sp', dbg_d[nm], t, reads=[rd], dkey=('dbg', nm))

        S.barrier()
        pp = [p_qk, p_v, p_g, p_o]
        for i in range(nchunks):
            t0 = i * CH
            b2 = i % 2
            for half in range(2):
                pt = pp[b2 * 2 + half]
                nm = ('pp', b2 * 2 + half)
                for j in range(4):
                    S.op('pe', lambda e, pt=pt, j=j, half=half, t0=t0: e.matmul(
                        pt[:], gT[:, j, t0:t0 + CH], wout[:, j, half * 512:(half + 1) * 512],
                        start=(j == 0), stop=(j == 3)), reads=[], writes=[nm])
                if half == 0:
                    S.op('act', lambda e, pt=pt, b2=b2: e.activation(ost[b2][:, 0:512], pt[:], AF.Copy),
                         reads=[nm], writes=[('ost', b2, 0)])
                else:
                    S.op('dve', lambda e, pt=pt, b2=b2: e.tensor_copy(ost[b2][:, 512:1024], pt[:]),
                         reads=[nm], writes=[('ost', b2, 1)])
            S.dma('sp', part_d[t0:t0 + CH, :], ost[b2][:], reads=[('ost', b2, 0), ('ost', b2, 1)],
                  writes=[], dkey=('ost', b2))
        S.emit()
    return nc


def rope_decay_table(head, seq=SEQ):
    half = RET_DK // 2
    inv = (10000.0 ** (-np.arange(half, dtype=np.float64) / half))
    pos = np.arange(seq, dtype=np.float64)
    ang = pos[:, None] * inv[None, :]
    c, s = np.cos(ang), np.sin(ang)
    lg = math.log1p(-2.0 ** (-5.0 - head))
    cidx = (np.arange(seq) % CH).astype(np.float64)
    fq = np.exp(lg * (cidx + 1.0))[:, None]
    fk = np.exp(-lg * (cidx + 1.0))[:, None] * (RET_DK ** -0.5)
    A = np.concatenate([c * fq, c * fq, c * fk, c * fk], axis=1)
    B = np.concatenate([-s * fq, s * fq, -s * fk, s * fk], axis=1)
    return np.concatenate([A, B], axis=1).astype(np.float32)


def l0_inputs(x, pre_norm_g, ret_w_in, ret_w_out, seq=SEQ):
    maps = []
    maskT = np.triu(np.ones((CH, CH), np.float32))
    ident = np.eye(128, dtype=np.float32)
    gpre = np.ascontiguousarray(pre_norm_g[0].reshape(8, 128).T)
    for core in range(NCORES):
        b, h = core // 4, core % 4
        w = ret_w_in[0]
        cols = np.concatenate([
            np.arange(h * RET_DK, (h + 1) * RET_DK),
            1024 + np.arange(h * RET_DK, (h + 1) * RET_DK),
            2048 + np.arange(h * RET_DV, (h + 1) * RET_DV),
            2048 + D_INNER + np.arange(h * RET_DV, (h + 1) * RET_DV)])
        gC = math.exp(math.log1p(-2.0 ** (-5.0 - h)) * CH)
        maps.append({
            "x": np.ascontiguousarray(x[b][:seq]),
            "gpre": gpre,
            "win": np.ascontiguousarray(w[:, cols]),
            "wout": np.ascontiguousarray(ret_w_out[0][h * RET_DV:(h + 1) * RET_DV, :]),
            "tab": rope_decay_table(h, seq),
            "mask": maskT,
            "ident": ident,
            "gc": np.full((128, 1), gC, np.float32),
        })
    return maps


LAMBDA_INIT = 0.8 - 0.6 * math.exp(-0.3 * 1)
DH = 64
QT = 512


def build_l1(seq=SEQ, dbg=False, nc=None, io=None, phase=None):
    nt = seq // QT
    nch = seq // CH
    if nc is None:
        nc = bass.Bass("TRN2", target_bir_lowering=False)
    if io is not None and 'xsrc' in io:
        xsrc = io['xsrc']
    else:
        x_d = _dt(nc, io, "x", [seq, D_MODEL], F32, "ExternalInput")
        xsrc = lambda t0: x_d[t0:t0 + CH, :]
    gpre_d = _dt(nc, io, "gpre", [128, 8], F32, "ExternalInput")
    win_d = _dt(nc, io, "win", [D_MODEL, 2048], F32, "ExternalInput")
    wout_d = _dt(nc, io, "wout", [512, D_MODEL], F32, "ExternalInput")
    bias_d = _dt(nc, io, "bias", [4, 128, 1024], F32, "ExternalInput")
    cb_d = _dt(nc, io, "cb", [128, 4], F32, "ExternalInput")
    lamv_d = _dt(nc, io, "lamv", [128, 4, DH], F32, "ExternalInput")
    gsub_d = _dt(nc, io, "gsub", [128, 1], F32, "ExternalInput")
    ident_d = _dt(nc, io, "ident", [128, 128], F32, "ExternalInput")
    part_d = _dt(nc, io, "part", [seq, D_MODEL], F32, "ExternalOutput")
    qT_d = _dt(nc, io, "qT_scr", [4, 128, seq], BF16, "Internal")
    kT_d = _dt(nc, io, "kT_scr", [4, 128, seq], BF16, "Internal")
    sgT_d = _dt(nc, io, "sgT_scr", [4, 128, seq], BF16, "Internal")
    v_d = _dt(nc, io, "v_scr", [seq, 512], BF16, "Internal")
    if dbg:
        dbg_d = {nm: nc.dram_tensor("dbg_" + nm, shp, dt, kind="ExternalOutput").ap() for nm, shp, dt in (
            ("gT", [128, 4 * seq], BF16), ("nlam", [128, 1], F32), ("qTd", [4, 128, seq], BF16),
            ("sgTd", [4, 128, seq], BF16), ("kTd", [4, 128, seq], BF16))}

    with ExitStack() as st:
        pfx = f"p{phase[1]}_" if phase is not None else ""

        def sb(name, shape, dt):
            return st.enter_context(nc.sbuf_tensor(pfx + name, shape, dt))

        def ps(name, shape, dt):
            return st.enter_context(nc.psum_tensor(pfx + name, shape, dt))

        big = sb("big", [128, 4 * SEQ], BF16)
        win = big[:, 0:8 * 2048].rearrange("p (a d) -> p a d", a=8)
        gT = big[:, 0:4 * seq].rearrange("p (a d) -> p a d", a=4)
        wout = sb("wout_bf", [128, 4, D_MODEL], BF16)
        wst = [sb(f"wst{i}", [128, 1024], F32) for i in range(2)]
        gpre = sb("gpre_s", [128, 8], F32)
        ident_f = sb("ident_f", [128, 128], F32)
        ident = sb("ident_b", [128, 128], BF16)
        ones = sb("ones_b", [128, 128], BF16)
        nhalf = sb("nhalf", [128, 1], F32)
        epsg = sb("epsg", [128, 1], F32)
        Phi = [sb(f"Phi{m}", [128, QT], BF16) for m in range(2)]
        Plo = [sb(f"Plo{m}", [128, QT], BF16) for m in range(2)]
        Pacc = [sb(f"Pacc{m}", [128, QT], F32) for m in range(2)]
        PaccB = sb("PaccB", [128, QT], F32)
        cb = sb("cb_s", [128, 4], F32)
        lamv = sb("lamv_s", [128, 4, DH], F32)
        lamp = sb("lamp", [128, 2, DH], F32)
        lams = sb("lams", [128, 2], F32)
        nlam = sb("nlam", [128, 1], F32)
        gsc = sb("gsc", [128, 1], F32)
        NX = 3
        xt = [sb(f"xt{i}", [128, D_MODEL], F32) for i in range(NX)]
        sq_junk = sb("sqj", [128, D_MODEL], BF16)
        ssum = [sb(f"ss{i}", [128, 1], F32) for i in range(2)]
        rstd = [sb(f"rstd{i}", [128, 1], F32) for i in range(2)]
        xn = [sb(f"xn{i}", [128, D_MODEL], BF16) for i in range(2)]
        hT = [sb(f"hT{i}", [128, 8, QT], BF16) for i in range(2)]
        NF = 4
        fst = [sb(f"fst{i}", [128, QT], BF16) for i in range(NF)]
        vst = [sb(f"vst{i}", [128, 512], BF16) for i in range(2)]
        kT = sb("kT_s", [128, seq], BF16)
        vS = sb("v_s", [128, nch, 128], BF16)
        bt = sb("bt_s", [128, 1024], F32)
        qt = [sb(f"qt{i}", [128, QT], BF16) for i in range(2)]
        sgt = [sb(f"sgt{i}", [128, QT], BF16) for i in range(2)]
        NP = 3
        PT = [[sb(f"PT{m}_{i}", [128, QT], BF16) for i in range(NP)] for m in range(2)]
        tmp = [[sb(f"tmp{m}_{i}", [128, QT], F32) for i in range(2)] for m in range(2)]
        O1s = sb("O1s", [128, QT], F32)
        O2s = sb("O2s", [128, QT], F32)
        s1s = sb("s1s", [128, QT], F32)
        s2s = sb("s2s", [128, QT], F32)
        osb = sb("osb", [128, QT], F32)
        sqb = sb("sqb", [128, QT], BF16)
        rsb = sb("rsb", [128, QT], F32)
        ost = [sb(f"ost{i}", [128, D_MODEL], F32) for i in range(2)]

        banks = [ps(f"bk{i}", [128, 512], F32) for i in range(8)]

        S = Sched(nc, phase)
        S.dma('sp', gpre[:], gpre_d, writes=['gpre'], dkey=('c0', 5))
        S.dma('sp', ident_f[:], ident_d, writes=['identf'], dkey=('c0', 6))
        S.dma('sp', cb[:], cb_d, writes=['cb'], dkey=('c0', 7))
        S.dma('sp', lamv[:], lamv_d, writes=['lamv'], dkey=('c0', 8))
        S.dma('sp', gsc[:], gsub_d, writes=['gsc'], dkey=('c0', 9))
        S.op('dve', lambda e: e.tensor_copy(ident[:], ident_f[:]), reads=['identf'], writes=['ident'])
        S.op('dve', lambda e: e.memset(ones[:], 1.0), writes=['ones'])
        S.op('dve', lambda e: e.memset(nhalf[:], -0.5), writes=['nhalf'])
        S.op('dve', lambda e: e.memset(epsg[:], GN_EPS), writes=['epsg'])
        S.op('dve', lambda e: e.tensor_tensor(lamp[:, 0, :], lamv[:, 0, :], lamv[:, 1, :], ALU.mult),
             reads=['lamv'], writes=['lamp'])
        S.op('dve', lambda e: e.tensor_tensor(lamp[:, 1, :], lamv[:, 2, :], lamv[:, 3, :], ALU.mult),
             reads=['lamv'], writes=['lamp'])
        S.op('dve', lambda e: e.reduce_sum(lams[:], lamp[:], AX.X), reads=['lamp'], writes=['lams'])
        S.op('act', lambda e: e.activation(lams[:], lams[:], AF.Exp), reads=['lams'], writes=['lams'])
        S.op('dve', lambda e: e.tensor_tensor(nlam[:], lams[:, 1:2], lams[:, 0:1], ALU.subtract),
             reads=['lams'], writes=['nlam'])
        S.op('dve', lambda e: e.tensor_scalar(nlam[:], nlam[:], -LAMBDA_INIT, None, ALU.add),
             reads=['nlam'], writes=['nlam'])
        S.op('dve', lambda e: e.tensor_scalar(gsc[:], gsc[:], 1.0 - LAMBDA_INIT, None, ALU.mult),
             reads=['gsc'], writes=['gsc'])
        k = 0
        for c in range(8):
            for c0 in (0, 1024):
                w = wst[k % 2]
                S.dma('sp', w[:], win_d[c * 128:(c + 1) * 128, c0:c0 + 1024],
                      writes=[('wst', k % 2)], dkey=('wst', k % 2))
                S.op('dve', lambda e, w=w, c=c, c0=c0: e.tensor_scalar(
                    win[:, c, c0:c0 + 1024], w[:], gpre[:, c:c + 1], None, ALU.mult),
                    reads=[('wst', k % 2), 'gpre'], writes=['win'])
                k += 1
        for j in range(4):
            w = wst[k % 2]
            S.dma('sp', w[:], wout_d[j * 128:(j + 1) * 128, :], writes=[('wst', k % 2)], dkey=('wst', k % 2))
            S.op('dve', lambda e, w=w, j=j: e.tensor_copy(wout[:, j, :], w[:]),
                 reads=[('wst', k % 2)], writes=['wout'])
            k += 1

        cnt = {'f': 0, 'v': 0, 'x': 0}

        def p1_norm(ti, sub):
            i = cnt['x']
            cnt['x'] += 1
            a = i % NX
            b2 = i % 2
            tb = ti % 2
            t0 = ti * QT + sub * CH
            pxT = banks[b2][:].bitcast(BF16).rearrange("p (a d) -> p a d", a=8)
            S.dma('sp', xt[a][:], xsrc(t0), writes=[('xt', a)], dkey=('xt', a))
            S.op('act', lambda e: e.activation(sq_junk[:], xt[a][:], AF.Square, scale=1.0 / 32.0,
                                               accum_out=ssum[b2][:]),
                 reads=[('xt', a)], writes=['sqj', ('ss', b2)])
            S.op('dve', lambda e: e.tensor_scalar(ssum[b2][:], ssum[b2][:], RMS_EPS, None, ALU.add),
                 reads=[('ss', b2)], writes=[('ss', b2)])
            S.op('pool', lambda e: e.tensor_tensor(rstd[b2][:], ssum[b2][:], nhalf[:], ALU.pow),
                 reads=[('ss', b2), 'nhalf'], writes=[('rstd', b2)])
            S.op('act', lambda e: e.activation(xn[b2][:], xt[a][:], AF.Copy, scale=rstd[b2][:]),
                 reads=[('xt', a), ('rstd', b2)], writes=[('xn', b2)])
            for c in range(8):
                S.op('pe', lambda e, c=c: e.transpose(pxT[:, c, :], xn[b2][:, c * 128:(c + 1) * 128], ident[:]),
                     reads=[('xn', b2), 'ident'], writes=[('bk', b2)])
            S.op('dve', lambda e: e.tensor_copy(hT[tb][:, :, sub * CH:(sub + 1) * CH], pxT),
                 reads=[('bk', b2)], writes=[('hT', tb, sub)])

        def p1_v(ti, sub):
            tb = ti % 2
            kk = cnt['v']
            cnt['v'] += 1
            bkx = 2 + kk % 2
            pv = banks[bkx]
            t0 = ti * QT + sub * CH
            for c in range(8):
                S.op('pe', lambda e, c=c: e.matmul(pv[:], hT[tb][:, c, sub * CH:(sub + 1) * CH],
                                                   win[:, c, 1536:2048], start=(c == 0), stop=(c == 7)),
                     reads=[('hT', tb, sub), 'win'], writes=[('bk', bkx)])
            vs = vst[kk % 2]
            S.op('dve', lambda e: e.tensor_copy(vs[:], pv[:]), reads=[('bk', bkx)], writes=[('vst', kk % 2)])
            S.dma('sp', v_d[t0:t0 + CH, :], vs[:], reads=[('vst', kk % 2)], dkey=('vst', kk % 2))

        def p1_f(ti, gi):
            tb = ti % 2
            kk = cnt['f']
            cnt['f'] += 1
            bkx = 4 + kk % 4
            pf = banks[bkx]
            t0 = ti * QT
            for c in range(8):
                S.op('pe', lambda e, c=c: e.matmul(pf[:], win[:, c, gi * 128:(gi + 1) * 128], hT[tb][:, c, :],
                                                   start=(c == 0), stop=(c == 7)),
                     reads=[('hT', tb, 0), ('hT', tb, 1), ('hT', tb, 2), ('hT', tb, 3), 'win'],
                     writes=[('bk', bkx)])
            fs = fst[kk % NF]
            h = gi % 4
            if gi < 4:
                S.op('act', lambda e: e.activation(fs[:], pf[:], AF.Copy, scale=DH ** -0.5),
                     reads=[('bk', bkx)], writes=[('fst', kk % NF)])
                dst = qT_d[h, :, t0:t0 + QT]
            elif gi < 8:
                S.op('dve', lambda e: e.tensor_copy(fs[:], pf[:]), reads=[('bk', bkx)], writes=[('fst', kk % NF)])
                dst = kT_d[h, :, t0:t0 + QT]
            else:
                S.op('act', lambda e: e.activation(fs[:], pf[:], AF.Silu),
                     reads=[('bk', bkx)], writes=[('fst', kk % NF)])
                dst = sgT_d[h, :, t0:t0 + QT]
            S.dma('sp', dst, fs[:], reads=[('fst', kk % NF)], dkey=('fst', kk % NF))

        for sub in range(4):
            p1_norm(0, sub)
        for ti in range(nt):
            for sub in range(4):
                p1_v(ti, sub)
            for gi in range(12):
                p1_f(ti, gi)
                if ti + 1 < nt and gi in (1, 4, 7, 10):
                    p1_norm(ti + 1, (gi - 1) // 3)

        S.barrier()
        SB = [banks[0], banks[1], banks[2]]
        pSS = banks[3]
        pO = [banks[4], banks[5]]
        pZ = [banks[6], banks[7]]
        items = [(h, qi, j) for h in range(4) for qi in range(nt) for j in range(4 * qi + 4)]
        nit = len(items)

        def geom(t):
            h, qi, j = items[t]
            r = j - 4 * qi
            lo = 128 * r if r > 0 else 0
            qb = (h * nt + qi) % 2
            return h, qi, j, r, lo, qb

        def QK(t):
            h, qi, j, r, lo, qb = geom(t)
            for m in range(2):
                si = (2 * t + m) % 3
                S.op('pe', lambda e, m=m, j=j, lo=lo, si=si, qb=qb: e.matmul(
                    SB[si][:, lo:QT], kT[m * DH:(m + 1) * DH, j * CH:(j + 1) * CH],
                    qt[qb][m * DH:(m + 1) * DH, lo:QT], start=True, stop=True),
                    reads=['kT', ('qt', qb)], writes=[('bk', si)])

        def EXP(t):
            h, qi, j, r, lo, qb = geom(t)
            pb = t % NP
            tb2 = t % 2
            for m in range(2):
                si = (2 * t + m) % 3
                if r >= -1:
                    off = 384 - 128 * r
                    S.op('dve', lambda e, m=m, lo=lo, si=si, off=off, tb2=tb2: e.tensor_tensor(
                        tmp[m][tb2][:, lo:QT], SB[si][:, lo:QT], bt[:, off + lo:off + QT], ALU.add),
                        reads=[('bk', si), 'bt'], writes=[('tmp', m, tb2)])
                    S.op('act', lambda e, m=m, lo=lo, pb=pb, tb2=tb2: e.activation(
                        PT[m][pb][:, lo:QT], tmp[m][tb2][:, lo:QT], AF.Exp),
                        reads=[('tmp', m, tb2)], writes=[('PT', m, pb)])
                else:
                    S.op('act', lambda e, m=m, pb=pb, si=si, h=h: e.activation(
                        PT[m][pb][:], SB[si][:], AF.Exp, bias=cb[:, h:h + 1]),
                        reads=[('bk', si), 'cb'], writes=[('PT', m, pb)])

        def PV(t):
            h, qi, j, r, lo, qb = geom(t)
            pb = t % NP
            nkc = 4 * qi + 4
            for m in range(2):
                S.op('pe', lambda e, m=m, j=j, lo=lo, pb=pb, nkc=nkc: e.matmul(
                    pO[m][:, lo:QT], vS[:, j, :], PT[m][pb][:, lo:QT], start=(j == 0), stop=(j == nkc - 1)),
                    reads=['vS', ('PT', m, pb)], writes=[('bk', 4 + m)])
            if j == 0:
                S.op('dve', lambda e, pb=pb: e.tensor_copy(Pacc[0][:], PT[0][pb][:]),
                     reads=[('PT', 0, pb)], writes=[('Pacc', 0)])
                S.op('dve', lambda e, pb=pb: e.tensor_copy(Pacc[1][:], PT[1][pb][:]),
                     reads=[('PT', 1, pb)], writes=[('Pacc', 1)])
                S.op('pool', lambda e: e.memset(PaccB[:], 0.0), writes=['PaccB'])
            else:
                S.op('dve', lambda e, lo=lo, pb=pb: e.tensor_tensor(
                    Pacc[0][:, lo:QT], Pacc[0][:, lo:QT], PT[0][pb][:, lo:QT], ALU.add),
                    reads=[('PT', 0, pb), ('Pacc', 0)], writes=[('Pacc', 0)])
                if j % 2 == 1:
                    S.op('pool', lambda e, lo=lo, pb=pb: e.tensor_tensor(
                        PaccB[:, lo:QT], PaccB[:, lo:QT], PT[1][pb][:, lo:QT], ALU.add),
                        reads=[('PT', 1, pb), 'PaccB'], writes=['PaccB'])
                else:
                    S.op('dve', lambda e, lo=lo, pb=pb: e.tensor_tensor(
                        Pacc[1][:, lo:QT], Pacc[1][:, lo:QT], PT[1][pb][:, lo:QT], ALU.add),
                        reads=[('PT', 1, pb), ('Pacc', 1)], writes=[('Pacc', 1)])

        def EPI_A(h, qi):
            S.op('act', lambda e: e.activation(O1s[:], pO[0][:], AF.Copy), reads=[('bk', 4)], writes=['O1s'])
            S.op('dve', lambda e: e.tensor_copy(O2s[:], pO[1][:]), reads=[('bk', 5)], writes=['O2s'])
            S.op('dve', lambda e: e.tensor_tensor(Pacc[1][:], Pacc[1][:], PaccB[:], ALU.add),
                 reads=[('Pacc', 1), 'PaccB'], writes=[('Pacc', 1)])
            for m in range(2):
                S.op('act', lambda e, m=m: e.activation(Phi[m][:], Pacc[m][:], AF.Copy),
                     reads=[('Pacc', m)], writes=[('Phi', m)])
                S.op('dve', lambda e, m=m: e.tensor_tensor(Plo[m][:], Pacc[m][:], Phi[m][:], ALU.subtract),
                     reads=[('Pacc', m), ('Phi', m)], writes=[('Plo', m)])
                S.op('pe', lambda e, m=m: e.matmul(pZ[m][:], ones[:], Phi[m][:], start=True, stop=False),
                     reads=['ones', ('Phi', m)], writes=[('bk', 6 + m)])
                S.op('pe', lambda e, m=m: e.matmul(pZ[m][:], ones[:], Plo[m][:], start=False, stop=True),
                     reads=['ones', ('Plo', m)], writes=[('bk', 6 + m)])
            S.op('act', lambda e: e.activation(s1s[:], pZ[0][:], AF.Copy), reads=[('bk', 6)], writes=['s1s'])
            S.op('dve', lambda e: e.tensor_copy(s2s[:], pZ[1][:]), reads=[('bk', 7)], writes=['s2s'])
            S.op('dve', lambda e: e.reciprocal(s1s[:], s1s[:]), reads=['s1s'], writes=['s1s'])
            S.op('dve', lambda e: e.reciprocal(s2s[:], s2s[:]), reads=['s2s'], writes=['s2s'])
            S.op('pool', lambda e: e.tensor_tensor(O1s[:], O1s[:], s1s[:], ALU.mult),
                 reads=['O1s', 's1s'], writes=['O1s'])
            S.op('pool', lambda e: e.tensor_tensor(O2s[:], O2s[:], s2s[:], ALU.mult),
                 reads=['O2s', 's2s'], writes=['O2s'])
            S.op('dve', lambda e: e.scalar_tensor_tensor(osb[:], O2s[:], nlam[:, 0:1], O1s[:], ALU.mult, ALU.add),
                 reads=['O1s', 'O2s', 'nlam'], writes=['osb'])
            S.op('pool', lambda e: e.tensor_tensor(sqb[:], osb[:], osb[:], ALU.mult),
                 reads=['osb'], writes=['sqb'])

        def EPI_B(h, qi):
            qb = (h * nt + qi) % 2
            q0 = qi * QT
            S.op('pe', lambda e: e.matmul(pSS[:], ones[:], sqb[:], start=True, stop=True),
                 reads=['ones', 'sqb'], writes=[('bk', 3)])
            S.op('act', lambda e: e.activation(rsb[:], pSS[:], AF.Ln, bias=epsg[:], scale=1.0 / 128.0),
                 reads=[('bk', 3), 'epsg'], writes=['rsb'])
            S.op('act', lambda e: e.activation(rsb[:], rsb[:], AF.Exp, scale=-0.5),
                 reads=['rsb'], writes=['rsb'])
            S.op('dve', lambda e: e.tensor_tensor(osb[:], osb[:], rsb[:], ALU.mult),
                 reads=['osb', 'rsb'], writes=['osb'])
            S.op('dve', lambda e: e.scalar_tensor_tensor(
                gT[:, h, q0:q0 + QT], osb[:], gsc[:, 0:1], sgt[qb][:], ALU.mult, ALU.mult),
                reads=['osb', 'gsc', ('sgt', qb)], writes=[('gT', h, qi)])

        pend = []
        for t in range(nit + 1):
            if t < nit:
                h, qi, j, r, lo, qb = geom(t)
                if j == 0:
                    if qi == 0:
                        S.dma('sp', kT[:], kT_d[h], writes=['kT'], dkey='kT')
                    q0 = qi * QT
                    S.dma('sp', qt[qb][:], qT_d[h, :, q0:q0 + QT], writes=[('qt', qb)], dkey=('qt', qb))
                    S.dma('sp', sgt[qb][:], sgT_d[h, :, q0:q0 + QT], writes=[('sgt', qb)], dkey=('sgt', qb))
                QK(t)
                if j == 0 and qi == 0:
                    S.dma('sp', bt[:], bias_d[h], writes=['bt'], dkey='bt')
                EXP(t)
            if t >= 1:
                h, qi, j, r, lo, qb = geom(t - 1)
                if j == 0 and qi == 0:
                    S.dma('sp', vS[:], v_d[:, h * 128:(h + 1) * 128].rearrange("(c p) d -> p c d", p=128),
                          writes=['vS'], dkey='vS')
                PV(t - 1)
                if j == 4 * qi + 3:
                    EPI_A(h, qi)
                    pend.append((t + 2, h, qi))
            while pend and pend[0][0] <= t:
                _, hh_, qq_ = pend.pop(0)
                EPI_B(hh_, qq_)
        for _, hh_, qq_ in pend:
            EPI_B(hh_, qq_)

        S.barrier()
        if dbg:
            S.dma('sp', dbg_d['gT'], big[:, 0:4 * seq], dkey=('dbg', 0))
            S.dma('sp', dbg_d['nlam'], nlam[:], dkey=('dbg', 1))
            S.dma('sp', dbg_d['qTd'], qT_d, dkey=('dbg', 2))
            S.dma('sp', dbg_d['sgTd'], sgT_d, dkey=('dbg', 3))
            S.dma('sp', dbg_d['kTd'], kT_d, dkey=('dbg', 4))
        for i in range(nch):
            t0 = i * CH
            b2 = i % 2
            for half in range(2):
                bkx = b2 * 2 + half
                pt = banks[bkx]
                for j in range(4):
                    S.op('pe', lambda e, pt=pt, j=j, half=half, t0=t0: e.matmul(
                        pt[:], gT[:, j, t0:t0 + CH], wout[:, j, half * 512:(half + 1) * 512],
                        start=(j == 0), stop=(j == 3)), reads=[], writes=[('bk', bkx)])
                if half == 0:
                    S.op('act', lambda e, pt=pt, b2=b2: e.activation(ost[b2][:, 0:512], pt[:], AF.Copy),
                         reads=[('bk', bkx)], writes=[('ost', b2, 0)])
                else:
                    S.op('dve', lambda e, pt=pt, b2=b2: e.tensor_copy(ost[b2][:, 512:1024], pt[:]),
                         reads=[('bk', bkx)], writes=[('ost', b2, 1)])
            S.dma('sp', part_d[t0:t0 + CH, :], ost[b2][:], reads=[('ost', b2, 0), ('ost', b2, 1)],
                  dkey=('ost', b2))
        S.emit()
    return nc


def t5_bucket_np(n):
    n = np.maximum(n, 0)
    nf = np.maximum(n, 16).astype(np.float32)
    large = 16 + (np.log(nf / np.float32(16)) / np.float32(math.log(128 / 16)) * np.float32(16)).astype(np.int32)
    large = np.minimum(large, 31)
    return np.where(n < 16, n, large)


def l1_inputs(x1, pre_norm_g, diff_w_in, diff_w_out, lq1, lk1, lq2, lk2, subln_g, rel_bias, seq=SEQ):
    maps = []
    ident = np.eye(128, dtype=np.float32)
    gpre = np.ascontiguousarray(pre_norm_g[1].reshape(8, 128).T)
    u = np.arange(1024)[None, :]
    p = np.arange(128)[:, None]
    n = u - 384 - p
    bidx = t5_bucket_np(n)
    lamv = np.ascontiguousarray(np.broadcast_to(
        np.stack([lq1[0], lk1[0], lq2[0], lk2[0]])[None], (128, 4, DH))).astype(np.float32)
    gsub = np.ascontiguousarray(subln_g[0].reshape(128, 1)).astype(np.float32)
    for core in range(NCORES):
        b, r = core // 4, core % 4
        w = diff_w_in[0]
        base = np.arange(r * 512, (r + 1) * 512)
        cols = np.concatenate([base, 2048 + base, 6144 + base, 4096 + base])
        bias = np.empty((4, 128, 1024), np.float32)
        cbv = np.empty((128, 4), np.float32)
        for h in range(4):
            hh = 4 * r + h
            tb = rel_bias[:, hh][bidx]
            bias[h] = np.where(n >= 0, tb, np.float32(NEG))
            cbv[:, h] = rel_bias[31, hh]
        maps.append({
            "x": np.ascontiguousarray(x1[b][:seq]),
            "gpre": gpre,
            "win": np.ascontiguousarray(w[:, cols]),
            "wout": np.ascontiguousarray(diff_w_out[0][r * 512:(r + 1) * 512, :]),
            "bias": bias,
            "cb": cbv,
            "lamv": lamv,
            "gsub": gsub,
            "ident": ident,
        })
    return maps


TOK = SEQ // 4


def build_red(ntok=TOK, nc=None, io=None, phase=None, npart=4):
    nchk = ntok // CH
    if nc is None:
        nc = bass.Bass("TRN2", target_bir_lowering=False)
    if io is not None and 'parts' in io:
        parts_l = io['parts']
    else:
        parts_d = nc.dram_tensor("parts", [4, ntok, D_MODEL], F32, kind="ExternalInput").ap()
        parts_l = [parts_d[k] for k in range(4)]
    xres_d = _dt(nc, io, "xres", [ntok, D_MODEL], F32, "ExternalInput")
    gpost_d = _dt(nc, io, "gpost", [128, D_MODEL], F32, "ExternalInput")
    out_d = _dt(nc, io, "out", [ntok, D_MODEL], F32, "ExternalOutput")
    with ExitStack() as st:
        pfx = f"p{phase[1]}_" if phase is not None else ""

        def sb(name, shape, dt):
            return st.enter_context(nc.sbuf_tensor(pfx + name, shape, dt))
        gpost = sb("gpost_s", [128, D_MODEL], F32)
        nhalf = sb("nhalf", [128, 1], F32)
        pt = [[sb(f"pt{k}_{i}", [128, D_MODEL], F32) for i in range(2)] for k in range(npart)]
        xr = [sb(f"xr{i}", [128, D_MODEL], F32) for i in range(2)]
        junk = sb("junk", [128, D_MODEL], BF16)
        ssum = [sb(f"ss{i}", [128, 1], F32) for i in range(2)]
        rstd = [sb(f"rstd{i}", [128, 1], F32) for i in range(2)]
        yo = [sb(f"yo{i}", [128, D_MODEL], F32) for i in range(2)]
        S = Sched(nc, phase)
        S.dma('sp', gpost[:], gpost_d, writes=['gpost'], dkey=('c0', 10))
        S.op('dve', lambda e: e.memset(nhalf[:], -0.5), writes=['nhalf'])
        for i in range(nchk):
            b2 = i % 2
            t0 = i * CH
            for k in range(npart):
                S.dma('sp', pt[k][b2][:], parts_l[k][t0:t0 + CH, :], writes=[('pt', k, b2)], dkey=('pt', k, b2))
            S.dma('sp', xr[b2][:], xres_d[t0:t0 + CH, :], writes=[('xr', b2)], dkey=('xr', b2))
            if npart == 4:
                S.op('dve', lambda e, b2=b2: e.tensor_tensor(pt[0][b2][:], pt[0][b2][:], pt[1][b2][:], ALU.add),
                     reads=[('pt', 0, b2), ('pt', 1, b2)], writes=[('pt', 0, b2)])
                S.op('pool', lambda e, b2=b2: e.tensor_tensor(pt[2][b2][:], pt[2][b2][:], pt[3][b2][:], ALU.add),
                     reads=[('pt', 2, b2), ('pt', 3, b2)], writes=[('pt', 2, b2)])
                S.op('dve', lambda e, b2=b2: e.tensor_tensor(pt[0][b2][:], pt[0][b2][:], pt[2][b2][:], ALU.add),
                     reads=[('pt', 0, b2), ('pt', 2, b2)], writes=[('pt', 0, b2)])
            S.op('act', lambda e, b2=b2: e.activation(junk[:], pt[0][b2][:], AF.Square, scale=1.0 / 32.0,
                                                      accum_out=ssum[b2][:]),
                 reads=[('pt', 0, b2)], writes=['junk', ('ss', b2)])
            S.op('dve', lambda e, b2=b2: e.tensor_scalar(ssum[b2][:], ssum[b2][:], RMS_EPS, None, ALU.add),
                 reads=[('ss', b2)], writes=[('ss', b2)])
            S.op('pool', lambda e, b2=b2: e.tensor_tensor(rstd[b2][:], ssum[b2][:], nhalf[:], ALU.pow),
                 reads=[('ss', b2), 'nhalf'], writes=[('rstd', b2)])
            S.op('act', lambda e, b2=b2: e.activation(yo[b2][:], pt[0][b2][:], AF.Copy, scale=rstd[b2][:]),
                 reads=[('pt', 0, b2), ('rstd', b2)], writes=[('yo', b2)])
            S.op('dve', lambda e, b2=b2: e.tensor_tensor(yo[b2][:], yo[b2][:], gpost[:], ALU.mult),
                 reads=[('yo', b2), 'gpost'], writes=[('yo', b2)])
            S.op('pool', lambda e, b2=b2: e.tensor_tensor(yo[b2][:], yo[b2][:], xr[b2][:], ALU.add),
                 reads=[('yo', b2), ('xr', b2)], writes=[('yo', b2)])
            S.dma('sp', out_d[t0:t0 + CH, :], yo[b2][:], reads=[('yo', b2)], dkey=('yo', b2))
        S.emit()
    return nc


def red_inputs(parts, xfull, g):
    maps = []
    gp = np.ascontiguousarray(np.broadcast_to(g[None, :], (128, D_MODEL))).astype(np.float32)
    for core in range(NCORES):
        b, r = core // 4, core % 4
        sl = slice(r * TOK, (r + 1) * TOK)
        maps.append({
            "parts": np.stack([parts[b * 4 + hh][sl] for hh in range(4)]),
            "xres": np.ascontiguousarray(xfull[b][sl]),
            "gpost": gp,
        })
    return maps


_CACHE = {}


def _get(name, fn):
    if name not in _CACHE:
        _CACHE[name] = fn()
    return _CACHE[name]


def emit_cc(nc, kind, op, in_ap, out_ap, phase):
    S = Sched(nc, phase)
    S.op('pool', lambda e: e.collective_compute(kind, op, replica_groups=[[0, 1, 2, 3], [4, 5, 6, 7]],
                                                ins=[in_ap.opt()], outs=[out_ap.opt()]),
         dkey=('cc',), inc=1)
    S.emit()


def build_fused(seq=SEQ, upto=6):
    tok = seq // 4
    nc = bass.Bass("TRN2", target_bir_lowering=False)
    E = lambda name, shape, dt=F32: nc.dram_tensor(name, shape, dt, kind="ExternalInput").ap()
    I = lambda name, shape, dt=F32: nc.dram_tensor(name, shape, dt).ap()
    with nc.semaphore("phase") as psem:
        ident = E("ident", [128, 128])
        part0 = I("part0", [seq, D_MODEL])
        io0 = dict(x=E("x", [seq, D_MODEL]), gpre=E("l0_gpre", [128, 8]), win=E("l0_win", [D_MODEL, RET_COLS]),
                   wout=E("l0_wout", [RET_DV, D_MODEL]), tab=E("tab", [seq, 1024]), mask=E("mask", [128, 128]),
                   ident=ident, gc=E("gc", [128, 1]), part=part0)
        build_l0(seq=seq, nc=nc, io=io0, phase=(psem, 0, False))
        red0 = I("red0", [tok, D_MODEL])
        emit_cc(nc, "ReduceScatter", ALU.add, part0, red0, (psem, 1, False))
        if upto == 2:
            out = nc.dram_tensor("out", [tok, D_MODEL], F32, kind="ExternalOutput").ap()
            build_red(ntok=tok, nc=nc, io=dict(parts=[red0], xres=E("xres", [tok, D_MODEL]),
                                               gpost=E("gpost0", [128, D_MODEL]), out=out),
                      phase=(psem, 2, True), npart=1)
            return nc
        x1s = I("x1s", [tok, D_MODEL])
        build_red(ntok=tok, nc=nc, io=dict(parts=[red0], xres=E("xres", [tok, D_MODEL]), gpost=E("gpost0", [128, D_MODEL]),
                                 out=x1s), phase=(psem, 2, False), npart=1)
        R = min(tok, 256)
        NS = tok // R
        xb = [I(f"x1f{i}", [4 * R, D_MODEL]) for i in range(NS)]
        ph = 3
        for i in range(NS):
            emit_cc(nc, "AllGather", ALU.bypass, x1s[i * R:(i + 1) * R, :], xb[i], (psem, ph, False))
            ph += 1

        def xsrc(t0):
            r_, rem = t0 // tok, t0 % tok
            i_, j_ = rem // R, rem % R
            return xb[i_][r_ * R + j_:r_ * R + j_ + CH, :]

        part1 = I("part1", [seq, D_MODEL])
        io1 = dict(xsrc=xsrc, gpre=E("l1_gpre", [128, 8]), win=E("l1_win", [D_MODEL, 2048]),
                   wout=E("l1_wout", [512, D_MODEL]), bias=E("bias", [4, 128, 1024]), cb=E("cb", [128, 4]),
                   lamv=E("lamv", [128, 4, DH]), gsub=E("gsub", [128, 1]), ident=ident, part=part1)
        build_l1(seq=seq, nc=nc, io=io1, phase=(psem, ph, False))
        red1 = I("red1", [tok, D_MODEL])
        emit_cc(nc, "ReduceScatter", ALU.add, part1, red1, (psem, ph + 1, False))
        out = nc.dram_tensor("out", [tok, D_MODEL], F32, kind="ExternalOutput").ap()
        build_red(ntok=tok, nc=nc, io=dict(parts=[red1], xres=x1s, gpost=E("gpost1", [128, D_MODEL]), out=out),
                  phase=(psem, ph + 2, True), npart=1)
    return nc


def fused_inputs(inp, seq=SEQ):
    tok = seq // 4
    f = lambda a: np.asarray(a, dtype=np.float32)
    x = f(inp['x'])
    m0 = l0_inputs(x, f(inp['pre_norm_g']), f(inp['ret_w_in']), f(inp['ret_w_out']), seq)
    m1 = l1_inputs(x, f(inp['pre_norm_g']), f(inp['diff_w_in']), f(inp['diff_w_out']), f(inp['diff_lambda_q1']),
                   f(inp['diff_lambda_k1']), f(inp['diff_lambda_q2']), f(inp['diff_lambda_k2']),
                   f(inp['diff_subln_g']), f(inp['rel_bias']), seq)
    pg = f(inp['post_norm_g'])
    gp = [np.ascontiguousarray(np.broadcast_to(pg[i][None, :], (128, D_MODEL))) for i in range(2)]
    maps = []
    for core in range(NCORES):
        b, r = core // 4, core % 4
        a, c = m0[core], m1[core]
        maps.append({
            "x": a["x"], "l0_gpre": a["gpre"], "l0_win": a["win"], "l0_wout": a["wout"], "tab": a["tab"],
            "mask": a["mask"], "ident": a["ident"], "gc": a["gc"],
            "xres": np.ascontiguousarray(x[b][r * tok:(r + 1) * tok]), "gpost0": gp[0], "gpost1": gp[1],
            "l1_gpre": c["gpre"], "l1_win": c["win"], "l1_wout": c["wout"], "bias": c["bias"], "cb": c["cb"],
            "lamv": c["lamv"], "gsub": c["gsub"],
        })
    return maps


def kernel(**inputs):
    nc = _get('fused', build_fused)
    res = run_bass_kernel_spmd(nc, fused_inputs(inputs), core_ids=list(range(NCORES)))
    out = np.stack([np.concatenate([res.results[b * 4 + r]["out"] for r in range(4)], axis=0)
                    for b in range(BATCH)])
    return out.astype(np.float32)


def kernel_unfused(x, pre_norm_g, post_norm_g, ret_w_in, ret_w_out, diff_w_in, diff_w_out,
                   diff_lambda_q1, diff_lambda_k1, diff_lambda_q2, diff_lambda_k2, diff_subln_g, rel_bias):
    f = lambda a: np.asarray(a, dtype=np.float32)
    x = f(x)
    cores = list(range(NCORES))
    res = run_bass_kernel_spmd(_get('l0', build_l0), l0_inputs(x, f(pre_norm_g), f(ret_w_in), f(ret_w_out)),
                               core_ids=cores)
    parts = [r["part"] for r in res.results]
    res = run_bass_kernel_spmd(_get('red', build_red), red_inputs(parts, x, f(post_norm_g)[0]), core_ids=cores)
    x1 = np.stack([np.concatenate([res.results[b * 4 + r]["out"] for r in range(4)], axis=0) for b in range(BATCH)])
    res = run_bass_kernel_spmd(_get('l1', build_l1), l1_inputs(
        x1, f(pre_norm_g), f(diff_w_in), f(diff_w_out), f(diff_lambda_q1), f(diff_lambda_k1),
        f(diff_lambda_q2), f(diff_lambda_k2), f(diff_subln_g), f(rel_bias)), core_ids=cores)
    parts = [r["part"] for r in res.results]
    res = run_bass_kernel_spmd(_get('red', build_red), red_inputs(parts, x1, f(post_norm_g)[1]), core_ids=cores)
    out = np.stack([np.concatenate([res.results[b * 4 + r]["out"] for r in range(4)], axis=0) for b in range(BATCH)])
    return out.astype(np.float32)
```

```python
import math
from contextlib import ExitStack

import numpy as np
import concourse.bass as bass
import concourse.mybir as mybir
from concourse.bass_utils import run_bass_kernel_spmd

F32 = mybir.dt.float32
BF16 = mybir.dt.bfloat16
AF = mybir.ActivationFunctionType
ALU = mybir.AluOpType
AX = mybir.AxisListType

D_MODEL = 1024
BATCH = 2
SEQ = 8192
D_INNER = 2048
RMS_EPS = 1e-6
GN_EPS = 1e-5
NCORES = 8
NEG = -30000.0

SEM_LIMIT = 30000


class Sched:
    def __init__(self, nc, phase=None):
        self.nc = nc
        self.phase = phase
        self.ops = []
        self.last_w = {}
        self.readers = {}
        self.base = set()

    def op(self, eng, fn, reads=(), writes=(), dkey=None, inc=16):
        idx = len(self.ops)
        deps = set(self.base)
        for b in reads:
            if b in self.last_w:
                deps.add(self.last_w[b])
        for b in writes:
            if b in self.last_w:
                deps.add(self.last_w[b])
            for r in self.readers.get(b, ()):
                deps.add(r)
        for b in reads:
            self.readers.setdefault(b, []).append(idx)
        for b in writes:
            self.last_w[b] = idx
            self.readers[b] = []
        deps.discard(idx)
        self.ops.append(dict(eng=eng, fn=fn, deps=sorted(deps), dkey=dkey, inc=inc))
        return idx

    def dma(self, eng, out, in_, reads=(), writes=(), dkey=None):
        assert dkey is not None
        return self.op(eng, lambda e, o=out, i=in_: e.dma_start(out=o, in_=i), reads, writes, dkey)

    def barrier(self):
        last = {}
        for i, o in enumerate(self.ops):
            if o['dkey'] is not None:
                last[('d', o['dkey'])] = i
            else:
                last[('e', o['eng'])] = i
        self.base = set(last.values())
        self.last_w = {}
        self.readers = {}

    def emit(self):
        nc = self.nc
        ops = self.ops
        n = len(ops)
        need = [False] * n
        for o in ops:
            for d in o['deps']:
                po = ops[d]
                if po['dkey'] is None and o['dkey'] is None and po['eng'] == 'pe' and o['eng'] == 'pe':
                    continue
                need[d] = True
        for i, o in enumerate(ops):
            if o['dkey'] is not None:
                need[i] = True
        last_op = {}
        for i, o in enumerate(ops):
            last_op[o['eng']] = i
        for e_, i in last_op.items():
            need[i] = True
        eng_cnt = {}
        eng_gen = {}
        sem_names = []
        sig = [None] * n
        dcount = {}
        dgen = {}
        for i, o in enumerate(ops):
            if not need[i]:
                continue
            if o['dkey'] is not None:
                k = o['dkey']
                c = dcount.get(k, 0) + o['inc']
                g = dgen.get(k, 0)
                if c > SEM_LIMIT:
                    g += 1
                    c = o['inc']
                dcount[k] = c
                dgen[k] = g
                name = ('d', k, g)
                sig[i] = (name, c, o['inc'])
            else:
                e = o['eng']
                c = eng_cnt.get(e, 0) + 1
                g = eng_gen.get(e, 0)
                if c > SEM_LIMIT:
                    g += 1
                    c = 1
                eng_cnt[e] = c
                eng_gen[e] = g
                name = ('e', e, g)
                sig[i] = (name, c, 1)
            if name not in sem_names:
                sem_names.append(name)
        final = {}
        for i in range(n):
            if sig[i] is not None and ops[i]['dkey'] is not None:
                final[sig[i][0]] = max(final.get(sig[i][0], 0), sig[i][1])
        with ExitStack() as st:
            sems = {}
            for j, name in enumerate(sem_names):
                spfx = f"p{self.phase[1]}" if self.phase is not None else ""
                sems[name] = st.enter_context(nc.semaphore(f"{spfx}s{j}"))
            block = st.enter_context(nc.Block())
            per = {e: [] for e in ('pe', 'act', 'dve', 'pool', 'sp')}
            for i, o in enumerate(ops):
                per[o['eng']].append(i)

            def run(engname, eng):
                waited = {}
                if self.phase is not None and self.phase[1] > 0:
                    eng.wait_ge(self.phase[0], self.phase[1])
                for i in per[engname]:
                    o = ops[i]
                    wl = {}
                    for d in o['deps']:
                        po = ops[d]
                        if po['dkey'] is None and o['dkey'] is None and po['eng'] == 'pe' and engname == 'pe':
                            continue
                        nm, c, _ = sig[d]
                        if waited.get(nm, 0) >= c:
                            continue
                        wl[nm] = max(wl.get(nm, 0), c)
                    for nm, c in wl.items():
                        eng.wait_ge(sems[nm], c)
                        waited[nm] = c
                    ins = o['fn'](eng)
                    if sig[i] is not None:
                        ins.then_inc(sems[sig[i][0]], sig[i][2])
                if engname == 'pool':
                    fin = dict(final)
                    for e_, i in last_op.items():
                        nm, c, _ = sig[i]
                        fin[nm] = max(fin.get(nm, 0), c)
                    for nm, c in fin.items():
                        if waited.get(nm, 0) < c:
                            eng.wait_ge(sems[nm], c)
                    lastc = None
                    for nm in sem_names:
                        lastc = eng.sem_clear(sems[nm])
                    if self.phase is not None:
                        if self.phase[2]:
                            eng.sem_clear(self.phase[0])
                        else:
                            lastc.then_inc(self.phase[0], 1)

            @block.tensor
            def _(e):
                run('pe', e)

            @block.scalar
            def _(e):
                run('act', e)

            @block.vector
            def _(e):
                run('dve', e)

            @block.gpsimd
            def _(e):
                run('pool', e)

            @block.sync
            def _(e):
                run('sp', e)


def _dt(nc, io, name, shape, dt, kind):
    if io is not None and name in io:
        return io[name]
    if kind == "Internal":
        return nc.dram_tensor(name, shape, dt).ap()
    return nc.dram_tensor(name, shape, dt, kind=kind).ap()


RET_DK = 256
RET_DV = 512
RET_COLS = 2 * RET_DK + 2 * RET_DV
CH = 128
NCH = SEQ // CH


def build_l0(seq=SEQ, dbg=False, nc=None, io=None, phase=None):
    nchunks = seq // CH
    if nc is None:
        nc = bass.Bass("TRN2", target_bir_lowering=False)
    x_d = _dt(nc, io, "x", [seq, D_MODEL], F32, "ExternalInput")
    gpre_d = _dt(nc, io, "gpre", [128, 8], F32, "ExternalInput")
    win_d = _dt(nc, io, "win", [D_MODEL, RET_COLS], F32, "ExternalInput")
    wout_d = _dt(nc, io, "wout", [RET_DV, D_MODEL], F32, "ExternalInput")
    tab_d = _dt(nc, io, "tab", [seq, 1024], F32, "ExternalInput")
    mask_d = _dt(nc, io, "mask", [128, 128], F32, "ExternalInput")
    ident_d = _dt(nc, io, "ident", [128, 128], F32, "ExternalInput")
    gc_d = _dt(nc, io, "gc", [128, 1], F32, "ExternalInput")
    part_d = _dt(nc, io, "part", [seq, D_MODEL], F32, "ExternalOutput")
    if dbg:
        dbg_d = {nm: nc.dram_tensor("dbg_" + nm, shp, dt, kind="ExternalOutput").ap() for nm, shp, dt in (
            ("xn", [128, 1024], BF16), ("hT", [128, 1024], BF16), ("qk", [128, 512], BF16), ("vb", [128, 512], BF16),
            ("sg", [128, 512], F32), ("sT", [128, 128], BF16), ("yn", [128, 512], F32), ("gated", [128, 512], BF16),
            ("qkT", [128, 512], BF16), ("win", [128, 8 * RET_COLS], BF16), ("rstd", [128, 1], F32))}

    with ExitStack() as st:
        pfx = f"p{phase[1]}_" if phase is not None else ""

        def sb(name, shape, dt):
            return st.enter_context(nc.sbuf_tensor(pfx + name, shape, dt))

        def ps(name, shape, dt):
            return st.enter_context(nc.psum_tensor(pfx + name, shape, dt))

        win = sb("win_bf", [128, 8, RET_COLS], BF16)
        wout = sb("wout_bf", [128, 4, D_MODEL], BF16)
        wst = [sb(f"wst{i}", [128, 1024], F32) for i in range(2)]
        gpre = sb("gpre_s", [128, 8], F32)
        mask = sb("mask_s", [128, 128], F32)
        ident_f = sb("ident_f", [128, 128], F32)
        ident = sb("ident_b", [128, 128], BF16)
        gc = sb("gc_s", [128, 1], F32)
        nhalf = sb("nhalf", [128, 1], F32)
        NX = 3
        xt = [sb(f"xt{i}", [128, D_MODEL], F32) for i in range(NX)]
        tabt = [sb(f"tab{i}", [128, 1024], F32) for i in range(3)]
        sq_junk = sb("sqj", [128, D_MODEL], BF16)
        ssum = [sb(f"ss{i}", [128, 1], F32) for i in range(2)]
        rstd = [sb(f"rstd{i}", [128, 1], F32) for i in range(2)]
        xn = [sb(f"xn{i}", [128, D_MODEL], BF16) for i in range(2)]
        hT = [sb(f"hT{i}", [128, 8, 128], BF16) for i in range(2)]
        qkr = [sb(f"qkr{i}", [128, 512], F32) for i in range(2)]
        e1 = [sb(f"e1_{i}", [128, 512], F32) for i in range(2)]
        e2 = [sb(f"e2_{i}", [128, 512], F32) for i in range(2)]
        qk = [sb(f"qk{i}", [128, 512], BF16) for i in range(2)]
        qkT = [sb(f"qkT{i}", [128, 4, 128], BF16) for i in range(2)]
        vb = [sb(f"vb{i}", [128, 512], BF16) for i in range(2)]
        sg = [sb(f"sg{i}", [128, 512], F32) for i in range(3)]
        sT = [sb(f"sT{i}", [128, 128], BF16) for i in range(2)]
        U = sb("U", [128, 2, 512], F32)
        Rb = [sb(f"Rb{i}", [128, 2, 512], BF16) for i in range(2)]
        stats = [sb(f"st{i}", [128, 6], F32) for i in range(2)]
        mv = [sb(f"mv{i}", [128, 2], F32) for i in range(2)]
        grs = [sb(f"grs{i}", [128, 1], F32) for i in range(2)]
        gnb = [sb(f"gnb{i}", [128, 1], F32) for i in range(2)]
        yn = [sb(f"yn{i}", [128, 512], F32) for i in range(2)]
        gated = [sb(f"gated{i}", [128, 512], BF16) for i in range(2)]
        gT = sb("gT_all", [128, 4, seq], BF16)
        ost = [sb(f"ost{i}", [128, D_MODEL], F32) for i in range(2)]

        p_xT = ps("p_xT", [128, 8, 128], BF16)
        p_qk = ps("p_qk", [128, 512], F32)
        p_v = ps("p_v", [128, 512], F32)
        p_g = ps("p_g", [128, 512], F32)
        p_m = ps("p_m", [128, 512], F32)
        p_mb = p_m[:].bitcast(BF16)
        p_A = [ps(f"p_A{i}", [128, 512], F32) for i in range(2)]
        p_o = ps("p_o", [128, 512], F32)

        S = Sched(nc, phase)

        S.dma('sp', gpre[:], gpre_d, writes=['gpre'], dkey=('c0', 1))
        S.dma('sp', mask[:], mask_d, writes=['mask'], dkey=('c0', 2))
        S.dma('sp', ident_f[:], ident_d, writes=['identf'], dkey=('c0', 3))
        S.dma('sp', gc[:], gc_d, writes=['gc'], dkey=('c0', 4))
        S.op('dve', lambda e: e.tensor_copy(ident[:], ident_f[:]), reads=['identf'], writes=['ident'])
        S.op('dve', lambda e: e.memset(U[:], 0.0), writes=['U'])
        S.op('dve', lambda e: e.memset(nhalf[:], -0.5), writes=['nhalf'])
        S.op('dve', lambda e: e.memset(Rb[0][:], 0.0), writes=[('Rb', 0)])
        k = 0
        for c in range(8):
            for (c0, c1) in ((0, 1024), (1024, RET_COLS)):
                w = wst[k % 2]
                S.dma('sp', w[:, 0:c1 - c0], win_d[c * 128:(c + 1) * 128, c0:c1],
                      writes=[('wst', k % 2)], dkey=('wst', k % 2))
                S.op('dve', lambda e, w=w, c=c, c0=c0, c1=c1: e.tensor_scalar(
                    win[:, c, c0:c1], w[:, 0:c1 - c0], gpre[:, c:c + 1], None, ALU.mult),
                    reads=[('wst', k % 2), 'gpre'], writes=['win'])
                k += 1
        for j in range(4):
            w = wst[k % 2]
            S.dma('sp', w[:], wout_d[j * 128:(j + 1) * 128, :], writes=[('wst', k % 2)], dkey=('wst', k % 2))
            S.op('dve', lambda e, w=w, j=j: e.tensor_copy(wout[:, j, :], w[:]),
                 reads=[('wst', k % 2)], writes=['wout'])
            k += 1

        def s1a_act(i):
            a = i % NX
            b2 = i % 2
            t0 = i * CH
            S.dma('sp', xt[a][:], x_d[t0:t0 + CH, :], writes=[('xt', a)], dkey=('xt', a))
            S.dma('sp', tabt[i % 3][:], tab_d[t0:t0 + CH, :], writes=[('tab', i % 3)], dkey=('tab', i % 3))
            S.op('act', lambda e: e.activation(sq_junk[:], xt[a][:], AF.Square, scale=1.0 / 32.0,
                                               accum_out=ssum[b2][:]),
                 reads=[('xt', a)], writes=['sqj', ('ss', b2)])
            S.op('dve', lambda e: e.tensor_scalar(ssum[b2][:], ssum[b2][:], RMS_EPS, None, ALU.add),
                 reads=[('ss', b2)], writes=[('ss', b2)])
            S.op('pool', lambda e: e.tensor_tensor(rstd[b2][:], ssum[b2][:], nhalf[:], ALU.pow),
                 reads=[('ss', b2), 'nhalf'], writes=[('rstd', b2)])
            S.op('act', lambda e: e.activation(xn[b2][:], xt[a][:], AF.Copy, scale=rstd[b2][:]),
                 reads=[('xt', a), ('rstd', b2)], writes=[('xn', b2)])

        def s1a_pe(i):
            b2 = i % 2
            for c in range(8):
                S.op('pe', lambda e, c=c: e.transpose(p_xT[:, c, :], xn[b2][:, c * 128:(c + 1) * 128], ident[:]),
                     reads=[('xn', b2), 'ident'], writes=['p_xT'])
            S.op('dve', lambda e: e.tensor_copy(hT[b2][:], p_xT[:]), reads=['p_xT'], writes=[('hT', b2)])

        def s1b(i):
            b2 = i % 2
            for (pt, nm, c0) in ((p_qk, 'p_qk', 0), (p_v, 'p_v', 512), (p_g, 'p_g', 1024)):
                for c in range(8):
                    S.op('pe', lambda e, pt=pt, c=c, c0=c0: e.matmul(
                        pt[:], hT[b2][:, c, :], win[:, c, c0:c0 + 512], start=(c == 0), stop=(c == 7)),
                        reads=[('hT', b2), 'win'], writes=[nm])

        def s2a(i):
            b2 = i % 2
            S.op('act', lambda e: e.activation(qkr[b2][:], p_qk[:], AF.Copy), reads=['p_qk'], writes=[('qkr', b2)])
            S.op('act', lambda e: e.activation(vb[b2][:], p_v[:], AF.Copy), reads=['p_v'], writes=[('vb', b2)])
            S.op('act', lambda e: e.activation(sg[i % 3][:], p_g[:], AF.Silu), reads=['p_g'], writes=[('sg', i % 3)])

        def s2b(i):
            b2 = i % 2
            b3 = i % 3
            tA = tabt[b3][:, 0:512]
            tB = tabt[b3][:, 512:1024]
            S.op('dve', lambda e: e.tensor_tensor(e1[b2][:], qkr[b2][:], tA, ALU.mult),
                 reads=[('qkr', b2), ('tab', b3)], writes=[('e1', b2)])
            pq = qkr[b2][:].rearrange("p (a h d) -> p a h d", a=2, h=2)
            tBv = tB.rearrange("p (a h d) -> p a h d", a=2, h=2)
            e2v = e2[b2][:].rearrange("p (a h d) -> p a h d", a=2, h=2)
            S.op('pool', lambda e: e.tensor_tensor(e2v[:, :, 0, :], pq[:, :, 1, :], tBv[:, :, 0, :], ALU.mult),
                 reads=[('qkr', b2), ('tab', b3)], writes=[('e2', b2, 0)])
            S.op('pool', lambda e: e.tensor_tensor(e2v[:, :, 1, :], pq[:, :, 0, :], tBv[:, :, 1, :], ALU.mult),
                 reads=[('qkr', b2), ('tab', b3)], writes=[('e2', b2, 1)])
            S.op('dve', lambda e: e.tensor_tensor(qk[b2][:], e1[b2][:], e2[b2][:], ALU.add),
                 reads=[('e1', b2), ('e2', b2, 0), ('e2', b2, 1)], writes=[('qk', b2)])
            for c in range(4):
                S.op('pe', lambda e, c=c: e.transpose(p_mb[:, c * 128:(c + 1) * 128],
                                                      qk[b2][:, c * 128:(c + 1) * 128], ident[:]),
                     reads=[('qk', b2), 'ident'], writes=['p_m'])
            S.op('dve', lambda e: e.tensor_copy(qkT[b2][:].rearrange("p a d -> p (a d)"), p_mb[:, 0:512]),
                 reads=['p_m'], writes=[('qkT', b2)])

        def s3(i):
            b2 = i % 2
            rb = i % 2
            for dc in range(2):
                S.op('pe', lambda e, dc=dc: e.matmul(p_m[:, 0:128], qkT[b2][:, 2 + dc, :], qkT[b2][:, dc, :],
                                                     start=(dc == 0), stop=(dc == 1)),
                     reads=[('qkT', b2)], writes=['p_m'])
            S.op('dve', lambda e: e.tensor_tensor(sT[b2][:], p_m[:, 0:128], mask[:], ALU.mult),
                 reads=['p_m', 'mask'], writes=[('sT', b2)])
            for dc in range(2):
                S.op('pe', lambda e, dc=dc: e.matmul(p_A[dc][:], qk[b2][:, 256 + dc * 128:256 + (dc + 1) * 128],
                                                     vb[b2][:], start=True, stop=True),
                     reads=[('qk', b2), ('vb', b2)], writes=[('p_A', dc)])
            S.op('pe', lambda e: e.matmul(p_o[:], sT[b2][:], vb[b2][:], start=True, stop=False),
                 reads=[('sT', b2), ('vb', b2)], writes=['p_o'])
            for dc in range(2):
                S.op('pe', lambda e, dc=dc: e.matmul(p_o[:], qkT[b2][:, dc, :], Rb[rb][:, dc, :],
                                                     start=False, stop=(dc == 1)),
                     reads=[('qkT', b2), ('Rb', rb)], writes=['p_o'])
            for dc in range(2):
                S.op('dve', lambda e, dc=dc: e.scalar_tensor_tensor(
                    U[:, dc, :], U[:, dc, :], gc[:, 0:1], p_A[dc][:], ALU.mult, ALU.add),
                    reads=['U', ('p_A', dc), 'gc'], writes=['U'])
            S.op('act', lambda e: e.activation(Rb[1 - rb][:].rearrange("p a d -> p (a d)"),
                                               U[:].rearrange("p a d -> p (a d)"), AF.Copy, scale=gc[:, 0:1]),
                 reads=['U', 'gc'], writes=[('Rb', 1 - rb)])

        def s4a(i):
            b2 = i % 2
            t0 = i * CH
            S.op('dve', lambda e: e.bn_stats(stats[b2][:], p_o[:]), reads=['p_o'], writes=[('stats', b2)])
            S.op('dve', lambda e: e.bn_aggr(mv[b2][:], stats[b2][:]), reads=[('stats', b2)], writes=[('mv', b2)])
            S.op('dve', lambda e: e.tensor_scalar(mv[b2][:, 1:2], mv[b2][:, 1:2], GN_EPS, None, ALU.add),
                 reads=[('mv', b2)], writes=[('mv', b2)])
            S.op('pool', lambda e: e.tensor_tensor(grs[b2][:], mv[b2][:, 1:2], nhalf[:], ALU.pow),
                 reads=[('mv', b2), 'nhalf'], writes=[('grs', b2)])
            S.op('dve', lambda e: e.scalar_tensor_tensor(gnb[b2][:], mv[b2][:, 0:1], -1.0, grs[b2][:],
                                                         ALU.mult, ALU.mult),
                 reads=[('mv', b2), ('grs', b2)], writes=[('gnb', b2)])
            S.op('act', lambda e: e.activation(yn[b2][:], p_o[:], AF.Identity, bias=gnb[b2][:], scale=grs[b2][:]),
                 reads=['p_o', ('gnb', b2), ('grs', b2)], writes=[('yn', b2)])
            S.op('pool', lambda e: e.tensor_tensor(gated[b2][:], yn[b2][:], sg[i % 3][:], ALU.mult),
                 reads=[('yn', b2), ('sg', i % 3)], writes=[('gated', b2)])

        def s4b(i):
            b2 = i % 2
            t0 = i * CH
            for c in range(4):
                S.op('pe', lambda e, c=c: e.transpose(p_mb[:, 512 + c * 128:512 + (c + 1) * 128],
                                                      gated[b2][:, c * 128:(c + 1) * 128], ident[:]),
                     reads=[('gated', b2), 'ident'], writes=['p_m'])
            S.op('act', lambda e: e.activation(
                gT[:, :, t0:t0 + CH], p_mb[:, 512:1024].rearrange("p (a d) -> p a d", a=4), AF.Copy),
                reads=['p_m'], writes=[('gT', i)])

        s1a_act(0)
        s1a_pe(0)
        if nchunks > 1:
            s1a_act(1)
            s1a_pe(1)
        s1b(0)
        s2a(0)
        for i in range(nchunks):
            if i + 2 < nchunks:
                s1a_act(i + 2)
            if i + 1 < nchunks:
                s1b(i + 1)
                s2a(i + 1)
            if i + 2 < nchunks:
                s1a_pe(i + 2)
            s2b(i)
            if i >= 1:
                s4a(i - 1)
                s4b(i - 1)
            s3(i)
            if i == nchunks - 1:
                s4a(i)
                s4b(i)
            if dbg and i == 0:
                for nm, t, rd in (("xn", xn[0][:], ('xn', 0)), ("hT", hT[0][:].rearrange("p a d -> p (a d)"), ('hT', 0)),
                                  ("qk", qk[0][:], ('qk', 0)), ("vb", vb[0][:], ('vb', 0)), ("sg", sg[0][:], ('sg', 0)),
                                  ("sT", sT[0][:], ('sT', 0)), ("yn", yn[0][:], ('yn', 0)), ("gated", gated[0][:], ('gated', 0)),
                                  ("qkT", qkT[0][:].rearrange("p a d -> p (a d)"), ('qkT', 0)),
                                  ("win", win[:].rearrange("p a d -> p (a d)"), 'win'), ("rstd", rstd[0][:], ('rstd', 0))):
                    S.dma('sp', dbg_d[nm], t, reads=[rd], dkey=('dbg', nm))

        S.barrier()
        pp = [p_qk, p_v, p_g, p_o]
        for i in range(nchunks):
            t0 = i * CH
            b2 = i % 2
            for half in range(2):
                pt = pp[b2 * 2 + half]
                nm = ('pp', b2 * 2 + half)
                for j in range(4):
                    S.op('pe', lambda e, pt=pt, j=j, half=half, t0=t0: e.matmul(
                        pt[:], gT[:, j, t0:t0 + CH], wout[:, j, half * 512:(half + 1) * 512],
                        start=(j == 0), stop=(j == 3)), reads=[], writes=[nm])
                if half == 0:
                    S.op('act', lambda e, pt=pt, b2=b2: e.activation(ost[b2][:, 0:512], pt[:], AF.Copy),
                         reads=[nm], writes=[('ost', b2, 0)])
                else:
                    S.op('dve', lambda e, pt=pt, b2=b2: e.tensor_copy(ost[b2][:, 512:1024], pt[:]),
                         reads=[nm], writes=[('ost', b2, 1)])
            S.dma('sp', part_d[t0:t0 + CH, :], ost[b2][:], reads=[('ost', b2, 0), ('ost', b2, 1)],
                  writes=[], dkey=('ost', b2))
        S.emit()
    return nc


def rope_decay_table(head, seq=SEQ):
    half = RET_DK // 2
    inv = (10000.0 ** (-np.arange(half, dtype=np.float64) / half))
    pos = np.arange(seq, dtype=np.float64)
    ang = pos[:, None] * inv[None, :]
    c, s = np.cos(ang), np.sin(ang)
    lg = math.log1p(-2.0 ** (-5.0 - head))
    cidx = (np.arange(seq) % CH).astype(np.float64)
    fq = np.exp(lg * (cidx + 1.0))[:, None]
    fk = np.exp(-lg * (cidx + 1.0))[:, None] * (RET_DK ** -0.5)
    A = np.concatenate([c * fq, c * fq, c * fk, c * fk], axis=1)
    B = np.concatenate([-s * fq, s * fq, -s * fk, s * fk], axis=1)
    return np.concatenate([A, B], axis=1).astype(np.float32)


def l0_inputs(x, pre_norm_g, ret_w_in, ret_w_out, seq=SEQ):
    maps = []
    maskT = np.triu(np.ones((CH, CH), np.float32))
    ident = np.eye(128, dtype=np.float32)
    gpre = np.ascontiguousarray(pre_norm_g[0].reshape(8, 128).T)
    for core in range(NCORES):
        b, h = core // 4, core % 4
        w = ret_w_in[0]
        cols = np.concatenate([
            np.arange(h * RET_DK, (h + 1) * RET_DK),
            1024 + np.arange(h * RET_DK, (h + 1) * RET_DK),
            2048 + np.arange(h * RET_DV, (h + 1) * RET_DV),
            2048 + D_INNER + np.arange(h * RET_DV, (h + 1) * RET_DV)])
        gC = math.exp(math.log1p(-2.0 ** (-5.0 - h)) * CH)
        maps.append({
            "x": np.ascontiguousarray(x[b][:seq]),
            "gpre": gpre,
            "win": np.ascontiguousarray(w[:, cols]),
            "wout": np.ascontiguousarray(ret_w_out[0][h * RET_DV:(h + 1) * RET_DV, :]),
            "tab": rope_decay_table(h, seq),
            "mask": maskT,
            "ident": ident,
            "gc": np.full((128, 1), gC, np.float32),
        })
    return maps


LAMBDA_INIT = 0.8 - 0.6 * math.exp(-0.3 * 1)
DH = 64
QT = 512


def build_l1(seq=SEQ, dbg=False, nc=None, io=None, phase=None):
    nt = seq // QT
    nch = seq // CH
    if nc is None:
        nc = bass.Bass("TRN2", target_bir_lowering=False)
    if io is not None and 'xsrc' in io:
        xsrc = io['xsrc']
    else:
        x_d = _dt(nc, io, "x", [seq, D_MODEL], F32, "ExternalInput")
        xsrc = lambda t0: x_d[t0:t0 + CH, :]
    gpre_d = _dt(nc, io, "gpre", [128, 8], F32, "ExternalInput")
    win_d = _dt(nc, io, "win", [D_MODEL, 2048], F32, "ExternalInput")
    wout_d = _dt(nc, io, "wout", [512, D_MODEL], F32, "ExternalInput")
    bias_d = _dt(nc, io, "bias", [4, 128, 1024], F32, "ExternalInput")
    cb_d = _dt(nc, io, "cb", [128, 4], F32, "ExternalInput")
    lamv_d = _dt(nc, io, "lamv", [128, 4, DH], F32, "ExternalInput")
    gsub_d = _dt(nc, io, "gsub", [128, 1], F32, "ExternalInput")
    ident_d = _dt(nc, io, "ident", [128, 128], F32, "ExternalInput")
    part_d = _dt(nc, io, "part", [seq, D_MODEL], F32, "ExternalOutput")
    qT_d = _dt(nc, io, "qT_scr", [4, 128, seq], BF16, "Internal")
    kT_d = _dt(nc, io, "kT_scr", [4, 128, seq], BF16, "Internal")
    sgT_d = _dt(nc, io, "sgT_scr", [4, 128, seq], BF16, "Internal")
    v_d = _dt(nc, io, "v_scr", [seq, 512], BF16, "Internal")
    if dbg:
        dbg_d = {nm: nc.dram_tensor("dbg_" + nm, shp, dt, kind="ExternalOutput").ap() for nm, shp, dt in (
            ("gT", [128, 4 * seq], BF16), ("nlam", [128, 1], F32), ("qTd", [4, 128, seq], BF16),
            ("sgTd", [4, 128, seq], BF16), ("kTd", [4, 128, seq], BF16))}

    with ExitStack() as st:
        pfx = f"p{phase[1]}_" if phase is not None else ""

        def sb(name, shape, dt):
            return st.enter_context(nc.sbuf_tensor(pfx + name, shape, dt))

        def ps(name, shape, dt):
            return st.enter_context(nc.psum_tensor(pfx + name, shape, dt))

        big = sb("big", [128, 4 * SEQ], BF16)
        win = big[:, 0:8 * 2048].rearrange("p (a d) -> p a d", a=8)
        gT = big[:, 0:4 * seq].rearrange("p (a d) -> p a d", a=4)
        wout = sb("wout_bf", [128, 4, D_MODEL], BF16)
        wst = [sb(f"wst{i}", [128, 1024], F32) for i in range(2)]
        gpre = sb("gpre_s", [128, 8], F32)
        ident_f = sb("ident_f", [128, 128], F32)
        ident = sb("ident_b", [128, 128], BF16)
        ones = sb("ones_b", [128, 128], BF16)
        nhalf = sb("nhalf", [128, 1], F32)
        epsg = sb("epsg", [128, 1], F32)
        Phi = [sb(f"Phi{m}", [128, QT], BF16) for m in range(2)]
        Plo = [sb(f"Plo{m}", [128, QT], BF16) for m in range(2)]
        Pacc = [sb(f"Pacc{m}", [128, QT], F32) for m in range(2)]
        cb = sb("cb_s", [128, 4], F32)
        lamv = sb("lamv_s", [128, 4, DH], F32)
        lamp = sb("lamp", [128, 2, DH], F32)
        lams = sb("lams", [128, 2], F32)
        nlam = sb("nlam", [128, 1], F32)
        gsc = sb("gsc", [128, 1], F32)
        NX = 3
        xt = [sb(f"xt{i}", [128, D_MODEL], F32) for i in range(NX)]
        sq_junk = sb("sqj", [128, D_MODEL], BF16)
        ssum = [sb(f"ss{i}", [128, 1], F32) for i in range(2)]
        rstd = [sb(f"rstd{i}", [128, 1], F32) for i in range(2)]
        xn = [sb(f"xn{i}", [128, D_MODEL], BF16) for i in range(2)]
        hT = [sb(f"hT{i}", [128, 8, QT], BF16) for i in range(2)]
        NF = 4
        fst = [sb(f"fst{i}", [128, QT], BF16) for i in range(NF)]
        vst = [sb(f"vst{i}", [128, 512], BF16) for i in range(2)]
        kT = sb("kT_s", [128, seq], BF16)
        vS = sb("v_s", [128, nch, 128], BF16)
        bt = sb("bt_s", [128, 1024], F32)
        qt = [sb(f"qt{i}", [128, QT], BF16) for i in range(2)]
        sgt = [sb(f"sgt{i}", [128, QT], BF16) for i in range(2)]
        NP = 3
        PT = [[sb(f"PT{m}_{i}", [128, QT], BF16) for i in range(NP)] for m in range(2)]
        tmp = [[sb(f"tmp{m}_{i}", [128, QT], F32) for i in range(2)] for m in range(2)]
        O1s = sb("O1s", [128, QT], F32)
        O2s = sb("O2s", [128, QT], F32)
        s1s = sb("s1s", [128, QT], F32)
        s2s = sb("s2s", [128, QT], F32)
        osb = sb("osb", [128, QT], F32)
        sqb = sb("sqb", [128, QT], BF16)
        rsb = sb("rsb", [128, QT], F32)
        ost = [sb(f"ost{i}", [128, D_MODEL], F32) for i in range(2)]

        banks = [ps(f"bk{i}", [128, 512], F32) for i in range(8)]

        S = Sched(nc, phase)
        S.dma('sp', gpre[:], gpre_d, writes=['gpre'], dkey=('c0', 5))
        S.dma('sp', ident_f[:], ident_d, writes=['identf'], dkey=('c0', 6))
        S.dma('sp', cb[:], cb_d, writes=['cb'], dkey=('c0', 7))
        S.dma('sp', lamv[:], lamv_d, writes=['lamv'], dkey=('c0', 8))
        S.dma('sp', gsc[:], gsub_d, writes=['gsc'], dkey=('c0', 9))
        S.op('dve', lambda e: e.tensor_copy(ident[:], ident_f[:]), reads=['identf'], writes=['ident'])
        S.op('dve', lambda e: e.memset(ones[:], 1.0), writes=['ones'])
        S.op('dve', lambda e: e.memset(nhalf[:], -0.5), writes=['nhalf'])
        S.op('dve', lambda e: e.memset(epsg[:], GN_EPS), writes=['epsg'])
        S.op('dve', lambda e: e.tensor_tensor(lamp[:, 0, :], lamv[:, 0, :], lamv[:, 1, :], ALU.mult),
             reads=['lamv'], writes=['lamp'])
        S.op('dve', lambda e: e.tensor_tensor(lamp[:, 1, :], lamv[:, 2, :], lamv[:, 3, :], ALU.mult),
             reads=['lamv'], writes=['lamp'])
        S.op('dve', lambda e: e.reduce_sum(lams[:], lamp[:], AX.X), reads=['lamp'], writes=['lams'])
        S.op('act', lambda e: e.activation(lams[:], lams[:], AF.Exp), reads=['lams'], writes=['lams'])
        S.op('dve', lambda e: e.tensor_tensor(nlam[:], lams[:, 1:2], lams[:, 0:1], ALU.subtract),
             reads=['lams'], writes=['nlam'])
        S.op('dve', lambda e: e.tensor_scalar(nlam[:], nlam[:], -LAMBDA_INIT, None, ALU.add),
             reads=['nlam'], writes=['nlam'])
        S.op('dve', lambda e: e.tensor_scalar(gsc[:], gsc[:], 1.0 - LAMBDA_INIT, None, ALU.mult),
             reads=['gsc'], writes=['gsc'])
        k = 0
        for c in range(8):
            for c0 in (0, 1024):
                w = wst[k % 2]
                S.dma('sp', w[:], win_d[c * 128:(c + 1) * 128, c0:c0 + 1024],
                      writes=[('wst', k % 2)], dkey=('wst', k % 2))
                S.op('dve', lambda e, w=w, c=c, c0=c0: e.tensor_scalar(
                    win[:, c, c0:c0 + 1024], w[:], gpre[:, c:c + 1], None, ALU.mult),
                    reads=[('wst', k % 2), 'gpre'], writes=['win'])
                k += 1
        for j in range(4):
            w = wst[k % 2]
            S.dma('sp', w[:], wout_d[j * 128:(j + 1) * 128, :], writes=[('wst', k % 2)], dkey=('wst', k % 2))
            S.op('dve', lambda e, w=w, j=j: e.tensor_copy(wout[:, j, :], w[:]),
                 reads=[('wst', k % 2)], writes=['wout'])
            k += 1

        cnt = {'f': 0, 'v': 0, 'x': 0}

        def p1_norm(ti, sub):
            i = cnt['x']
            cnt['x'] += 1
            a = i % NX
            b2 = i % 2
            tb = ti % 2
            t0 = ti * QT + sub * CH
            pxT = banks[b2][:].bitcast(BF16).rearrange("p (a d) -> p a d", a=8)
            S.dma('sp', xt[a][:], xsrc(t0), writes=[('xt', a)], dkey=('xt', a))
            S.op('act', lambda e: e.activation(sq_junk[:], xt[a][:], AF.Square, scale=1.0 / 32.0,
                                               accum_out=ssum[b2][:]),
                 reads=[('xt', a)], writes=['sqj', ('ss', b2)])
            S.op('dve', lambda e: e.tensor_scalar(ssum[b2][:], ssum[b2][:], RMS_EPS, None, ALU.add),
                 reads=[('ss', b2)], writes=[('ss', b2)])
            S.op('pool', lambda e: e.tensor_tensor(rstd[b2][:], ssum[b2][:], nhalf[:], ALU.pow),
                 reads=[('ss', b2), 'nhalf'], writes=[('rstd', b2)])
            S.op('act', lambda e: e.activation(xn[b2][:], xt[a][:], AF.Copy, scale=rstd[b2][:]),
                 reads=[('xt', a), ('rstd', b2)], writes=[('xn', b2)])
            for c in range(8):
                S.op('pe', lambda e, c=c: e.transpose(pxT[:, c, :], xn[b2][:, c * 128:(c + 1) * 128], ident[:]),
                     reads=[('xn', b2), 'ident'], writes=[('bk', b2)])
            S.op('dve', lambda e: e.tensor_copy(hT[tb][:, :, sub * CH:(sub + 1) * CH], pxT),
                 reads=[('bk', b2)], writes=[('hT', tb, sub)])

        def p1_v(ti, sub):
            tb = ti % 2
            kk = cnt['v']
            cnt['v'] += 1
            bkx = 2 + kk % 2
            pv = banks[bkx]
            t0 = ti * QT + sub * CH
            for c in range(8):
                S.op('pe', lambda e, c=c: e.matmul(pv[:], hT[tb][:, c, sub * CH:(sub + 1) * CH],
                                                   win[:, c, 1536:2048], start=(c == 0), stop=(c == 7)),
                     reads=[('hT', tb, sub), 'win'], writes=[('bk', bkx)])
            vs = vst[kk % 2]
            S.op('dve', lambda e: e.tensor_copy(vs[:], pv[:]), reads=[('bk', bkx)], writes=[('vst', kk % 2)])
            S.dma('sp', v_d[t0:t0 + CH, :], vs[:], reads=[('vst', kk % 2)], dkey=('vst', kk % 2))

        def p1_f(ti, gi):
            tb = ti % 2
            kk = cnt['f']
            cnt['f'] += 1
            bkx = 4 + kk % 4
            pf = banks[bkx]
            t0 = ti * QT
            for c in range(8):
                S.op('pe', lambda e, c=c: e.matmul(pf[:], win[:, c, gi * 128:(gi + 1) * 128], hT[tb][:, c, :],
                                                   start=(c == 0), stop=(c == 7)),
                     reads=[('hT', tb, 0), ('hT', tb, 1), ('hT', tb, 2), ('hT', tb, 3), 'win'],
                     writes=[('bk', bkx)])
            fs = fst[kk % NF]
            h = gi % 4
            if gi < 4:
                S.op('act', lambda e: e.activation(fs[:], pf[:], AF.Copy, scale=DH ** -0.5),
                     reads=[('bk', bkx)], writes=[('fst', kk % NF)])
                dst = qT_d[h, :, t0:t0 + QT]
            elif gi < 8:
                S.op('dve', lambda e: e.tensor_copy(fs[:], pf[:]), reads=[('bk', bkx)], writes=[('fst', kk % NF)])
                dst = kT_d[h, :, t0:t0 + QT]
            else:
                S.op('act', lambda e: e.activation(fs[:], pf[:], AF.Silu),
                     reads=[('bk', bkx)], writes=[('fst', kk % NF)])
                dst = sgT_d[h, :, t0:t0 + QT]
            S.dma('sp', dst, fs[:], reads=[('fst', kk % NF)], dkey=('fst', kk % NF))

        for sub in range(4):
            p1_norm(0, sub)
        for ti in range(nt):
            for sub in range(4):
                p1_v(ti, sub)
            for gi in range(12):
                p1_f(ti, gi)
                if ti + 1 < nt and gi in (1, 4, 7, 10):
                    p1_norm(ti + 1, (gi - 1) // 3)

        S.barrier()
        SB = [banks[0], banks[1], banks[2]]
        pSS = banks[3]
        pO = [banks[4], banks[5]]
        pZ = [banks[6], banks[7]]
        items = [(h, qi, j) for h in range(4) for qi in range(nt) for j in range(4 * qi + 4)]
        nit = len(items)

        def geom(t):
            h, qi, j = items[t]
            r = j - 4 * qi
            lo = 128 * r if r > 0 else 0
            qb = (h * nt + qi) % 2
            return h, qi, j, r, lo, qb

        def QK(t):
            h, qi, j, r, lo, qb = geom(t)
            for m in range(2):
                si = (2 * t + m) % 3
                S.op('pe', lambda e, m=m, j=j, lo=lo, si=si, qb=qb: e.matmul(
                    SB[si][:, lo:QT], kT[m * DH:(m + 1) * DH, j * CH:(j + 1) * CH],
                    qt[qb][m * DH:(m + 1) * DH, lo:QT], start=True, stop=True),
                    reads=['kT', ('qt', qb)], writes=[('bk', si)])

        def EXP(t):
            h, qi, j, r, lo, qb = geom(t)
            pb = t % NP
            tb2 = t % 2
            for m in range(2):
                si = (2 * t + m) % 3
                if r >= -1:
                    off = 384 - 128 * r
                    S.op('dve', lambda e, m=m, lo=lo, si=si, off=off, tb2=tb2: e.tensor_tensor(
                        tmp[m][tb2][:, lo:QT], SB[si][:, lo:QT], bt[:, off + lo:off + QT], ALU.add),
                        reads=[('bk', si), 'bt'], writes=[('tmp', m, tb2)])
                    S.op('act', lambda e, m=m, lo=lo, pb=pb, tb2=tb2: e.activation(
                        PT[m][pb][:, lo:QT], tmp[m][tb2][:, lo:QT], AF.Exp),
                        reads=[('tmp', m, tb2)], writes=[('PT', m, pb)])
                else:
                    S.op('act', lambda e, m=m, pb=pb, si=si, h=h: e.activation(
                        PT[m][pb][:], SB[si][:], AF.Exp, bias=cb[:, h:h + 1]),
                        reads=[('bk', si), 'cb'], writes=[('PT', m, pb)])

        def PV(t):
            h, qi, j, r, lo, qb = geom(t)
            pb = t % NP
            nkc = 4 * qi + 4
            for m in range(2):
                S.op('pe', lambda e, m=m, j=j, lo=lo, pb=pb, nkc=nkc: e.matmul(
                    pO[m][:, lo:QT], vS[:, j, :], PT[m][pb][:, lo:QT], start=(j == 0), stop=(j == nkc - 1)),
                    reads=['vS', ('PT', m, pb)], writes=[('bk', 4 + m)])
            for m in range(2):
                if j == 0:
                    S.op('dve', lambda e, m=m, pb=pb: e.tensor_copy(Pacc[m][:], PT[m][pb][:]),
                         reads=[('PT', m, pb)], writes=[('Pacc', m)])
                else:
                    S.op('dve', lambda e, m=m, lo=lo, pb=pb: e.tensor_tensor(
                        Pacc[m][:, lo:QT], Pacc[m][:, lo:QT], PT[m][pb][:, lo:QT], ALU.add),
                        reads=[('PT', m, pb), ('Pacc', m)], writes=[('Pacc', m)])

        def EPI_A(h, qi):
            S.op('act', lambda e: e.activation(O1s[:], pO[0][:], AF.Copy), reads=[('bk', 4)], writes=['O1s'])
            S.op('dve', lambda e: e.tensor_copy(O2s[:], pO[1][:]), reads=[('bk', 5)], writes=['O2s'])
            for m in range(2):
                S.op('act', lambda e, m=m: e.activation(Phi[m][:], Pacc[m][:], AF.Copy),
                     reads=[('Pacc', m)], writes=[('Phi', m)])
                S.op('dve', lambda e, m=m: e.tensor_tensor(Plo[m][:], Pacc[m][:], Phi[m][:], ALU.subtract),
                     reads=[('Pacc', m), ('Phi', m)], writes=[('Plo', m)])
                S.op('pe', lambda e, m=m: e.matmul(pZ[m][:], ones[:], Phi[m][:], start=True, stop=False),
                     reads=['ones', ('Phi', m)], writes=[('bk', 6 + m)])
                S.op('pe', lambda e, m=m: e.matmul(pZ[m][:], ones[:], Plo[m][:], start=False, stop=True),
                     reads=['ones', ('Plo', m)], writes=[('bk', 6 + m)])
            S.op('act', lambda e: e.activation(s1s[:], pZ[0][:], AF.Copy), reads=[('bk', 6)], writes=['s1s'])
            S.op('dve', lambda e: e.tensor_copy(s2s[:], pZ[1][:]), reads=[('bk', 7)], writes=['s2s'])
            S.op('dve', lambda e: e.reciprocal(s1s[:], s1s[:]), reads=['s1s'], writes=['s1s'])
            S.op('dve', lambda e: e.reciprocal(s2s[:], s2s[:]), reads=['s2s'], writes=['s2s'])
            S.op('pool', lambda e: e.tensor_tensor(O1s[:], O1s[:], s1s[:], ALU.mult),
                 reads=['O1s', 's1s'], writes=['O1s'])
            S.op('pool', lambda e: e.tensor_tensor(O2s[:], O2s[:], s2s[:], ALU.mult),
                 reads=['O2s', 's2s'], writes=['O2s'])
            S.op('dve', lambda e: e.scalar_tensor_tensor(osb[:], O2s[:], nlam[:, 0:1], O1s[:], ALU.mult, ALU.add),
                 reads=['O1s', 'O2s', 'nlam'], writes=['osb'])
            S.op('pool', lambda e: e.tensor_tensor(sqb[:], osb[:], osb[:], ALU.mult),
                 reads=['osb'], writes=['sqb'])

        def EPI_B(h, qi):
            qb = (h * nt + qi) % 2
            q0 = qi * QT
            S.op('pe', lambda e: e.matmul(pSS[:], ones[:], sqb[:], start=True, stop=True),
                 reads=['ones', 'sqb'], writes=[('bk', 3)])
            S.op('act', lambda e: e.activation(rsb[:], pSS[:], AF.Ln, bias=epsg[:], scale=1.0 / 128.0),
                 reads=[('bk', 3), 'epsg'], writes=['rsb'])
            S.op('act', lambda e: e.activation(rsb[:], rsb[:], AF.Exp, scale=-0.5),
                 reads=['rsb'], writes=['rsb'])
            S.op('dve', lambda e: e.tensor_tensor(osb[:], osb[:], rsb[:], ALU.mult),
                 reads=['osb', 'rsb'], writes=['osb'])
            S.op('dve', lambda e: e.scalar_tensor_tensor(
                gT[:, h, q0:q0 + QT], osb[:], gsc[:, 0:1], sgt[qb][:], ALU.mult, ALU.mult),
                reads=['osb', 'gsc', ('sgt', qb)], writes=[('gT', h, qi)])

        pend = []
        for t in range(nit + 1):
            if t < nit:
                h, qi, j, r, lo, qb = geom(t)
                if j == 0:
                    if qi == 0:
                        S.dma('sp', kT[:], kT_d[h], writes=['kT'], dkey='kT')
                    q0 = qi * QT
                    S.dma('sp', qt[qb][:], qT_d[h, :, q0:q0 + QT], writes=[('qt', qb)], dkey=('qt', qb))
                    S.dma('sp', sgt[qb][:], sgT_d[h, :, q0:q0 + QT], writes=[('sgt', qb)], dkey=('sgt', qb))
                QK(t)
                if j == 0 and qi == 0:
                    S.dma('sp', bt[:], bias_d[h], writes=['bt'], dkey='bt')
                EXP(t)
            if t >= 1:
                h, qi, j, r, lo, qb = geom(t - 1)
                if j == 0 and qi == 0:
                    S.dma('sp', vS[:], v_d[:, h * 128:(h + 1) * 128].rearrange("(c p) d -> p c d", p=128),
                          writes=['vS'], dkey='vS')
                PV(t - 1)
                if j == 4 * qi + 3:
                    EPI_A(h, qi)
                    pend.append((t + 2, h, qi))
            while pend and pend[0][0] <= t:
                _, hh_, qq_ = pend.pop(0)
                EPI_B(hh_, qq_)
        for _, hh_, qq_ in pend:
            EPI_B(hh_, qq_)

        S.barrier()
        if dbg:
            S.dma('sp', dbg_d['gT'], big[:, 0:4 * seq], dkey=('dbg', 0))
            S.dma('sp', dbg_d['nlam'], nlam[:], dkey=('dbg', 1))
            S.dma('sp', dbg_d['qTd'], qT_d, dkey=('dbg', 2))
            S.dma('sp', dbg_d['sgTd'], sgT_d, dkey=('dbg', 3))
            S.dma('sp', dbg_d['kTd'], kT_d, dkey=('dbg', 4))
        for i in range(nch):
            t0 = i * CH
            b2 = i % 2
            for half in range(2):
                bkx = b2 * 2 + half
                pt = banks[bkx]
                for j in range(4):
                    S.op('pe', lambda e, pt=pt, j=j, half=half, t0=t0: e.matmul(
                        pt[:], gT[:, j, t0:t0 + CH], wout[:, j, half * 512:(half + 1) * 512],
                        start=(j == 0), stop=(j == 3)), reads=[], writes=[('bk', bkx)])
                if half == 0:
                    S.op('act', lambda e, pt=pt, b2=b2: e.activation(ost[b2][:, 0:512], pt[:], AF.Copy),
                         reads=[('bk', bkx)], writes=[('ost', b2, 0)])
                else:
                    S.op('dve', lambda e, pt=pt, b2=b2: e.tensor_copy(ost[b2][:, 512:1024], pt[:]),
                         reads=[('bk', bkx)], writes=[('ost', b2, 1)])
            S.dma('sp', part_d[t0:t0 + CH, :], ost[b2][:], reads=[('ost', b2, 0), ('ost', b2, 1)],
                  dkey=('ost', b2))
        S.emit()
    return nc


def t5_bucket_np(n):
    n = np.maximum(n, 0)
    nf = np.maximum(n, 16).astype(np.float32)
    large = 16 + (np.log(nf / np.float32(16)) / np.float32(math.log(128 / 16)) * np.float32(16)).astype(np.int32)
    large = np.minimum(large, 31)
    return np.where(n < 16, n, large)


def l1_inputs(x1, pre_norm_g, diff_w_in, diff_w_out, lq1, lk1, lq2, lk2, subln_g, rel_bias, seq=SEQ):
    maps = []
    ident = np.eye(128, dtype=np.float32)
    gpre = np.ascontiguousarray(pre_norm_g[1].reshape(8, 128).T)
    u = np.arange(1024)[None, :]
    p = np.arange(128)[:, None]
    n = u - 384 - p
    bidx = t5_bucket_np(n)
    lamv = np.ascontiguousarray(np.broadcast_to(
        np.stack([lq1[0], lk1[0], lq2[0], lk2[0]])[None], (128, 4, DH))).astype(np.float32)
    gsub = np.ascontiguousarray(subln_g[0].reshape(128, 1)).astype(np.float32)
    for core in range(NCORES):
        b, r = core // 4, core % 4
        w = diff_w_in[0]
        base = np.arange(r * 512, (r + 1) * 512)
        cols = np.concatenate([base, 2048 + base, 6144 + base, 4096 + base])
        bias = np.empty((4, 128, 1024), np.float32)
        cbv = np.empty((128, 4), np.float32)
        for h in range(4):
            hh = 4 * r + h
            tb = rel_bias[:, hh][bidx]
            bias[h] = np.where(n >= 0, tb, np.float32(NEG))
            cbv[:, h] = rel_bias[31, hh]
        maps.append({
            "x": np.ascontiguousarray(x1[b][:seq]),
            "gpre": gpre,
            "win": np.ascontiguousarray(w[:, cols]),
            "wout": np.ascontiguousarray(diff_w_out[0][r * 512:(r + 1) * 512, :]),
            "bias": bias,
            "cb": cbv,
            "lamv": lamv,
            "gsub": gsub,
            "ident": ident,
        })
    return maps


TOK = SEQ // 4


def build_red(ntok=TOK, nc=None, io=None, phase=None, npart=4):
    nchk = ntok // CH
    if nc is None:
        nc = bass.Bass("TRN2", target_bir_lowering=False)
    if io is not None and 'parts' in io:
        parts_l = io['parts']
    else:
        parts_d = nc.dram_tensor("parts", [4, ntok, D_MODEL], F32, kind="ExternalInput").ap()
        parts_l = [parts_d[k] for k in range(4)]
    xres_d = _dt(nc, io, "xres", [ntok, D_MODEL], F32, "ExternalInput")
    gpost_d = _dt(nc, io, "gpost", [128, D_MODEL], F32, "ExternalInput")
    out_d = _dt(nc, io, "out", [ntok, D_MODEL], F32, "ExternalOutput")
    with ExitStack() as st:
        pfx = f"p{phase[1]}_" if phase is not None else ""

        def sb(name, shape, dt):
            return st.enter_context(nc.sbuf_tensor(pfx + name, shape, dt))
        gpost = sb("gpost_s", [128, D_MODEL], F32)
        nhalf = sb("nhalf", [128, 1], F32)
        pt = [[sb(f"pt{k}_{i}", [128, D_MODEL], F32) for i in range(2)] for k in range(npart)]
        xr = [sb(f"xr{i}", [128, D_MODEL], F32) for i in range(2)]
        junk = sb("junk", [128, D_MODEL], BF16)
        ssum = [sb(f"ss{i}", [128, 1], F32) for i in range(2)]
        rstd = [sb(f"rstd{i}", [128, 1], F32) for i in range(2)]
        yo = [sb(f"yo{i}", [128, D_MODEL], F32) for i in range(2)]
        S = Sched(nc, phase)
        S.dma('sp', gpost[:], gpost_d, writes=['gpost'], dkey=('c0', 10))
        S.op('dve', lambda e: e.memset(nhalf[:], -0.5), writes=['nhalf'])
        for i in range(nchk):
            b2 = i % 2
            t0 = i * CH
            for k in range(npart):
                S.dma('sp', pt[k][b2][:], parts_l[k][t0:t0 + CH, :], writes=[('pt', k, b2)], dkey=('pt', k, b2))
            S.dma('sp', xr[b2][:], xres_d[t0:t0 + CH, :], writes=[('xr', b2)], dkey=('xr', b2))
            if npart == 4:
                S.op('dve', lambda e, b2=b2: e.tensor_tensor(pt[0][b2][:], pt[0][b2][:], pt[1][b2][:], ALU.add),
                     reads=[('pt', 0, b2), ('pt', 1, b2)], writes=[('pt', 0, b2)])
                S.op('pool', lambda e, b2=b2: e.tensor_tensor(pt[2][b2][:], pt[2][b2][:], pt[3][b2][:], ALU.add),
                     reads=[('pt', 2, b2), ('pt', 3, b2)], writes=[('pt', 2, b2)])
                S.op('dve', lambda e, b2=b2: e.tensor_tensor(pt[0][b2][:], pt[0][b2][:], pt[2][b2][:], ALU.add),
                     reads=[('pt', 0, b2), ('pt', 2, b2)], writes=[('pt', 0, b2)])
            S.op('act', lambda e, b2=b2: e.activation(junk[:], pt[0][b2][:], AF.Square, scale=1.0 / 32.0,
                                                      accum_out=ssum[b2][:]),
                 reads=[('pt', 0, b2)], writes=['junk', ('ss', b2)])
            S.op('dve', lambda e, b2=b2: e.tensor_scalar(ssum[b2][:], ssum[b2][:], RMS_EPS, None, ALU.add),
                 reads=[('ss', b2)], writes=[('ss', b2)])
            S.op('pool', lambda e, b2=b2: e.tensor_tensor(rstd[b2][:], ssum[b2][:], nhalf[:], ALU.pow),
                 reads=[('ss', b2), 'nhalf'], writes=[('rstd', b2)])
            S.op('act', lambda e, b2=b2: e.activation(yo[b2][:], pt[0][b2][:], AF.Copy, scale=rstd[b2][:]),
                 reads=[('pt', 0, b2), ('rstd', b2)], writes=[('yo', b2)])
            S.op('dve', lambda e, b2=b2: e.tensor_tensor(yo[b2][:], yo[b2][:], gpost[:], ALU.mult),
                 reads=[('yo', b2), 'gpost'], writes=[('yo', b2)])
            S.op('pool', lambda e, b2=b2: e.tensor_tensor(yo[b2][:], yo[b2][:], xr[b2][:], ALU.add),
                 reads=[('yo', b2), ('xr', b2)], writes=[('yo', b2)])
            S.dma('sp', out_d[t0:t0 + CH, :], yo[b2][:], reads=[('yo', b2)], dkey=('yo', b2))
        S.emit()
    return nc


def red_inputs(parts, xfull, g):
    maps = []
    gp = np.ascontiguousarray(np.broadcast_to(g[None, :], (128, D_MODEL))).astype(np.float32)
    for core in range(NCORES):
        b, r = core // 4, core % 4
        sl = slice(r * TOK, (r + 1) * TOK)
        maps.append({
            "parts": np.stack([parts[b * 4 + hh][sl] for hh in range(4)]),
            "xres": np.ascontiguousarray(xfull[b][sl]),
            "gpost": gp,
        })
    return maps


_CACHE = {}


def _get(name, fn):
    if name not in _CACHE:
        _CACHE[name] = fn()
    return _CACHE[name]


def emit_cc(nc, kind, op, in_ap, out_ap, phase):
    S = Sched(nc, phase)
    S.op('pool', lambda e: e.collective_compute(kind, op, replica_groups=[[0, 1, 2, 3], [4, 5, 6, 7]],
                                                ins=[in_ap.opt()], outs=[out_ap.opt()]),
         dkey=('cc',), inc=1)
    S.emit()


def build_fused(seq=SEQ, upto=6):
    tok = seq // 4
    nc = bass.Bass("TRN2", target_bir_lowering=False)
    E = lambda name, shape, dt=F32: nc.dram_tensor(name, shape, dt, kind="ExternalInput").ap()
    I = lambda name, shape, dt=F32: nc.dram_tensor(name, shape, dt).ap()
    with nc.semaphore("phase") as psem:
        ident = E("ident", [128, 128])
        part0 = I("part0", [seq, D_MODEL])
        io0 = dict(x=E("x", [seq, D_MODEL]), gpre=E("l0_gpre", [128, 8]), win=E("l0_win", [D_MODEL, RET_COLS]),
                   wout=E("l0_wout", [RET_DV, D_MODEL]), tab=E("tab", [seq, 1024]), mask=E("mask", [128, 128]),
                   ident=ident, gc=E("gc", [128, 1]), part=part0)
        build_l0(seq=seq, nc=nc, io=io0, phase=(psem, 0, False))
        red0 = I("red0", [tok, D_MODEL])
        emit_cc(nc, "ReduceScatter", ALU.add, part0, red0, (psem, 1, False))
        if upto == 2:
            out = nc.dram_tensor("out", [tok, D_MODEL], F32, kind="ExternalOutput").ap()
            build_red(ntok=tok, nc=nc, io=dict(parts=[red0], xres=E("xres", [tok, D_MODEL]),
                                               gpost=E("gpost0", [128, D_MODEL]), out=out),
                      phase=(psem, 2, True), npart=1)
            return nc
        x1s = I("x1s", [tok, D_MODEL])
        build_red(ntok=tok, nc=nc, io=dict(parts=[red0], xres=E("xres", [tok, D_MODEL]), gpost=E("gpost0", [128, D_MODEL]),
                                 out=x1s), phase=(psem, 2, False), npart=1)
        R = min(tok, 256)
        NS = tok // R
        xb = [I(f"x1f{i}", [4 * R, D_MODEL]) for i in range(NS)]
        ph = 3
        for i in range(NS):
            emit_cc(nc, "AllGather", ALU.bypass, x1s[i * R:(i + 1) * R, :], xb[i], (psem, ph, False))
            ph += 1

        def xsrc(t0):
            r_, rem = t0 // tok, t0 % tok
            i_, j_ = rem // R, rem % R
            return xb[i_][r_ * R + j_:r_ * R + j_ + CH, :]

        part1 = I("part1", [seq, D_MODEL])
        io1 = dict(xsrc=xsrc, gpre=E("l1_gpre", [128, 8]), win=E("l1_win", [D_MODEL, 2048]),
                   wout=E("l1_wout", [512, D_MODEL]), bias=E("bias", [4, 128, 1024]), cb=E("cb", [128, 4]),
                   lamv=E("lamv", [128, 4, DH]), gsub=E("gsub", [128, 1]), ident=ident, part=part1)
        build_l1(seq=seq, nc=nc, io=io1, phase=(psem, ph, False))
        red1 = I("red1", [tok, D_MODEL])
        emit_cc(nc, "ReduceScatter", ALU.add, part1, red1, (psem, ph + 1, False))
        out = nc.dram_tensor("out", [tok, D_MODEL], F32, kind="ExternalOutput").ap()
        build_red(ntok=tok, nc=nc, io=dict(parts=[red1], xres=x1s, gpost=E("gpost1", [128, D_MODEL]), out=out),
                  phase=(psem, ph + 2, True), npart=1)
    return nc


def fused_inputs(inp, seq=SEQ):
    tok = seq // 4
    f = lambda a: np.asarray(a, dtype=np.float32)
    x = f(inp['x'])
    m0 = l0_inputs(x, f(inp['pre_norm_g']), f(inp['ret_w_in']), f(inp['ret_w_out']), seq)
    m1 = l1_inputs(x, f(inp['pre_norm_g']), f(inp['diff_w_in']), f(inp['diff_w_out']), f(inp['diff_lambda_q1']),
                   f(inp['diff_lambda_k1']), f(inp['diff_lambda_q2']), f(inp['diff_lambda_k2']),
                   f(inp['diff_subln_g']), f(inp['rel_bias']), seq)
    pg = f(inp['post_norm_g'])
    gp = [np.ascontiguousarray(np.broadcast_to(pg[i][None, :], (128, D_MODEL))) for i in range(2)]
    maps = []
    for core in range(NCORES):
        b, r = core // 4, core % 4
        a, c = m0[core], m1[core]
        maps.append({
            "x": a["x"], "l0_gpre": a["gpre"], "l0_win": a["win"], "l0_wout": a["wout"], "tab": a["tab"],
            "mask": a["mask"], "ident": a["ident"], "gc": a["gc"],
            "xres": np.ascontiguousarray(x[b][r * tok:(r + 1) * tok]), "gpost0": gp[0], "gpost1": gp[1],
            "l1_gpre": c["gpre"], "l1_win": c["win"], "l1_wout": c["wout"], "bias": c["bias"], "cb": c["cb"],
            "lamv": c["lamv"], "gsub": c["gsub"],
        })
    return maps


def kernel(**inputs):
    nc = _get('fused', build_fused)
    res = run_bass_kernel_spmd(nc, fused_inputs(inputs), core_ids=list(range(NCORES)))
    out = np.stack([np.concatenate([res.results[b * 4 + r]["out"] for r in range(4)], axis=0)
                    for b in range(BATCH)])
    return out.astype(np.float32)


def kernel_unfused(x, pre_norm_g, post_norm_g, ret_w_in, ret_w_out, diff_w_in, diff_w_out,
                   diff_lambda_q1, diff_lambda_k1, diff_lambda_q2, diff_lambda_k2, diff_subln_g, rel_bias):
    f = lambda a: np.asarray(a, dtype=np.float32)
    x = f(x)
    cores = list(range(NCORES))
    res = run_bass_kernel_spmd(_get('l0', build_l0), l0_inputs(x, f(pre_norm_g), f(ret_w_in), f(ret_w_out)),
                               core_ids=cores)
    parts = [r["part"] for r in res.results]
    res = run_bass_kernel_spmd(_get('red', build_red), red_inputs(parts, x, f(post_norm_g)[0]), core_ids=cores)
    x1 = np.stack([np.concatenate([res.results[b * 4 + r]["out"] for r in range(4)], axis=0) for b in range(BATCH)])
    res = run_bass_kernel_spmd(_get('l1', build_l1), l1_inputs(
        x1, f(pre_norm_g), f(diff_w_in), f(diff_w_out), f(diff_lambda_q1), f(diff_lambda_k1),
        f(diff_lambda_q2), f(diff_lambda_k2), f(diff_subln_g), f(rel_bias)), core_ids=cores)
    parts = [r["part"] for r in res.results]
    res = run_bass_kernel_spmd(_get('red', build_red), red_inputs(parts, x1, f(post_norm_g)[1]), core_ids=cores)
    out = np.stack([np.concatenate([res.results[b * 4 + r]["out"] for r in range(4)], axis=0) for b in range(BATCH)])
    return out.astype(np.float32)
```

```python
import math
from contextlib import ExitStack

import numpy as np
import concourse.bass as bass
import concourse.mybir as mybir
from concourse.bass_utils import run_bass_kernel_spmd

F32 = mybir.dt.float32
BF16 = mybir.dt.bfloat16
AF = mybir.ActivationFunctionType
ALU = mybir.AluOpType
AX = mybir.AxisListType

D_MODEL = 1024
BATCH = 2
SEQ = 8192
D_INNER = 2048
RMS_EPS = 1e-6
GN_EPS = 1e-5
NCORES = 8
NEG = -30000.0

SEM_LIMIT = 30000


class Sched:
    def __init__(self, nc, phase=None):
        self.nc = nc
        self.phase = phase
        self.ops = []
        self.last_w = {}
        self.readers = {}
        self.base = set()

    def op(self, eng, fn, reads=(), writes=(), dkey=None, inc=16):
        idx = len(self.ops)
        deps = set(self.base)
        for b in reads:
            if b in self.last_w:
                deps.add(self.last_w[b])
        for b in writes:
            if b in self.last_w:
                deps.add(self.last_w[b])
            for r in self.readers.get(b, ()):
                deps.add(r)
        for b in reads:
            self.readers.setdefault(b, []).append(idx)
        for b in writes:
            self.last_w[b] = idx
            self.readers[b] = []
        deps.discard(idx)
        self.ops.append(dict(eng=eng, fn=fn, deps=sorted(deps), dkey=dkey, inc=inc))
        return idx

    def dma(self, eng, out, in_, reads=(), writes=(), dkey=None):
        assert dkey is not None
        return self.op(eng, lambda e, o=out, i=in_: e.dma_start(out=o, in_=i), reads, writes, dkey)

    def barrier(self):
        last = {}
        for i, o in enumerate(self.ops):
            if o['dkey'] is not None:
                last[('d', o['dkey'])] = i
            else:
                last[('e', o['eng'])] = i
        self.base = set(last.values())
        self.last_w = {}
        self.readers = {}

    def emit(self):
        nc = self.nc
        ops = self.ops
        n = len(ops)
        need = [False] * n
        for o in ops:
            for d in o['deps']:
                po = ops[d]
                if po['dkey'] is None and o['dkey'] is None and po['eng'] == 'pe' and o['eng'] == 'pe':
                    continue
                need[d] = True
        for i, o in enumerate(ops):
            if o['dkey'] is not None:
                need[i] = True
        last_op = {}
        for i, o in enumerate(ops):
            last_op[o['eng']] = i
        for e_, i in last_op.items():
            need[i] = True
        eng_cnt = {}
        eng_gen = {}
        sem_names = []
        sig = [None] * n
        dcount = {}
        dgen = {}
        for i, o in enumerate(ops):
            if not need[i]:
                continue
            if o['dkey'] is not None:
                k = o['dkey']
                c = dcount.get(k, 0) + o['inc']
                g = dgen.get(k, 0)
                if c > SEM_LIMIT:
                    g += 1
                    c = o['inc']
                dcount[k] = c
                dgen[k] = g
                name = ('d', k, g)
                sig[i] = (name, c, o['inc'])
            else:
                e = o['eng']
                c = eng_cnt.get(e, 0) + 1
                g = eng_gen.get(e, 0)
                if c > SEM_LIMIT:
                    g += 1
                    c = 1
                eng_cnt[e] = c
                eng_gen[e] = g
                name = ('e', e, g)
                sig[i] = (name, c, 1)
            if name not in sem_names:
                sem_names.append(name)
        final = {}
        for i in range(n):
            if sig[i] is not None and ops[i]['dkey'] is not None:
                final[sig[i][0]] = max(final.get(sig[i][0], 0), sig[i][1])
        with ExitStack() as st:
            sems = {}
            for j, name in enumerate(sem_names):
                spfx = f"p{self.phase[1]}" if self.phase is not None else ""
                sems[name] = st.enter_context(nc.semaphore(f"{spfx}s{j}"))
            block = st.enter_context(nc.Block())
            per = {e: [] for e in ('pe', 'act', 'dve', 'pool', 'sp')}
            for i, o in enumerate(ops):
                per[o['eng']].append(i)

            def run(engname, eng):
                waited = {}
                if self.phase is not None and self.phase[1] > 0:
                    eng.wait_ge(self.phase[0], self.phase[1])
                for i in per[engname]:
                    o = ops[i]
                    wl = {}
                    for d in o['deps']:
                        po = ops[d]
                        if po['dkey'] is None and o['dkey'] is None and po['eng'] == 'pe' and engname == 'pe':
                            continue
                        nm, c, _ = sig[d]
                        if waited.get(nm, 0) >= c:
                            continue
                        wl[nm] = max(wl.get(nm, 0), c)
                    for nm, c in wl.items():
                        eng.wait_ge(sems[nm], c)
                        waited[nm] = c
                    ins = o['fn'](eng)
                    if sig[i] is not None:
                        ins.then_inc(sems[sig[i][0]], sig[i][2])
                if engname == 'pool':
                    fin = dict(final)
                    for e_, i in last_op.items():
                        nm, c, _ = sig[i]
                        fin[nm] = max(fin.get(nm, 0), c)
                    for nm, c in fin.items():
                        if waited.get(nm, 0) < c:
                            eng.wait_ge(sems[nm], c)
                    lastc = None
                    for nm in sem_names:
                        lastc = eng.sem_clear(sems[nm])
                    if self.phase is not None:
                        if self.phase[2]:
                            eng.sem_clear(self.phase[0])
                        else:
                            lastc.then_inc(self.phase[0], 1)

            @block.tensor
            def _(e):
                run('pe', e)

            @block.scalar
            def _(e):
                run('act', e)

            @block.vector
            def _(e):
                run('dve', e)

            @block.gpsimd
            def _(e):
                run('pool', e)

            @block.sync
            def _(e):
                run('sp', e)


def _dt(nc, io, name, shape, dt, kind):
    if io is not None and name in io:
        return io[name]
    if kind == "Internal":
        return nc.dram_tensor(name, shape, dt).ap()
    return nc.dram_tensor(name, shape, dt, kind=kind).ap()


RET_DK = 256
RET_DV = 512
RET_COLS = 2 * RET_DK + 2 * RET_DV
CH = 128
NCH = SEQ // CH


def build_l0(seq=SEQ, dbg=False, nc=None, io=None, phase=None):
    nchunks = seq // CH
    if nc is None:
        nc = bass.Bass("TRN2", target_bir_lowering=False)
    x_d = _dt(nc, io, "x", [seq, D_MODEL], F32, "ExternalInput")
    gpre_d = _dt(nc, io, "gpre", [128, 8], F32, "ExternalInput")
    win_d = _dt(nc, io, "win", [D_MODEL, RET_COLS], F32, "ExternalInput")
    wout_d = _dt(nc, io, "wout", [RET_DV, D_MODEL], F32, "ExternalInput")
    tab_d = _dt(nc, io, "tab", [seq, 1024], F32, "ExternalInput")
    mask_d = _dt(nc, io, "mask", [128, 128], F32, "ExternalInput")
    ident_d = _dt(nc, io, "ident", [128, 128], F32, "ExternalInput")
    gc_d = _dt(nc, io, "gc", [128, 1], F32, "ExternalInput")
    part_d = _dt(nc, io, "part", [seq, D_MODEL], F32, "ExternalOutput")
    if dbg:
        dbg_d = {nm: nc.dram_tensor("dbg_" + nm, shp, dt, kind="ExternalOutput").ap() for nm, shp, dt in (
            ("xn", [128, 1024], BF16), ("hT", [128, 1024], BF16), ("qk", [128, 512], BF16), ("vb", [128, 512], BF16),
            ("sg", [128, 512], F32), ("sT", [128, 128], BF16), ("yn", [128, 512], F32), ("gated", [128, 512], BF16),
            ("qkT", [128, 512], BF16), ("win", [128, 8 * RET_COLS], BF16), ("rstd", [128, 1], F32))}

    with ExitStack() as st:
        pfx = f"p{phase[1]}_" if phase is not None else ""

        def sb(name, shape, dt):
            return st.enter_context(nc.sbuf_tensor(pfx + name, shape, dt))

        def ps(name, shape, dt):
            return st.enter_context(nc.psum_tensor(pfx + name, shape, dt))

        win = sb("win_bf", [128, 8, RET_COLS], BF16)
        wout = sb("wout_bf", [128, 4, D_MODEL], BF16)
        wst = [sb(f"wst{i}", [128, 1024], F32) for i in range(2)]
        gpre = sb("gpre_s", [128, 8], F32)
        mask = sb("mask_s", [128, 128], F32)
        ident_f = sb("ident_f", [128, 128], F32)
        ident = sb("ident_b", [128, 128], BF16)
        gc = sb("gc_s", [128, 1], F32)
        nhalf = sb("nhalf", [128, 1], F32)
        NX = 3
        xt = [sb(f"xt{i}", [128, D_MODEL], F32) for i in range(NX)]
        tabt = [sb(f"tab{i}", [128, 1024], F32) for i in range(3)]
        sq_junk = sb("sqj", [128, D_MODEL], BF16)
        ssum = [sb(f"ss{i}", [128, 1], F32) for i in range(2)]
        rstd = [sb(f"rstd{i}", [128, 1], F32) for i in range(2)]
        xn = [sb(f"xn{i}", [128, D_MODEL], BF16) for i in range(2)]
        hT = [sb(f"hT{i}", [128, 8, 128], BF16) for i in range(2)]
        qkr = [sb(f"qkr{i}", [128, 512], F32) for i in range(2)]
        e1 = [sb(f"e1_{i}", [128, 512], F32) for i in range(2)]
        e2 = [sb(f"e2_{i}", [128, 512], F32) for i in range(2)]
        qk = [sb(f"qk{i}", [128, 512], BF16) for i in range(2)]
        qkT = [sb(f"qkT{i}", [128, 4, 128], BF16) for i in range(2)]
        vb = [sb(f"vb{i}", [128, 512], BF16) for i in range(2)]
        sg = [sb(f"sg{i}", [128, 512], F32) for i in range(3)]
        sT = [sb(f"sT{i}", [128, 128], BF16) for i in range(2)]
        U = sb("U", [128, 2, 512], F32)
        Rb = [sb(f"Rb{i}", [128, 2, 512], BF16) for i in range(2)]
        stats = [sb(f"st{i}", [128, 6], F32) for i in range(2)]
        mv = [sb(f"mv{i}", [128, 2], F32) for i in range(2)]
        grs = [sb(f"grs{i}", [128, 1], F32) for i in range(2)]
        gnb = [sb(f"gnb{i}", [128, 1], F32) for i in range(2)]
        yn = [sb(f"yn{i}", [128, 512], F32) for i in range(2)]
        gated = [sb(f"gated{i}", [128, 512], BF16) for i in range(2)]
        gT = sb("gT_all", [128, 4, seq], BF16)
        ost = [sb(f"ost{i}", [128, D_MODEL], F32) for i in range(2)]

        p_xT = ps("p_xT", [128, 8, 128], BF16)
        p_qk = ps("p_qk", [128, 512], F32)
        p_v = ps("p_v", [128, 512], F32)
        p_g = ps("p_g", [128, 512], F32)
        p_m = ps("p_m", [128, 512], F32)
        p_mb = p_m[:].bitcast(BF16)
        p_A = [ps(f"p_A{i}", [128, 512], F32) for i in range(2)]
        p_o = ps("p_o", [128, 512], F32)

        S = Sched(nc, phase)

        S.dma('sp', gpre[:], gpre_d, writes=['gpre'], dkey=('c0', 1))
        S.dma('sp', mask[:], mask_d, writes=['mask'], dkey=('c0', 2))
        S.dma('sp', ident_f[:], ident_d, writes=['identf'], dkey=('c0', 3))
        S.dma('sp', gc[:], gc_d, writes=['gc'], dkey=('c0', 4))
        S.op('dve', lambda e: e.tensor_copy(ident[:], ident_f[:]), reads=['identf'], writes=['ident'])
        S.op('dve', lambda e: e.memset(U[:], 0.0), writes=['U'])
        S.op('dve', lambda e: e.memset(nhalf[:], -0.5), writes=['nhalf'])
        S.op('dve', lambda e: e.memset(Rb[0][:], 0.0), writes=[('Rb', 0)])
        k = 0
        for c in range(8):
            for (c0, c1) in ((0, 1024), (1024, RET_COLS)):
                w = wst[k % 2]
                S.dma('sp', w[:, 0:c1 - c0], win_d[c * 128:(c + 1) * 128, c0:c1],
                      writes=[('wst', k % 2)], dkey=('wst', k % 2))
                S.op('dve', lambda e, w=w, c=c, c0=c0, c1=c1: e.tensor_scalar(
                    win[:, c, c0:c1], w[:, 0:c1 - c0], gpre[:, c:c + 1], None, ALU.mult),
                    reads=[('wst', k % 2), 'gpre'], writes=['win'])
                k += 1
        for j in range(4):
            w = wst[k % 2]
            S.dma('sp', w[:], wout_d[j * 128:(j + 1) * 128, :], writes=[('wst', k % 2)], dkey=('wst', k % 2))
            S.op('dve', lambda e, w=w, j=j: e.tensor_copy(wout[:, j, :], w[:]),
                 reads=[('wst', k % 2)], writes=['wout'])
            k += 1

        def s1a_act(i):
            a = i % NX
            b2 = i % 2
            t0 = i * CH
            S.dma('sp', xt[a][:], x_d[t0:t0 + CH, :], writes=[('xt', a)], dkey=('xt', a))
            S.dma('sp', tabt[i % 3][:], tab_d[t0:t0 + CH, :], writes=[('tab', i % 3)], dkey=('tab', i % 3))
            S.op('act', lambda e: e.activation(sq_junk[:], xt[a][:], AF.Square, scale=1.0 / 32.0,
                                               accum_out=ssum[b2][:]),
                 reads=[('xt', a)], writes=['sqj', ('ss', b2)])
            S.op('dve', lambda e: e.tensor_scalar(ssum[b2][:], ssum[b2][:], RMS_EPS, None, ALU.add),
                 reads=[('ss', b2)], writes=[('ss', b2)])
            S.op('pool', lambda e: e.tensor_tensor(rstd[b2][:], ssum[b2][:], nhalf[:], ALU.pow),
                 reads=[('ss', b2), 'nhalf'], writes=[('rstd', b2)])
            S.op('act', lambda e: e.activation(xn[b2][:], xt[a][:], AF.Copy, scale=rstd[b2][:]),
                 reads=[('xt', a), ('rstd', b2)], writes=[('xn', b2)])

        def s1a_pe(i):
            b2 = i % 2
            for c in range(8):
                S.op('pe', lambda e, c=c: e.transpose(p_xT[:, c, :], xn[b2][:, c * 128:(c + 1) * 128], ident[:]),
                     reads=[('xn', b2), 'ident'], writes=['p_xT'])
            S.op('dve', lambda e: e.tensor_copy(hT[b2][:], p_xT[:]), reads=['p_xT'], writes=[('hT', b2)])

        def s1b(i):
            b2 = i % 2
            for (pt, nm, c0) in ((p_qk, 'p_qk', 0), (p_v, 'p_v', 512), (p_g, 'p_g', 1024)):
                for c in range(8):
                    S.op('pe', lambda e, pt=pt, c=c, c0=c0: e.matmul(
                        pt[:], hT[b2][:, c, :], win[:, c, c0:c0 + 512], start=(c == 0), stop=(c == 7)),
                        reads=[('hT', b2), 'win'], writes=[nm])

        def s2a(i):
            b2 = i % 2
            S.op('act', lambda e: e.activation(qkr[b2][:], p_qk[:], AF.Copy), reads=['p_qk'], writes=[('qkr', b2)])
            S.op('act', lambda e: e.activation(vb[b2][:], p_v[:], AF.Copy), reads=['p_v'], writes=[('vb', b2)])
            S.op('act', lambda e: e.activation(sg[i % 3][:], p_g[:], AF.Silu), reads=['p_g'], writes=[('sg', i % 3)])

        def s2b(i):
            b2 = i % 2
            b3 = i % 3
            tA = tabt[b3][:, 0:512]
            tB = tabt[b3][:, 512:1024]
            S.op('dve', lambda e: e.tensor_tensor(e1[b2][:], qkr[b2][:], tA, ALU.mult),
                 reads=[('qkr', b2), ('tab', b3)], writes=[('e1', b2)])
            pq = qkr[b2][:].rearrange("p (a h d) -> p a h d", a=2, h=2)
            tBv = tB.rearrange("p (a h d) -> p a h d", a=2, h=2)
            e2v = e2[b2][:].rearrange("p (a h d) -> p a h d", a=2, h=2)
            S.op('pool', lambda e: e.tensor_tensor(e2v[:, :, 0, :], pq[:, :, 1, :], tBv[:, :, 0, :], ALU.mult),
                 reads=[('qkr', b2), ('tab', b3)], writes=[('e2', b2, 0)])
            S.op('pool', lambda e: e.tensor_tensor(e2v[:, :, 1, :], pq[:, :, 0, :], tBv[:, :, 1, :], ALU.mult),
                 reads=[('qkr', b2), ('tab', b3)], writes=[('e2', b2, 1)])
            S.op('dve', lambda e: e.tensor_tensor(qk[b2][:], e1[b2][:], e2[b2][:], ALU.add),
                 reads=[('e1', b2), ('e2', b2, 0), ('e2', b2, 1)], writes=[('qk', b2)])
            for c in range(4):
                S.op('pe', lambda e, c=c: e.transpose(p_mb[:, c * 128:(c + 1) * 128],
                                                      qk[b2][:, c * 128:(c + 1) * 128], ident[:]),
                     reads=[('qk', b2), 'ident'], writes=['p_m'])
            S.op('dve', lambda e: e.tensor_copy(qkT[b2][:].rearrange("p a d -> p (a d)"), p_mb[:, 0:512]),
                 reads=['p_m'], writes=[('qkT', b2)])

        def s3(i):
            b2 = i % 2
            rb = i % 2
            for dc in range(2):
                S.op('pe', lambda e, dc=dc: e.matmul(p_m[:, 0:128], qkT[b2][:, 2 + dc, :], qkT[b2][:, dc, :],
                                                     start=(dc == 0), stop=(dc == 1)),
                     reads=[('qkT', b2)], writes=['p_m'])
            S.op('dve', lambda e: e.tensor_tensor(sT[b2][:], p_m[:, 0:128], mask[:], ALU.mult),
                 reads=['p_m', 'mask'], writes=[('sT', b2)])
            for dc in range(2):
                S.op('pe', lambda e, dc=dc: e.matmul(p_A[dc][:], qk[b2][:, 256 + dc * 128:256 + (dc + 1) * 128],
                                                     vb[b2][:], start=True, stop=True),
                     reads=[('qk', b2), ('vb', b2)], writes=[('p_A', dc)])
            S.op('pe', lambda e: e.matmul(p_o[:], sT[b2][:], vb[b2][:], start=True, stop=False),
                 reads=[('sT', b2), ('vb', b2)], writes=['p_o'])
            for dc in range(2):
                S.op('pe', lambda e, dc=dc: e.matmul(p_o[:], qkT[b2][:, dc, :], Rb[rb][:, dc, :],
                                                     start=False, stop=(dc == 1)),
                     reads=[('qkT', b2), ('Rb', rb)], writes=['p_o'])
            for dc in range(2):
                S.op('dve', lambda e, dc=dc: e.scalar_tensor_tensor(
                    U[:, dc, :], U[:, dc, :], gc[:, 0:1], p_A[dc][:], ALU.mult, ALU.add),
                    reads=['U', ('p_A', dc), 'gc'], writes=['U'])
            S.op('act', lambda e: e.activation(Rb[1 - rb][:].rearrange("p a d -> p (a d)"),
                                               U[:].rearrange("p a d -> p (a d)"), AF.Copy, scale=gc[:, 0:1]),
                 reads=['U', 'gc'], writes=[('Rb', 1 - rb)])

        def s4a(i):
            b2 = i % 2
            t0 = i * CH
            S.op('dve', lambda e: e.bn_stats(stats[b2][:], p_o[:]), reads=['p_o'], writes=[('stats', b2)])
            S.op('dve', lambda e: e.bn_aggr(mv[b2][:], stats[b2][:]), reads=[('stats', b2)], writes=[('mv', b2)])
            S.op('dve', lambda e: e.tensor_scalar(mv[b2][:, 1:2], mv[b2][:, 1:2], GN_EPS, None, ALU.add),
                 reads=[('mv', b2)], writes=[('mv', b2)])
            S.op('pool', lambda e: e.tensor_tensor(grs[b2][:], mv[b2][:, 1:2], nhalf[:], ALU.pow),
                 reads=[('mv', b2), 'nhalf'], writes=[('grs', b2)])
            S.op('dve', lambda e: e.scalar_tensor_tensor(gnb[b2][:], mv[b2][:, 0:1], -1.0, grs[b2][:],
                                                         ALU.mult, ALU.mult),
                 reads=[('mv', b2), ('grs', b2)], writes=[('gnb', b2)])
            S.op('act', lambda e: e.activation(yn[b2][:], p_o[:], AF.Identity, bias=gnb[b2][:], scale=grs[b2][:]),
                 reads=['p_o', ('gnb', b2), ('grs', b2)], writes=[('yn', b2)])
            S.op('pool', lambda e: e.tensor_tensor(gated[b2][:], yn[b2][:], sg[i % 3][:], ALU.mult),
                 reads=[('yn', b2), ('sg', i % 3)], writes=[('gated', b2)])

        def s4b(i):
            b2 = i % 2
            t0 = i * CH
            for c in range(4):
                S.op('pe', lambda e, c=c: e.transpose(p_mb[:, 512 + c * 128:512 + (c + 1) * 128],
                                                      gated[b2][:, c * 128:(c + 1) * 128], ident[:]),
                     reads=[('gated', b2), 'ident'], writes=['p_m'])
            S.op('act', lambda e: e.activation(
                gT[:, :, t0:t0 + CH], p_mb[:, 512:1024].rearrange("p (a d) -> p a d", a=4), AF.Copy),
                reads=['p_m'], writes=[('gT', i)])

        s1a_act(0)
        s1a_pe(0)
        if nchunks > 1:
            s1a_act(1)
            s1a_pe(1)
        s1b(0)
        s2a(0)
        for i in range(nchunks):
            if i + 2 < nchunks:
                s1a_act(i + 2)
            if i + 1 < nchunks:
                s1b(i + 1)
                s2a(i + 1)
            if i + 2 < nchunks:
                s1a_pe(i + 2)
            s2b(i)
            if i >= 1:
                s4a(i - 1)
                s4b(i - 1)
            s3(i)
            if i == nchunks - 1:
                s4a(i)
                s4b(i)
            if dbg and i == 0:
                for nm, t, rd in (("xn", xn[0][:], ('xn', 0)), ("hT", hT[0][:].rearrange("p a d -> p (a d)"), ('hT', 0)),
                                  ("qk", qk[0][:], ('qk', 0)), ("vb", vb[0][:], ('vb', 0)), ("sg", sg[0][:], ('sg', 0)),
                                  ("sT", sT[0][:], ('sT', 0)), ("yn", yn[0][:], ('yn', 0)), ("gated", gated[0][:], ('gated', 0)),
                                  ("qkT", qkT[0][:].rearrange("p a d -> p (a d)"), ('qkT', 0)),
                                  ("win", win[:].rearrange("p a d -> p (a d)"), 'win'), ("rstd", rstd[0][:], ('rstd', 0))):
                    S.dma('sp', dbg_d[nm], t, reads=[rd], dkey=('dbg', nm))

        S.barrier()
        pp = [p_qk, p_v, p_g, p_o]
        for i in range(nchunks):
            t0 = i * CH
            b2 = i % 2
            for half in range(2):
                pt = pp[b2 * 2 + half]
                nm = ('pp', b2 * 2 + half)
                for j in range(4):
                    S.op('pe', lambda e, pt=pt, j=j, half=half, t0=t0: e.matmul(
                        pt[:], gT[:, j, t0:t0 + CH], wout[:, j, half * 512:(half + 1) * 512],
                        start=(j == 0), stop=(j == 3)), reads=[], writes=[nm])
                if half == 0:
                    S.op('act', lambda e, pt=pt, b2=b2: e.activation(ost[b2][:, 0:512], pt[:], AF.Copy),
                         reads=[nm], writes=[('ost', b2, 0)])
                else:
                    S.op('dve', lambda e, pt=pt, b2=b2: e.tensor_copy(ost[b2][:, 512:1024], pt[:]),
                         reads=[nm], writes=[('ost', b2, 1)])
            S.dma('sp', part_d[t0:t0 + CH, :], ost[b2][:], reads=[('ost', b2, 0), ('ost', b2, 1)],
                  writes=[], dkey=('ost', b2))
        S.emit()
    return nc


def rope_decay_table(head, seq=SEQ):
    half = RET_DK // 2
    inv = (10000.0 ** (-np.arange(half, dtype=np.float64) / half))
    pos = np.arange(seq, dtype=np.float64)
    ang = pos[:, None] * inv[None, :]
    c, s = np.cos(ang), np.sin(ang)
    lg = math.log1p(-2.0 ** (-5.0 - head))
    cidx = (np.arange(seq) % CH).astype(np.float64)
    fq = np.exp(lg * (cidx + 1.0))[:, None]
    fk = np.exp(-lg * (cidx + 1.0))[:, None] * (RET_DK ** -0.5)
    A = np.concatenate([c * fq, c * fq, c * fk, c * fk], axis=1)
    B = np.concatenate([-s * fq, s * fq, -s * fk, s * fk], axis=1)
    return np.concatenate([A, B], axis=1).astype(np.float32)


def l0_inputs(x, pre_norm_g, ret_w_in, ret_w_out, seq=SEQ):
    maps = []
    maskT = np.triu(np.ones((CH, CH), np.float32))
    ident = np.eye(128, dtype=np.float32)
    gpre = np.ascontiguousarray(pre_norm_g[0].reshape(8, 128).T)
    for core in range(NCORES):
        b, h = core // 4, core % 4
        w = ret_w_in[0]
        cols = np.concatenate([
            np.arange(h * RET_DK, (h + 1) * RET_DK),
            1024 + np.arange(h * RET_DK, (h + 1) * RET_DK),
            2048 + np.arange(h * RET_DV, (h + 1) * RET_DV),
            2048 + D_INNER + np.arange(h * RET_DV, (h + 1) * RET_DV)])
        gC = math.exp(math.log1p(-2.0 ** (-5.0 - h)) * CH)
        maps.append({
            "x": np.ascontiguousarray(x[b][:seq]),
            "gpre": gpre,
            "win": np.ascontiguousarray(w[:, cols]),
            "wout": np.ascontiguousarray(ret_w_out[0][h * RET_DV:(h + 1) * RET_DV, :]),
            "tab": rope_decay_table(h, seq),
            "mask": maskT,
            "ident": ident,
            "gc": np.full((128, 1), gC, np.float32),
        })
    return maps


LAMBDA_INIT = 0.8 - 0.6 * math.exp(-0.3 * 1)
DH = 64
QT = 512


def build_l1(seq=SEQ, dbg=False, nc=None, io=None, phase=None):
    nt = seq // QT
    nch = seq // CH
    if nc is None:
        nc = bass.Bass("TRN2", target_bir_lowering=False)
    if io is not None and 'xsrc' in io:
        xsrc = io['xsrc']
    else:
        x_d = _dt(nc, io, "x", [seq, D_MODEL], F32, "ExternalInput")
        xsrc = lambda t0: x_d[t0:t0 + CH, :]
    gpre_d = _dt(nc, io, "gpre", [128, 8], F32, "ExternalInput")
    win_d = _dt(nc, io, "win", [D_MODEL, 2048], F32, "ExternalInput")
    wout_d = _dt(nc, io, "wout", [512, D_MODEL], F32, "ExternalInput")
    bias_d = _dt(nc, io, "bias", [4, 128, 1024], F32, "ExternalInput")
    cb_d = _dt(nc, io, "cb", [128, 4], F32, "ExternalInput")
    lamv_d = _dt(nc, io, "lamv", [128, 4, DH], F32, "ExternalInput")
    gsub_d = _dt(nc, io, "gsub", [128, 1], F32, "ExternalInput")
    ident_d = _dt(nc, io, "ident", [128, 128], F32, "ExternalInput")
    part_d = _dt(nc, io, "part", [seq, D_MODEL], F32, "ExternalOutput")
    qT_d = _dt(nc, io, "qT_scr", [4, 128, seq], BF16, "Internal")
    kT_d = _dt(nc, io, "kT_scr", [4, 128, seq], BF16, "Internal")
    sgT_d = _dt(nc, io, "sgT_scr", [4, 128, seq], BF16, "Internal")
    v_d = _dt(nc, io, "v_scr", [seq, 512], BF16, "Internal")
    if dbg:
        dbg_d = {nm: nc.dram_tensor("dbg_" + nm, shp, dt, kind="ExternalOutput").ap() for nm, shp, dt in (
            ("gT", [128, 4 * seq], BF16), ("nlam", [128, 1], F32), ("qTd", [4, 128, seq], BF16),
            ("sgTd", [4, 128, seq], BF16), ("kTd", [4, 128, seq], BF16))}

    with ExitStack() as st:
        pfx = f"p{phase[1]}_" if phase is not None else ""

        def sb(name, shape, dt):
            return st.enter_context(nc.sbuf_tensor(pfx + name, shape, dt))

        def ps(name, shape, dt):
            return st.enter_context(nc.psum_tensor(pfx + name, shape, dt))

        big = sb("big", [128, 4 * SEQ], BF16)
        win = big[:, 0:8 * 2048].rearrange("p (a d) -> p a d", a=8)
        gT = big[:, 0:4 * seq].rearrange("p (a d) -> p a d", a=4)
        wout = sb("wout_bf", [128, 4, D_MODEL], BF16)
        wst = [sb(f"wst{i}", [128, 1024], F32) for i in range(2)]
        gpre = sb("gpre_s", [128, 8], F32)
        ident_f = sb("ident_f", [128, 128], F32)
        ident = sb("ident_b", [128, 128], BF16)
        ones = sb("ones_b", [128, 128], BF16)
        nhalf = sb("nhalf", [128, 1], F32)
        epsg = sb("epsg", [128, 1], F32)
        Phi = [sb(f"Phi{m}", [128, QT], BF16) for m in range(2)]
        Plo = [sb(f"Plo{m}", [128, QT], BF16) for m in range(2)]
        Pacc = [sb(f"Pacc{m}", [128, QT], F32) for m in range(2)]
        cb = sb("cb_s", [128, 4], F32)
        lamv = sb("lamv_s", [128, 4, DH], F32)
        lamp = sb("lamp", [128, 2, DH], F32)
        lams = sb("lams", [128, 2], F32)
        nlam = sb("nlam", [128, 1], F32)
        gsc = sb("gsc", [128, 1], F32)
        NX = 3
        xt = [sb(f"xt{i}", [128, D_MODEL], F32) for i in range(NX)]
        sq_junk = sb("sqj", [128, D_MODEL], BF16)
        ssum = [sb(f"ss{i}", [128, 1], F32) for i in range(2)]
        rstd = [sb(f"rstd{i}", [128, 1], F32) for i in range(2)]
        xn = [sb(f"xn{i}", [128, D_MODEL], BF16) for i in range(2)]
        hT = [sb(f"hT{i}", [128, 8, QT], BF16) for i in range(2)]
        NF = 4
        fst = [sb(f"fst{i}", [128, QT], BF16) for i in range(NF)]
        vst = [sb(f"vst{i}", [128, 512], BF16) for i in range(2)]
        kT = sb("kT_s", [128, seq], BF16)
        vS = sb("v_s", [128, nch, 128], BF16)
        bt = sb("bt_s", [128, 1024], F32)
        qt = [sb(f"qt{i}", [128, QT], BF16) for i in range(2)]
        sgt = [sb(f"sgt{i}", [128, QT], BF16) for i in range(2)]
        NP = 3
        PT = [[sb(f"PT{m}_{i}", [128, QT], BF16) for i in range(NP)] for m in range(2)]
        tmp = [[sb(f"tmp{m}_{i}", [128, QT], F32) for i in range(2)] for m in range(2)]
        O1s = sb("O1s", [128, QT], F32)
        O2s = sb("O2s", [128, QT], F32)
        s1s = sb("s1s", [128, QT], F32)
        s2s = sb("s2s", [128, QT], F32)
        osb = sb("osb", [128, QT], F32)
        sqb = sb("sqb", [128, QT], BF16)
        rsb = sb("rsb", [128, QT], F32)
        ost = [sb(f"ost{i}", [128, D_MODEL], F32) for i in range(2)]

        banks = [ps(f"bk{i}", [128, 512], F32) for i in range(8)]

        S = Sched(nc, phase)
        S.dma('sp', gpre[:], gpre_d, writes=['gpre'], dkey=('c0', 5))
        S.dma('sp', ident_f[:], ident_d, writes=['identf'], dkey=('c0', 6))
        S.dma('sp', cb[:], cb_d, writes=['cb'], dkey=('c0', 7))
        S.dma('sp', lamv[:], lamv_d, writes=['lamv'], dkey=('c0', 8))
        S.dma('sp', gsc[:], gsub_d, writes=['gsc'], dkey=('c0', 9))
        S.op('dve', lambda e: e.tensor_copy(ident[:], ident_f[:]), reads=['identf'], writes=['ident'])
        S.op('dve', lambda e: e.memset(ones[:], 1.0), writes=['ones'])
        S.op('dve', lambda e: e.memset(nhalf[:], -0.5), writes=['nhalf'])
        S.op('dve', lambda e: e.memset(epsg[:], GN_EPS), writes=['epsg'])
        S.op('dve', lambda e: e.tensor_tensor(lamp[:, 0, :], lamv[:, 0, :], lamv[:, 1, :], ALU.mult),
             reads=['lamv'], writes=['lamp'])
        S.op('dve', lambda e: e.tensor_tensor(lamp[:, 1, :], lamv[:, 2, :], lamv[:, 3, :], ALU.mult),
             reads=['lamv'], writes=['lamp'])
        S.op('dve', lambda e: e.reduce_sum(lams[:], lamp[:], AX.X), reads=['lamp'], writes=['lams'])
        S.op('act', lambda e: e.activation(lams[:], lams[:], AF.Exp), reads=['lams'], writes=['lams'])
        S.op('dve', lambda e: e.tensor_tensor(nlam[:], lams[:, 1:2], lams[:, 0:1], ALU.subtract),
             reads=['lams'], writes=['nlam'])
        S.op('dve', lambda e: e.tensor_scalar(nlam[:], nlam[:], -LAMBDA_INIT, None, ALU.add),
             reads=['nlam'], writes=['nlam'])
        S.op('dve', lambda e: e.tensor_scalar(gsc[:], gsc[:], 1.0 - LAMBDA_INIT, None, ALU.mult),
             reads=['gsc'], writes=['gsc'])
        k = 0
        for c in range(8):
            for c0 in (0, 1024):
                w = wst[k % 2]
                S.dma('sp', w[:], win_d[c * 128:(c + 1) * 128, c0:c0 + 1024],
                      writes=[('wst', k % 2)], dkey=('wst', k % 2))
                S.op('dve', lambda e, w=w, c=c, c0=c0: e.tensor_scalar(
                    win[:, c, c0:c0 + 1024], w[:], gpre[:, c:c + 1], None, ALU.mult),
                    reads=[('wst', k % 2), 'gpre'], writes=['win'])
                k += 1
        for j in range(4):
            w = wst[k % 2]
            S.dma('sp', w[:], wout_d[j * 128:(j + 1) * 128, :], writes=[('wst', k % 2)], dkey=('wst', k % 2))
            S.op('dve', lambda e, w=w, j=j: e.tensor_copy(wout[:, j, :], w[:]),
                 reads=[('wst', k % 2)], writes=['wout'])
            k += 1

        cnt = {'f': 0, 'v': 0, 'x': 0}
        normq = []

        def p1_norm(ti, sub):
            i = cnt['x']
            cnt['x'] += 1
            a = i % NX
            b2 = i % 2
            tb = ti % 2
            t0 = ti * QT + sub * CH
            pxT = banks[b2][:].bitcast(BF16).rearrange("p (a d) -> p a d", a=8)
            S.dma('sp', xt[a][:], xsrc(t0), writes=[('xt', a)], dkey=('xt', a))
            S.op('act', lambda e: e.activation(sq_junk[:], xt[a][:], AF.Square, scale=1.0 / 32.0,
                                               accum_out=ssum[b2][:]),
                 reads=[('xt', a)], writes=['sqj', ('ss', b2)])
            S.op('dve', lambda e: e.tensor_scalar(ssum[b2][:], ssum[b2][:], RMS_EPS, None, ALU.add),
                 reads=[('ss', b2)], writes=[('ss', b2)])
            S.op('pool', lambda e: e.tensor_tensor(rstd[b2][:], ssum[b2][:], nhalf[:], ALU.pow),
                 reads=[('ss', b2), 'nhalf'], writes=[('rstd', b2)])
            S.op('act', lambda e: e.activation(xn[b2][:], xt[a][:], AF.Copy, scale=rstd[b2][:]),
                 reads=[('xt', a), ('rstd', b2)], writes=[('xn', b2)])
            normq.append((ti, sub, b2))

        def p1_norm_pe():
            ti, sub, b2 = normq.pop(0)
            tb = ti % 2
            pxT = banks[b2][:].bitcast(BF16).rearrange("p (a d) -> p a d", a=8)
            for c in range(8):
                S.op('pe', lambda e, c=c: e.transpose(pxT[:, c, :], xn[b2][:, c * 128:(c + 1) * 128], ident[:]),
                     reads=[('xn', b2), 'ident'], writes=[('bk', b2)])
            S.op('dve', lambda e: e.tensor_copy(hT[tb][:, :, sub * CH:(sub + 1) * CH], pxT),
                 reads=[('bk', b2)], writes=[('hT', tb, sub)])

        def p1_v(ti, sub):
            tb = ti % 2
            kk = cnt['v']
            cnt['v'] += 1
            bkx = 2 + kk % 2
            pv = banks[bkx]
            t0 = ti * QT + sub * CH
            for c in range(8):
                S.op('pe', lambda e, c=c: e.matmul(pv[:], hT[tb][:, c, sub * CH:(sub + 1) * CH],
                                                   win[:, c, 1536:2048], start=(c == 0), stop=(c == 7)),
                     reads=[('hT', tb, sub), 'win'], writes=[('bk', bkx)])
            vs = vst[kk % 2]
            S.op('dve', lambda e: e.tensor_copy(vs[:], pv[:]), reads=[('bk', bkx)], writes=[('vst', kk % 2)])
            S.dma('sp', v_d[t0:t0 + CH, :], vs[:], reads=[('vst', kk % 2)], dkey=('vst', kk % 2))

        def p1_f(ti, gi):
            tb = ti % 2
            kk = cnt['f']
            cnt['f'] += 1
            bkx = 4 + kk % 4
            pf = banks[bkx]
            t0 = ti * QT
            for c in range(8):
                S.op('pe', lambda e, c=c: e.matmul(pf[:], win[:, c, gi * 128:(gi + 1) * 128], hT[tb][:, c, :],
                                                   start=(c == 0), stop=(c == 7)),
                     reads=[('hT', tb, 0), ('hT', tb, 1), ('hT', tb, 2), ('hT', tb, 3), 'win'],
                     writes=[('bk', bkx)])
            fs = fst[kk % NF]
            h = gi % 4
            if gi < 4:
                S.op('act', lambda e: e.activation(fs[:], pf[:], AF.Copy, scale=DH ** -0.5),
                     reads=[('bk', bkx)], writes=[('fst', kk % NF)])
                dst = qT_d[h, :, t0:t0 + QT]
            elif gi < 8:
                S.op('dve', lambda e: e.tensor_copy(fs[:], pf[:]), reads=[('bk', bkx)], writes=[('fst', kk % NF)])
                dst = kT_d[h, :, t0:t0 + QT]
            else:
                S.op('act', lambda e: e.activation(fs[:], pf[:], AF.Silu),
                     reads=[('bk', bkx)], writes=[('fst', kk % NF)])
                dst = sgT_d[h, :, t0:t0 + QT]
            S.dma('sp', dst, fs[:], reads=[('fst', kk % NF)], dkey=('fst', kk % NF))

        for sub in range(4):
            p1_norm(0, sub)
            p1_norm_pe()
        for ti in range(nt):
            if ti + 1 < nt:
                p1_norm(ti + 1, 0)
            for sub in range(4):
                p1_v(ti, sub)
            for gi in range(12):
                p1_f(ti, gi)
                if ti + 1 < nt and gi in (1, 4, 7, 10):
                    p1_norm_pe()
                    if gi < 10:
                        p1_norm(ti + 1, (gi - 1) // 3 + 1)

        S.barrier()
        SB = [banks[0], banks[1], banks[2]]
        pSS = banks[3]
        pO = [banks[4], banks[5]]
        pZ = [banks[6], banks[7]]
        items = [(h, qi, j) for h in range(4) for qi in range(nt) for j in range(4 * qi + 4)]
        nit = len(items)

        def geom(t):
            h, qi, j = items[t]
            r = j - 4 * qi
            lo = 128 * r if r > 0 else 0
            qb = (h * nt + qi) % 2
            return h, qi, j, r, lo, qb

        def QK(t):
            h, qi, j, r, lo, qb = geom(t)
            for m in range(2):
                si = (2 * t + m) % 3
                S.op('pe', lambda e, m=m, j=j, lo=lo, si=si, qb=qb: e.matmul(
                    SB[si][:, lo:QT], kT[m * DH:(m + 1) * DH, j * CH:(j + 1) * CH],
                    qt[qb][m * DH:(m + 1) * DH, lo:QT], start=True, stop=True),
                    reads=['kT', ('qt', qb)], writes=[('bk', si)])

        def EXP(t):
            h, qi, j, r, lo, qb = geom(t)
            pb = t % NP
            tb2 = t % 2
            for m in range(2):
                si = (2 * t + m) % 3
                if r >= -1:
                    off = 384 - 128 * r
                    S.op('dve', lambda e, m=m, lo=lo, si=si, off=off, tb2=tb2: e.tensor_tensor(
                        tmp[m][tb2][:, lo:QT], SB[si][:, lo:QT], bt[:, off + lo:off + QT], ALU.add),
                        reads=[('bk', si), 'bt'], writes=[('tmp', m, tb2)])
                    S.op('act', lambda e, m=m, lo=lo, pb=pb, tb2=tb2: e.activation(
                        PT[m][pb][:, lo:QT], tmp[m][tb2][:, lo:QT], AF.Exp),
                        reads=[('tmp', m, tb2)], writes=[('PT', m, pb)])
                else:
                    S.op('act', lambda e, m=m, pb=pb, si=si, h=h: e.activation(
                        PT[m][pb][:], SB[si][:], AF.Exp, bias=cb[:, h:h + 1]),
                        reads=[('bk', si), 'cb'], writes=[('PT', m, pb)])

        def PV(t):
            h, qi, j, r, lo, qb = geom(t)
            pb = t % NP
            nkc = 4 * qi + 4
            for m in range(2):
                S.op('pe', lambda e, m=m, j=j, lo=lo, pb=pb, nkc=nkc: e.matmul(
                    pO[m][:, lo:QT], vS[:, j, :], PT[m][pb][:, lo:QT], start=(j == 0), stop=(j == nkc - 1)),
                    reads=['vS', ('PT', m, pb)], writes=[('bk', 4 + m)])
            for m in range(2):
                if j == 0:
                    S.op('dve', lambda e, m=m, pb=pb: e.tensor_copy(Pacc[m][:], PT[m][pb][:]),
                         reads=[('PT', m, pb)], writes=[('Pacc', m)])
                else:
                    S.op('dve', lambda e, m=m, lo=lo, pb=pb: e.tensor_tensor(
                        Pacc[m][:, lo:QT], Pacc[m][:, lo:QT], PT[m][pb][:, lo:QT], ALU.add),
                        reads=[('PT', m, pb), ('Pacc', m)], writes=[('Pacc', m)])

        def EPI_A(h, qi):
            S.op('act', lambda e: e.activation(O1s[:], pO[0][:], AF.Copy), reads=[('bk', 4)], writes=['O1s'])
            S.op('dve', lambda e: e.tensor_copy(O2s[:], pO[1][:]), reads=[('bk', 5)], writes=['O2s'])
            for m in range(2):
                S.op('act', lambda e, m=m: e.activation(Phi[m][:], Pacc[m][:], AF.Copy),
                     reads=[('Pacc', m)], writes=[('Phi', m)])
                S.op('dve', lambda e, m=m: e.tensor_tensor(Plo[m][:], Pacc[m][:], Phi[m][:], ALU.subtract),
                     reads=[('Pacc', m), ('Phi', m)], writes=[('Plo', m)])
                S.op('pe', lambda e, m=m: e.matmul(pZ[m][:], ones[:], Phi[m][:], start=True, stop=False),
                     reads=['ones', ('Phi', m)], writes=[('bk', 6 + m)])
                S.op('pe', lambda e, m=m: e.matmul(pZ[m][:], ones[:], Plo[m][:], start=False, stop=True),
                     reads=['ones', ('Plo', m)], writes=[('bk', 6 + m)])
            S.op('act', lambda e: e.activation(s1s[:], pZ[0][:], AF.Copy), reads=[('bk', 6)], writes=['s1s'])
            S.op('dve', lambda e: e.tensor_copy(s2s[:], pZ[1][:]), reads=[('bk', 7)], writes=['s2s'])
            S.op('dve', lambda e: e.reciprocal(s1s[:], s1s[:]), reads=['s1s'], writes=['s1s'])
            S.op('dve', lambda e: e.reciprocal(s2s[:], s2s[:]), reads=['s2s'], writes=['s2s'])
            S.op('pool', lambda e: e.tensor_tensor(O1s[:], O1s[:], s1s[:], ALU.mult),
                 reads=['O1s', 's1s'], writes=['O1s'])
            S.op('pool', lambda e: e.tensor_tensor(O2s[:], O2s[:], s2s[:], ALU.mult),
                 reads=['O2s', 's2s'], writes=['O2s'])
            S.op('dve', lambda e: e.scalar_tensor_tensor(osb[:], O2s[:], nlam[:, 0:1], O1s[:], ALU.mult, ALU.add),
                 reads=['O1s', 'O2s', 'nlam'], writes=['osb'])
            S.op('pool', lambda e: e.tensor_tensor(sqb[:], osb[:], osb[:], ALU.mult),
                 reads=['osb'], writes=['sqb'])

        def EPI_B(h, qi):
            qb = (h * nt + qi) % 2
            q0 = qi * QT
            S.op('pe', lambda e: e.matmul(pSS[:], ones[:], sqb[:], start=True, stop=True),
                 reads=['ones', 'sqb'], writes=[('bk', 3)])
            S.op('act', lambda e: e.activation(rsb[:], pSS[:], AF.Ln, bias=epsg[:], scale=1.0 / 128.0),
                 reads=[('bk', 3), 'epsg'], writes=['rsb'])
            S.op('act', lambda e: e.activation(rsb[:], rsb[:], AF.Exp, scale=-0.5),
                 reads=['rsb'], writes=['rsb'])
            S.op('dve', lambda e: e.tensor_tensor(osb[:], osb[:], rsb[:], ALU.mult),
                 reads=['osb', 'rsb'], writes=['osb'])
            S.op('dve', lambda e: e.scalar_tensor_tensor(
                gT[:, h, q0:q0 + QT], osb[:], gsc[:, 0:1], sgt[qb][:], ALU.mult, ALU.mult),
                reads=['osb', 'gsc', ('sgt', qb)], writes=[('gT', h, qi)])

        pend = []
        for t in range(nit + 1):
            if t < nit:
                h, qi, j, r, lo, qb = geom(t)
                if j == 0:
                    if qi == 0:
                        S.dma('sp', kT[:], kT_d[h], writes=['kT'], dkey='kT')
                    q0 = qi * QT
                    S.dma('sp', qt[qb][:], qT_d[h, :, q0:q0 + QT], writes=[('qt', qb)], dkey=('qt', qb))
                    S.dma('sp', sgt[qb][:], sgT_d[h, :, q0:q0 + QT], writes=[('sgt', qb)], dkey=('sgt', qb))
                QK(t)
                if j == 0 and qi == 0:
                    S.dma('sp', bt[:], bias_d[h], writes=['bt'], dkey='bt')
                EXP(t)
            if t >= 1:
                h, qi, j, r, lo, qb = geom(t - 1)
                if j == 0 and qi == 0:
                    S.dma('sp', vS[:], v_d[:, h * 128:(h + 1) * 128].rearrange("(c p) d -> p c d", p=128),
                          writes=['vS'], dkey='vS')
                PV(t - 1)
                if j == 4 * qi + 3:
                    EPI_A(h, qi)
                    pend.append((t + 2, h, qi))
            while pend and pend[0][0] <= t:
                _, hh_, qq_ = pend.pop(0)
                EPI_B(hh_, qq_)
        for _, hh_, qq_ in pend:
            EPI_B(hh_, qq_)

        S.barrier()
        if dbg:
            S.dma('sp', dbg_d['gT'], big[:, 0:4 * seq], dkey=('dbg', 0))
            S.dma('sp', dbg_d['nlam'], nlam[:], dkey=('dbg', 1))
            S.dma('sp', dbg_d['qTd'], qT_d, dkey=('dbg', 2))
            S.dma('sp', dbg_d['sgTd'], sgT_d, dkey=('dbg', 3))
            S.dma('sp', dbg_d['kTd'], kT_d, dkey=('dbg', 4))
        for i in range(nch):
            t0 = i * CH
            b2 = i % 2
            for half in range(2):
                bkx = b2 * 2 + half
                pt = banks[bkx]
                for j in range(4):
                    S.op('pe', lambda e, pt=pt, j=j, half=half, t0=t0: e.matmul(
                        pt[:], gT[:, j, t0:t0 + CH], wout[:, j, half * 512:(half + 1) * 512],
                        start=(j == 0), stop=(j == 3)), reads=[], writes=[('bk', bkx)])
                if half == 0:
                    S.op('act', lambda e, pt=pt, b2=b2: e.activation(ost[b2][:, 0:512], pt[:], AF.Copy),
                         reads=[('bk', bkx)], writes=[('ost', b2, 0)])
                else:
                    S.op('dve', lambda e, pt=pt, b2=b2: e.tensor_copy(ost[b2][:, 512:1024], pt[:]),
                         reads=[('bk', bkx)], writes=[('ost', b2, 1)])
            S.dma('sp', part_d[t0:t0 + CH, :], ost[b2][:], reads=[('ost', b2, 0), ('ost', b2, 1)],
                  dkey=('ost', b2))
        S.emit()
    return nc


def t5_bucket_np(n):
    n = np.maximum(n, 0)
    nf = np.maximum(n, 16).astype(np.float32)
    large = 16 + (np.log(nf / np.float32(16)) / np.float32(math.log(128 / 16)) * np.float32(16)).astype(np.int32)
    large = np.minimum(large, 31)
    return np.where(n < 16, n, large)


def l1_inputs(x1, pre_norm_g, diff_w_in, diff_w_out, lq1, lk1, lq2, lk2, subln_g, rel_bias, seq=SEQ):
    maps = []
    ident = np.eye(128, dtype=np.float32)
    gpre = np.ascontiguousarray(pre_norm_g[1].reshape(8, 128).T)
    u = np.arange(1024)[None, :]
    p = np.arange(128)[:, None]
    n = u - 384 - p
    bidx = t5_bucket_np(n)
    lamv = np.ascontiguousarray(np.broadcast_to(
        np.stack([lq1[0], lk1[0], lq2[0], lk2[0]])[None], (128, 4, DH))).astype(np.float32)
    gsub = np.ascontiguousarray(subln_g[0].reshape(128, 1)).astype(np.float32)
    for core in range(NCORES):
        b, r = core // 4, core % 4
        w = diff_w_in[0]
        base = np.arange(r * 512, (r + 1) * 512)
        cols = np.concatenate([base, 2048 + base, 6144 + base, 4096 + base])
        bias = np.empty((4, 128, 1024), np.float32)
        cbv = np.empty((128, 4), np.float32)
        for h in range(4):
            hh = 4 * r + h
            tb = rel_bias[:, hh][bidx]
            bias[h] = np.where(n >= 0, tb, np.float32(NEG))
            cbv[:, h] = rel_bias[31, hh]
        maps.append({
            "x": np.ascontiguousarray(x1[b][:seq]),
            "gpre": gpre,
            "win": np.ascontiguousarray(w[:, cols]),
            "wout": np.ascontiguousarray(diff_w_out[0][r * 512:(r + 1) * 512, :]),
            "bias": bias,
            "cb": cbv,
            "lamv": lamv,
            "gsub": gsub,
            "ident": ident,
        })
    return maps


TOK = SEQ // 4


def build_red(ntok=TOK, nc=None, io=None, phase=None, npart=4):
    nchk = ntok // CH
    if nc is None:
        nc = bass.Bass("TRN2", target_bir_lowering=False)
    if io is not None and 'parts' in io:
        parts_l = io['parts']
    else:
        parts_d = nc.dram_tensor("parts", [4, ntok, D_MODEL], F32, kind="ExternalInput").ap()
        parts_l = [parts_d[k] for k in range(4)]
    xres_d = _dt(nc, io, "xres", [ntok, D_MODEL], F32, "ExternalInput")
    gpost_d = _dt(nc, io, "gpost", [128, D_MODEL], F32, "ExternalInput")
    out_d = _dt(nc, io, "out", [ntok, D_MODEL], F32, "ExternalOutput")
    with ExitStack() as st:
        pfx = f"p{phase[1]}_" if phase is not None else ""

        def sb(name, shape, dt):
            return st.enter_context(nc.sbuf_tensor(pfx + name, shape, dt))
        gpost = sb("gpost_s", [128, D_MODEL], F32)
        nhalf = sb("nhalf", [128, 1], F32)
        pt = [[sb(f"pt{k}_{i}", [128, D_MODEL], F32) for i in range(2)] for k in range(npart)]
        xr = [sb(f"xr{i}", [128, D_MODEL], F32) for i in range(2)]
        junk = sb("junk", [128, D_MODEL], BF16)
        ssum = [sb(f"ss{i}", [128, 1], F32) for i in range(2)]
        rstd = [sb(f"rstd{i}", [128, 1], F32) for i in range(2)]
        yo = [sb(f"yo{i}", [128, D_MODEL], F32) for i in range(2)]
        S = Sched(nc, phase)
        S.dma('sp', gpost[:], gpost_d, writes=['gpost'], dkey=('c0', 10))
        S.op('dve', lambda e: e.memset(nhalf[:], -0.5), writes=['nhalf'])
        for i in range(nchk):
            b2 = i % 2
            t0 = i * CH
            for k in range(npart):
                S.dma('sp', pt[k][b2][:], parts_l[k][t0:t0 + CH, :], writes=[('pt', k, b2)], dkey=('pt', k, b2))
            S.dma('sp', xr[b2][:], xres_d[t0:t0 + CH, :], writes=[('xr', b2)], dkey=('xr', b2))
            if npart == 4:
                S.op('dve', lambda e, b2=b2: e.tensor_tensor(pt[0][b2][:], pt[0][b2][:], pt[1][b2][:], ALU.add),
                     reads=[('pt', 0, b2), ('pt', 1, b2)], writes=[('pt', 0, b2)])
                S.op('pool', lambda e, b2=b2: e.tensor_tensor(pt[2][b2][:], pt[2][b2][:], pt[3][b2][:], ALU.add),
                     reads=[('pt', 2, b2), ('pt', 3, b2)], writes=[('pt', 2, b2)])
                S.op('dve', lambda e, b2=b2: e.tensor_tensor(pt[0][b2][:], pt[0][b2][:], pt[2][b2][:], ALU.add),
                     reads=[('pt', 0, b2), ('pt', 2, b2)], writes=[('pt', 0, b2)])
            S.op('act', lambda e, b2=b2: e.activation(junk[:], pt[0][b2][:], AF.Square, scale=1.0 / 32.0,
                                                      accum_out=ssum[b2][:]),
                 reads=[('pt', 0, b2)], writes=['junk', ('ss', b2)])
            S.op('dve', lambda e, b2=b2: e.tensor_scalar(ssum[b2][:], ssum[b2][:], RMS_EPS, None, ALU.add),
                 reads=[('ss', b2)], writes=[('ss', b2)])
            S.op('pool', lambda e, b2=b2: e.tensor_tensor(rstd[b2][:], ssum[b2][:], nhalf[:], ALU.pow),
                 reads=[('ss', b2), 'nhalf'], writes=[('rstd', b2)])
            S.op('act', lambda e, b2=b2: e.activation(yo[b2][:], pt[0][b2][:], AF.Copy, scale=rstd[b2][:]),
                 reads=[('pt', 0, b2), ('rstd', b2)], writes=[('yo', b2)])
            S.op('dve', lambda e, b2=b2: e.tensor_tensor(yo[b2][:], yo[b2][:], gpost[:], ALU.mult),
                 reads=[('yo', b2), 'gpost'], writes=[('yo', b2)])
            S.op('pool', lambda e, b2=b2: e.tensor_tensor(yo[b2][:], yo[b2][:], xr[b2][:], ALU.add),
                 reads=[('yo', b2), ('xr', b2)], writes=[('yo', b2)])
            S.dma('sp', out_d[t0:t0 + CH, :], yo[b2][:], reads=[('yo', b2)], dkey=('yo', b2))
        S.emit()
    return nc


def red_inputs(parts, xfull, g):
    maps = []
    gp = np.ascontiguousarray(np.broadcast_to(g[None, :], (128, D_MODEL))).astype(np.float32)
    for core in range(NCORES):
        b, r = core // 4, core % 4
        sl = slice(r * TOK, (r + 1) * TOK)
        maps.append({
            "parts": np.stack([parts[b * 4 + hh][sl] for hh in range(4)]),
            "xres": np.ascontiguousarray(xfull[b][sl]),
            "gpost": gp,
        })
    return maps


_CACHE = {}


def _get(name, fn):
    if name not in _CACHE:
        _CACHE[name] = fn()
    return _CACHE[name]


def emit_cc(nc, kind, op, in_ap, out_ap, phase):
    S = Sched(nc, phase)
    S.op('pool', lambda e: e.collective_compute(kind, op, replica_groups=[[0, 1, 2, 3], [4, 5, 6, 7]],
                                                ins=[in_ap.opt()], outs=[out_ap.opt()]),
         dkey=('cc',), inc=1)
    S.emit()


def build_fused(seq=SEQ, upto=6):
    tok = seq // 4
    nc = bass.Bass("TRN2", target_bir_lowering=False)
    E = lambda name, shape, dt=F32: nc.dram_tensor(name, shape, dt, kind="ExternalInput").ap()
    I = lambda name, shape, dt=F32: nc.dram_tensor(name, shape, dt).ap()
    with nc.semaphore("phase") as psem:
        ident = E("ident", [128, 128])
        part0 = I("part0", [seq, D_MODEL])
        io0 = dict(x=E("x", [seq, D_MODEL]), gpre=E("l0_gpre", [128, 8]), win=E("l0_win", [D_MODEL, RET_COLS]),
                   wout=E("l0_wout", [RET_DV, D_MODEL]), tab=E("tab", [seq, 1024]), mask=E("mask", [128, 128]),
                   ident=ident, gc=E("gc", [128, 1]), part=part0)
        build_l0(seq=seq, nc=nc, io=io0, phase=(psem, 0, False))
        red0 = I("red0", [tok, D_MODEL])
        emit_cc(nc, "ReduceScatter", ALU.add, part0, red0, (psem, 1, False))
        if upto == 2:
            out = nc.dram_tensor("out", [tok, D_MODEL], F32, kind="ExternalOutput").ap()
            build_red(ntok=tok, nc=nc, io=dict(parts=[red0], xres=E("xres", [tok, D_MODEL]),
                                               gpost=E("gpost0", [128, D_MODEL]), out=out),
                      phase=(psem, 2, True), npart=1)
            return nc
        x1s = I("x1s", [tok, D_MODEL])
        build_red(ntok=tok, nc=nc, io=dict(parts=[red0], xres=E("xres", [tok, D_MODEL]), gpost=E("gpost0", [128, D_MODEL]),
                                 out=x1s), phase=(psem, 2, False), npart=1)
        R = min(tok, 256)
        NS = tok // R
        xb = [I(f"x1f{i}", [4 * R, D_MODEL]) for i in range(NS)]
        ph = 3
        for i in range(NS):
            emit_cc(nc, "AllGather", ALU.bypass, x1s[i * R:(i + 1) * R, :], xb[i], (psem, ph, False))
            ph += 1

        def xsrc(t0):
            r_, rem = t0 // tok, t0 % tok
            i_, j_ = rem // R, rem % R
            return xb[i_][r_ * R + j_:r_ * R + j_ + CH, :]

        part1 = I("part1", [seq, D_MODEL])
        io1 = dict(xsrc=xsrc, gpre=E("l1_gpre", [128, 8]), win=E("l1_win", [D_MODEL, 2048]),
                   wout=E("l1_wout", [512, D_MODEL]), bias=E("bias", [4, 128, 1024]), cb=E("cb", [128, 4]),
                   lamv=E("lamv", [128, 4, DH]), gsub=E("gsub", [128, 1]), ident=ident, part=part1)
        build_l1(seq=seq, nc=nc, io=io1, phase=(psem, ph, False))
        red1 = I("red1", [tok, D_MODEL])
        emit_cc(nc, "ReduceScatter", ALU.add, part1, red1, (psem, ph + 1, False))
        out = nc.dram_tensor("out", [tok, D_MODEL], F32, kind="ExternalOutput").ap()
        build_red(ntok=tok, nc=nc, io=dict(parts=[red1], xres=x1s, gpost=E("gpost1", [128, D_MODEL]), out=out),
                  phase=(psem, ph + 2, True), npart=1)
    return nc


def fused_inputs(inp, seq=SEQ):
    tok = seq // 4
    f = lambda a: np.asarray(a, dtype=np.float32)
    x = f(inp['x'])
    m0 = l0_inputs(x, f(inp['pre_norm_g']), f(inp['ret_w_in']), f(inp['ret_w_out']), seq)
    m1 = l1_inputs(x, f(inp['pre_norm_g']), f(inp['diff_w_in']), f(inp['diff_w_out']), f(inp['diff_lambda_q1']),
                   f(inp['diff_lambda_k1']), f(inp['diff_lambda_q2']), f(inp['diff_lambda_k2']),
                   f(inp['diff_subln_g']), f(inp['rel_bias']), seq)
    pg = f(inp['post_norm_g'])
    gp = [np.ascontiguousarray(np.broadcast_to(pg[i][None, :], (128, D_MODEL))) for i in range(2)]
    maps = []
    for core in range(NCORES):
        b, r = core // 4, core % 4
        a, c = m0[core], m1[core]
        maps.append({
            "x": a["x"], "l0_gpre": a["gpre"], "l0_win": a["win"], "l0_wout": a["wout"], "tab": a["tab"],
            "mask": a["mask"], "ident": a["ident"], "gc": a["gc"],
            "xres": np.ascontiguousarray(x[b][r * tok:(r + 1) * tok]), "gpost0": gp[0], "gpost1": gp[1],
            "l1_gpre": c["gpre"], "l1_win": c["win"], "l1_wout": c["wout"], "bias": c["bias"], "cb": c["cb"],
            "lamv": c["lamv"], "gsub": c["gsub"],
        })
    return maps


def kernel(**inputs):
    nc = _get('fused', build_fused)
    res = run_bass_kernel_spmd(nc, fused_inputs(inputs), core_ids=list(range(NCORES)))
    out = np.stack([np.concatenate([res.results[b * 4 + r]["out"] for r in range(4)], axis=0)
                    for b in range(BATCH)])
    return out.astype(np.float32)


def kernel_unfused(x, pre_norm_g, post_norm_g, ret_w_in, ret_w_out, diff_w_in, diff_w_out,
                   diff_lambda_q1, diff_lambda_k1, diff_lambda_q2, diff_lambda_k2, diff_subln_g, rel_bias):
    f = lambda a: np.asarray(a, dtype=np.float32)
    x = f(x)
    cores = list(range(NCORES))
    res = run_bass_kernel_spmd(_get('l0', build_l0), l0_inputs(x, f(pre_norm_g), f(ret_w_in), f(ret_w_out)),
                               core_ids=cores)
    parts = [r["part"] for r in res.results]
    res = run_bass_kernel_spmd(_get('red', build_red), red_inputs(parts, x, f(post_norm_g)[0]), core_ids=cores)
    x1 = np.stack([np.concatenate([res.results[b * 4 + r]["out"] for r in range(4)], axis=0) for b in range(BATCH)])
    res = run_bass_kernel_spmd(_get('l1', build_l1), l1_inputs(
        x1, f(pre_norm_g), f(diff_w_in), f(diff_w_out), f(diff_lambda_q1), f(diff_lambda_k1),
        f(diff_lambda_q2), f(diff_lambda_k2), f(diff_subln_g), f(rel_bias)), core_ids=cores)
    parts = [r["part"] for r in res.results]
    res = run_bass_kernel_spmd(_get('red', build_red), red_inputs(parts, x1, f(post_norm_g)[1]), core_ids=cores)
    out = np.stack([np.concatenate([res.results[b * 4 + r]["out"] for r in range(4)], axis=0) for b in range(BATCH)])
    return out.astype(np.float32)
```
